# Optimizing a Trainium2 kernel written in Bass

```python
import jax, jax.numpy as jnp
from jax import lax
import numpy as np

D_MODEL = 1024
BATCH = 8
SEQ = 2048
DEPTH = 2
DEC_BATCH = 16
DEC_SEQ = 2048
PAST_LEN = 128

PLE_DIM = 256
GRID_W = 64
N_BRANCH = 4
BRANCH_W = D_MODEL // 2
RET_HEADS = 4
RET_QK = BRANCH_W // (2 * RET_HEADS)
RET_V = BRANCH_W // RET_HEADS
RET_CHUNK = 128
ROPE_BASE = 10000.0
NAT_HEADS = 8
NAT_HD = BRANCH_W // NAT_HEADS
NAT_WIN_ROWS = 8
NAT_WIN_COLS = 16
LRU_BLOCKS = 8
LRU_BW = BRANCH_W // LRU_BLOCKS
LRU_CONV = 4
LRU_C = 8.0
HGRN_HEADS = 4
HGRN_DK = BRANCH_W // HGRN_HEADS
HGRN_DV = BRANCH_W // HGRN_HEADS
HGRN_CHUNK = 32
EPS = 1e-6

IN_SPLIT_SIZES = (
    RET_HEADS * RET_QK, RET_HEADS * RET_QK, RET_HEADS * RET_V, BRANCH_W,
    BRANCH_W, BRANCH_W, BRANCH_W, BRANCH_W,
    BRANCH_W, BRANCH_W,
    HGRN_HEADS * HGRN_DK, HGRN_HEADS * HGRN_DK, HGRN_HEADS * HGRN_DK,
    HGRN_HEADS * HGRN_DV, BRANCH_W,
)
W_IN = sum(IN_SPLIT_SIZES)
IN_SPLIT_POINTS = tuple(int(c) for c in np.cumsum(IN_SPLIT_SIZES)[:-1])

kernel_name = 'hybrid_bidir_gated_encoder'

F32 = jnp.float32


def rms_norm(x, g):
    xf = x.astype(F32)
    y = xf * lax.rsqrt(jnp.mean(xf * xf, axis=-1, keepdims=True) + EPS)
    return (y * g.astype(F32)).astype(x.dtype)


def head_rms(o):
    return o * lax.rsqrt(jnp.mean(o * o, axis=-1, keepdims=True) + EPS)


def rev_t(a):
    return jnp.flip(a, axis=1)


def rotary(x, pos):
    half = x.shape[-1] // 2
    inv = ROPE_BASE ** (-jnp.arange(half, dtype=F32) / half)
    ang = pos[:, None] * inv[None, :]
    cos = jnp.cos(ang)[:, None, :]
    sin = jnp.sin(ang)[:, None, :]
    x1, x2 = x[..., :half], x[..., half:]
    return jnp.concatenate([x1 * cos - x2 * sin, x1 * sin + x2 * cos], axis=-1)


def retention_scan(q, k, v, log_gamma):
    B, T, H, dk = q.shape
    dv = v.shape[-1]
    C = RET_CHUNK
    N = T // C
    qc = q.reshape(B, N, C, H, dk)
    kc = k.reshape(B, N, C, H, dk)
    vc = v.reshape(B, N, C, H, dv)
    tt = jnp.arange(C, dtype=F32)
    diff = tt[:, None] - tt[None, :]
    intra = jnp.where(diff >= 0, jnp.exp(jnp.maximum(diff, 0.0)[None] * log_gamma[:, None, None]), 0.0)
    scores = jnp.einsum('bnthd,bnshd->bnhts', qc, kc) * intra
    o = jnp.einsum('bnhts,bnshv->bnthv', scores, vc)
    k_tail = jnp.exp((C - 1 - tt)[:, None] * log_gamma[None, :])
    local = jnp.einsum('bnshd,sh,bnshv->bnhdv', kc, k_tail, vc)
    chunk_decay = jnp.exp(C * log_gamma)[:, None, None]

    def step(state, loc):
        return chunk_decay * state + loc, state

    _, prev = lax.scan(step, jnp.zeros((B, H, dk, dv), F32), jnp.moveaxis(local, 1, 0))
    prev = jnp.moveaxis(prev, 0, 1)
    q_head = jnp.exp((tt + 1.0)[:, None] * log_gamma[None, :])
    o = o + jnp.einsum('bnthd,th,bnhdv->bnthv', qc, q_head, prev)
    return o.reshape(B, T, H, dv)


def retention_branch(q, k, v, decay_logit):
    B, T, _ = q.shape
    pos = jnp.arange(T, dtype=F32)
    qh = rotary(q.astype(F32).reshape(B, T, RET_HEADS, RET_QK), pos)
    kh = rotary(k.astype(F32).reshape(B, T, RET_HEADS, RET_QK), pos) * (RET_QK ** -0.5)
    vh = v.astype(F32).reshape(B, T, RET_HEADS, RET_V)
    log_gamma = jax.nn.log_sigmoid(decay_logit.astype(F32))
    o = retention_scan(qh, kh, vh, log_gamma[0]) + rev_t(
        retention_scan(rev_t(qh), rev_t(kh), rev_t(vh), log_gamma[1]))
    return head_rms(o).reshape(B, T, BRANCH_W).astype(q.dtype)


def neighbourhood_attention_branch(q, k, v, rpb):
    B, T, _ = q.shape
    rows = T // GRID_W
    wr = min(NAT_WIN_ROWS, rows)
    n_cb = GRID_W // NAT_WIN_COLS
    kw = 2 * NAT_WIN_COLS
    grid = (B, rows, GRID_W, NAT_HEADS, NAT_HD)
    qg = (q * (NAT_HD ** -0.5)).reshape(B, rows, n_cb, NAT_WIN_COLS, NAT_HEADS, NAT_HD)
    kg = k.reshape(grid)
    vg = v.reshape(grid)
    qcols = jnp.arange(GRID_W).reshape(n_cb, NAT_WIN_COLS)
    kstart = jnp.clip(jnp.arange(n_cb) * NAT_WIN_COLS - NAT_WIN_COLS // 2, 0, GRID_W - kw)
    kcols = kstart[:, None] + jnp.arange(kw)
    wstart = jnp.clip(qcols - NAT_WIN_COLS // 2, 0, GRID_W - NAT_WIN_COLS)
    col_ok = (kcols[:, None, :] >= wstart[..., None]) & (kcols[:, None, :] < wstart[..., None] + NAT_WIN_COLS)
    dc_idx = jnp.clip(kcols[:, None, :] - qcols[..., None] + NAT_WIN_COLS - 1, 0, 2 * NAT_WIN_COLS - 2)
    rpb = rpb.astype(F32)

    def row_block(r):
        rs = jnp.clip(r - wr // 2, 0, rows - wr)
        kb = lax.dynamic_slice_in_dim(kg, rs, wr, axis=1)[:, :, kcols]
        vb = lax.dynamic_slice_in_dim(vg, rs, wr, axis=1)[:, :, kcols]
        qr = qg[:, r]
        s = jnp.einsum('bjqhd,bwjkhd->bhjqwk', qr, kb).astype(F32)
        dr_idx = rs + jnp.arange(wr) - r + NAT_WIN_ROWS - 1
        bias = rpb[:, dr_idx[:, None, None, None], dc_idx[None]]
        bias = bias.transpose(0, 2, 3, 1, 4)
        s = jnp.where(col_ok[:, :, None, :], s + bias, -jnp.inf)
        p = jax.nn.softmax(s, axis=(-2, -1)).astype(vb.dtype)
        return jnp.einsum('bhjqwk,bwjkhd->bjqhd', p, vb)

    o = lax.map(row_block, jnp.arange(rows))
    return jnp.moveaxis(o, 0, 1).reshape(B, T, BRANCH_W).astype(q.dtype)


def rg_lru(xc, wa, ba, wx, bx, lam, reverse):
    B, T, W = xc.shape
    xb = xc.reshape(B, T, LRU_BLOCKS, LRU_BW)
    r = jax.nn.sigmoid(jnp.einsum('btnj,njk->btnk', xb, wa.astype(F32)).reshape(B, T, W) + ba.astype(F32))
    i = jax.nn.sigmoid(jnp.einsum('btnj,njk->btnk', xb, wx.astype(F32)).reshape(B, T, W) + bx.astype(F32))
    log_a = -LRU_C * r * jax.nn.softplus(-lam.astype(F32))
    a = jnp.exp(log_a)
    b = jnp.sqrt(-jnp.expm1(2.0 * log_a)) * (i * xc)

    def combine(e1, e2):
        a1, b1 = e1
        a2, b2 = e2
        return a1 * a2, a2 * b1 + b2

    _, h = lax.associative_scan(combine, (a, b), axis=1, reverse=reverse)
    return h


def rglru_branch(xin, conv_w, conv_b, wa, ba, wx, bx, lam):
    C = xin.shape[-1]
    xc = lax.conv_general_dilated(
        xin.astype(F32), conv_w.astype(F32)[:, None, :], window_strides=(1,),
        padding=[(LRU_CONV // 2, LRU_CONV - 1 - LRU_CONV // 2)],
        dimension_numbers=('NWC', 'WIO', 'NWC'), feature_group_count=C) + conv_b.astype(F32)
    h = rg_lru(xc, wa[0], ba[0], wx[0], bx[0], lam[0], False) + rg_lru(xc, wa[1], ba[1], wx[1], bx[1], lam[1], True)
    return h.astype(xin.dtype)


def gla_chunk(q, k, v, log_f):
    B, T, H, dk = q.shape
    dv = v.shape[-1]
    C = HGRN_CHUNK
    N = T // C

    def chunks(a):
        return jnp.moveaxis(a.reshape(B, N, C, H, a.shape[-1]), 1, 0)

    causal = jnp.tril(jnp.ones((C, C), bool))[None, :, :, None, None]

    def step(S, inp):
        qn, kn, vn, gn = inp
        b = jnp.cumsum(gn, axis=1)
        o_inter = jnp.einsum('bthk,bhkv->bthv', qn * jnp.exp(b), S)
        diff = b[:, :, None] - b[:, None, :]
        dec = jnp.exp(jnp.where(causal, diff, -jnp.inf))
        att = jnp.einsum('btshk,bshk->bhts', qn[:, :, None] * dec, kn)
        o_intra = jnp.einsum('bhts,bshv->bthv', att, vn)
        b_last = b[:, -1]
        S = jnp.exp(b_last)[..., None] * S + jnp.einsum('bshk,bshv->bhkv', kn * jnp.exp(b_last[:, None] - b), vn)
        return S, o_inter + o_intra

    _, o = lax.scan(step, jnp.zeros((B, H, dk, dv), F32), (chunks(q), chunks(k), chunks(v), chunks(log_f)))
    return jnp.moveaxis(o, 0, 1).reshape(B, T, H, dv)


def hgrn2_branch(q, f_fwd, f_bwd, i, lb, gain):
    B, T, _ = q.shape
    kshape = (B, T, HGRN_HEADS, HGRN_DK)
    qh = jax.nn.silu(q.astype(F32)).reshape(kshape)
    vh = i.astype(F32).reshape(B, T, HGRN_HEADS, HGRN_DV)

    def gates(zf, lb_d):
        f = lb_d + (1.0 - lb_d) * jax.nn.sigmoid(zf.astype(F32))
        return (1.0 - f).reshape(kshape), jnp.log(f).reshape(kshape)

    k_f, g_f = gates(f_fwd, lb[0])
    k_b, g_b = gates(f_bwd, lb[1])
    o = gla_chunk(qh, k_f, vh, g_f) + rev_t(gla_chunk(rev_t(qh), rev_t(k_b), rev_t(vh), rev_t(g_b)))
    o = head_rms(o) * gain.astype(F32).reshape(HGRN_HEADS, HGRN_DV)
    return o.reshape(B, T, BRANCH_W).astype(q.dtype)


def hgrn_lower_bounds(logits):
    sm = jax.nn.softmax(logits.astype(F32), axis=0)
    return jnp.cumsum(sm, axis=0) - sm[0:1]


def mixer_layer(x, p_l, g_mix, w_in_l, ret_logit, rpb, conv_w, conv_b, wa, ba, wx, bx, lam, lb, hgrn_gain,
                w_br, w_mg, w_o, g_ple, w_pg, w_pp):
    h = rms_norm(x, g_mix)
    z = h @ w_in_l
    (rq, rk, rv, rg, nq, nk, nv, ng, lx, lg, hq, hff, hfb, hi, hg) = jnp.split(z, IN_SPLIT_POINTS, axis=-1)
    branches = (
        retention_branch(rq, rk, rv, ret_logit) * jax.nn.silu(rg),
        neighbourhood_attention_branch(nq, nk, nv, rpb) * jax.nn.silu(ng),
        rglru_branch(lx, conv_w, conv_b, wa, ba, wx, bx, lam) * jax.nn.silu(lg),
        hgrn2_branch(hq, hff, hfb, hi, lb, hgrn_gain) * jax.nn.silu(hg),
    )
    merged = jax.nn.sigmoid(h @ w_mg[0]) * (branches[0] @ w_br[0])
    for j in range(1, N_BRANCH):
        merged = merged + jax.nn.sigmoid(h @ w_mg[j]) * (branches[j] @ w_br[j])
    x = x + merged @ w_o
    gate = jax.nn.sigmoid(rms_norm(x, g_ple) @ w_pg)
    return x + gate * (p_l.astype(x.dtype) @ w_pp)


def encoder(x, p, lb_all, weights):
    (norm_mix, w_in, ret_decay_logit, nat_rpb, lru_conv_w, lru_conv_b, lru_wa, lru_ba, lru_wx, lru_bx,
     lru_lambda, hgrn_norm, w_branch, w_merge, w_out, ple_norm, w_ple_gate, w_ple_proj, final_norm) = weights
    for l in range(DEPTH):
        x = mixer_layer(x, p[l], norm_mix[l], w_in[l], ret_decay_logit[l], nat_rpb[l], lru_conv_w[l], lru_conv_b[l],
                        lru_wa[l], lru_ba[l], lru_wx[l], lru_bx[l], lru_lambda[l], lb_all[l], hgrn_norm[l],
                        w_branch[l], w_merge[l], w_out[l], ple_norm[l], w_ple_gate[l], w_ple_proj[l])
    return rms_norm(x, final_norm)


def setup_inputs(seed: int = 0) -> dict:
    key = jax.random.key(seed)
    ks = jax.random.split(key, 24)

    def nrm(k, shape, scale):
        return jax.random.normal(k, shape, F32) * scale

    ret_base = jnp.asarray(np.log(2.0 ** (5.0 + np.arange(RET_HEADS)) - 1.0), F32)
    a_c = jax.random.uniform(ks[14], (DEPTH, 2, BRANCH_W), F32, 0.9, 0.999)
    a = a_c ** (1.0 / LRU_C)
    return {
        'x_prompt': nrm(ks[0], (BATCH, SEQ, D_MODEL), 1.0),
        'x_sample': nrm(ks[1], (DEC_BATCH, DEC_SEQ, D_MODEL), 1.0),
        'p_prompt': nrm(ks[2], (DEPTH, BATCH, SEQ, PLE_DIM), 1.0),
        'p_sample': nrm(ks[3], (DEPTH, DEC_BATCH, DEC_SEQ, PLE_DIM), 1.0),
        'norm_mix': 1.0 + nrm(ks[4], (DEPTH, D_MODEL), 0.05),
        'w_in': nrm(ks[5], (DEPTH, D_MODEL, W_IN), D_MODEL ** -0.5),
        'ret_decay_logit': ret_base + nrm(ks[6], (DEPTH, 2, RET_HEADS), 0.05),
        'nat_rpb': nrm(ks[7], (DEPTH, NAT_HEADS, 2 * NAT_WIN_ROWS - 1, 2 * NAT_WIN_COLS - 1), 0.02),
        'lru_conv_w': nrm(ks[8], (DEPTH, LRU_CONV, BRANCH_W), LRU_CONV ** -0.5),
        'lru_conv_b': nrm(ks[9], (DEPTH, BRANCH_W), 0.02),
        'lru_wa': nrm(ks[10], (DEPTH, 2, LRU_BLOCKS, LRU_BW, LRU_BW), LRU_BW ** -0.5),
        'lru_ba': nrm(ks[11], (DEPTH, 2, BRANCH_W), 0.02),
        'lru_wx': nrm(ks[12], (DEPTH, 2, LRU_BLOCKS, LRU_BW, LRU_BW), LRU_BW ** -0.5),
        'lru_bx': nrm(ks[13], (DEPTH, 2, BRANCH_W), 0.02),
        'lru_lambda': jnp.log(a) - jnp.log1p(-a),
        'hgrn_lb_logits': nrm(ks[15], (DEPTH, 2, HGRN_HEADS * HGRN_DK), 0.5),
        'hgrn_norm': 1.0 + nrm(ks[16], (DEPTH, HGRN_HEADS * HGRN_DV), 0.05),
        'w_branch': nrm(ks[17], (DEPTH, N_BRANCH, BRANCH_W, D_MODEL), BRANCH_W ** -0.5),
        'w_merge': nrm(ks[18], (DEPTH, N_BRANCH, D_MODEL, D_MODEL), D_MODEL ** -0.5),
        'w_out': nrm(ks[19], (DEPTH, D_MODEL, D_MODEL), D_MODEL ** -0.5),
        'ple_norm': 1.0 + nrm(ks[20], (DEPTH, D_MODEL), 0.05),
        'w_ple_gate': nrm(ks[21], (DEPTH, D_MODEL, D_MODEL), D_MODEL ** -0.5),
        'w_ple_proj': nrm(ks[22], (DEPTH, PLE_DIM, D_MODEL), PLE_DIM ** -0.5),
        'final_norm': 1.0 + nrm(ks[23], (D_MODEL,), 0.05),
    }


def reference(x_prompt, x_sample, p_prompt, p_sample, norm_mix, w_in, ret_decay_logit, nat_rpb, lru_conv_w,
              lru_conv_b, lru_wa, lru_ba, lru_wx, lru_bx, lru_lambda, hgrn_lb_logits, hgrn_norm, w_branch,
              w_merge, w_out, ple_norm, w_ple_gate, w_ple_proj, final_norm):
    lb_all = hgrn_lower_bounds(hgrn_lb_logits)
    weights = (norm_mix, w_in, ret_decay_logit, nat_rpb, lru_conv_w, lru_conv_b, lru_wa, lru_ba, lru_wx, lru_bx,
               lru_lambda, hgrn_norm, w_branch, w_merge, w_out, ple_norm, w_ple_gate, w_ple_proj, final_norm)
    y_prompt = encoder(x_prompt, p_prompt, lb_all, weights)
    y_sample = encoder(x_sample, p_sample, lb_all, weights)
    return (y_prompt, y_sample)
```

```python
import numpy as np
from contextlib import ExitStack
import concourse.bass as bass
import concourse.mybir as mybir
from concourse.bass_utils import run_bass_kernel_spmd
from concourse.alu_op_type import AluOpType as ALU

AF = mybir.ActivationFunctionType
F32 = mybir.dt.float32
BF16 = mybir.dt.bfloat16
AX = mybir.AxisListType

T = 2048
DM = 1024
NL = 2
PLE = 256
W_IN = 7168
EPS = 1e-6
NTT = T // 128
NTQ = T // 512

import os
BST = int(os.environ.get('BST', '3'))
ENGS = ['pe', 'act', 'dve', 'pool', 'sp']
SEM_LIM = 20000
N_EPOCH = 6
N_DMA_SEMS = 32


class Buf:
    __slots__ = ('name', 'writers', 'readers', 'round_deps')

    def __init__(self, name=''):
        self.name = name
        self.writers = []
        self.readers = []
        self.round_deps = []


class Op:
    __slots__ = ('eng', 'fn', 'deps', 'sig', 'idx', 'dma', 'sem', 'val', 'prev_dma', 'gidx')

    def __init__(self, eng, fn, dma):
        self.eng = eng
        self.fn = fn
        self.deps = set()
        self.sig = False
        self.dma = dma
        self.sem = None
        self.val = 0
        self.prev_dma = None


class Prog:
    def __init__(self, nc):
        self.nc = nc
        self.ops = {e: [] for e in ENGS}
        self.all = []
        self.dma_last = [None] * N_DMA_SEMS
        self.dma_cnt = [0] * N_DMA_SEMS
        self.dma_rr = 0
        self.dma_rr_sw = 0
        self.bar_deps = {e: set() for e in ENGS}

    def add(self, eng, fn, r=(), w=(), wp=(), dma=False):
        o = Op(eng, fn, dma)
        o.idx = len(self.ops[eng])
        o.gidx = len(self.all)
        deps = set(self.bar_deps[eng])
        self.bar_deps[eng] = set()
        for b in r:
            deps.update(b.writers)
        for b in w:
            d = set(b.writers) | set(b.readers)
            deps.update(d)
            b.round_deps = list(d)
        for b in wp:
            deps.update(b.round_deps)
            deps.update(b.readers)
            if b.writers:
                deps.add(b.writers[0])
        for b in r:
            b.readers.append(o)
        for b in w:
            b.writers = [o]
            b.readers = []
        for b in wp:
            b.writers.append(o)
        o.deps = deps
        if dma:
            half = N_DMA_SEMS // 2
            if eng == 'pool':
                s = half + self.dma_rr_sw
                self.dma_rr_sw = (self.dma_rr_sw + 1) % half
            else:
                s = self.dma_rr
                self.dma_rr = (self.dma_rr + 1) % half
            o.sem = ('dma', s)
            self.dma_cnt[s] += 16
            o.val = self.dma_cnt[s]
            o.prev_dma = self.dma_last[s]
            self.dma_last[s] = o
        self.ops[eng].append(o)
        self.all.append(o)
        return o

    def barrier(self):
        last = set()
        for e in ENGS:
            lst = [o for o in self.ops[e] if not o.dma]
            if lst:
                last.add(lst[-1])
        for s in range(N_DMA_SEMS):
            if self.dma_last[s] is not None:
                last.add(self.dma_last[s])
        for e in ENGS:
            self.bar_deps[e] = set(last)

    def pe(self, fn, **k):
        return self.add('pe', fn, **k)

    def act(self, fn, **k):
        return self.add('act', fn, **k)

    def dve(self, fn, **k):
        return self.add('dve', fn, **k)

    def pool(self, fn, **k):
        return self.add('pool', fn, **k)

    def dma(self, fn, eng='sp', **k):
        return self.add(eng, fn, dma=True, **k)

    def emit(self, final_wait=()):
        nc = self.nc
        for o in self.all:
            nd = set()
            for d in o.deps:
                if d.dma:
                    nd.add(d)
                    continue
                if d.eng == o.eng and o.eng == 'pe' and not o.dma:
                    continue
                nd.add(d)
            o.deps = nd
            for d in nd:
                d.sig = True
        with ExitStack() as st:
            csem = {}
            for e in ['pe', 'act', 'dve', 'pool']:
                csem[e] = [st.enter_context(nc.semaphore(f"c_{e}{i}")) for i in range(N_EPOCH)]
            dsem = [st.enter_context(nc.semaphore(f"d{i}")) for i in range(N_DMA_SEMS)]
            for e in ['pe', 'act', 'dve', 'pool']:
                k = 0
                for o in self.ops[e]:
                    if o.dma:
                        continue
                    if o.sig:
                        o.sem = ('c', e, k // SEM_LIM)
                        o.val = k % SEM_LIM + 1
                        k += 1
                assert k < SEM_LIM * N_EPOCH, (e, k)

            def semh(s):
                return dsem[s[1]] if s[0] == 'dma' else csem[s[1]][s[2]]

            block = st.enter_context(nc.Block())
            fw = list(final_wait)

            def run(eng_name, h):
                waited = {}
                for o in self.ops[eng_name]:
                    need = {}
                    deps = list(o.deps)
                    if o.dma and o.prev_dma is not None:
                        deps.append(o.prev_dma)
                    for d in deps:
                        if d.sem is None:
                            continue
                        if need.get(d.sem, 0) < d.val:
                            need[d.sem] = d.val
                    for s, v in need.items():
                        if waited.get(s, 0) < v:
                            h.wait_ge(semh(s), v)
                            waited[s] = v
                    ins = o.fn(h)
                    if o.dma:
                        ins.then_inc(semh(o.sem), 16)
                    elif o.sig:
                        ins.then_inc(semh(o.sem), 1)
                if eng_name == 'sp':
                    for o in fw:
                        if waited.get(o.sem, 0) < o.val:
                            h.wait_ge(semh(o.sem), o.val)
                            waited[o.sem] = o.val

            @block.sync
            def _(h):
                run('sp', h)

            @block.tensor
            def _(h):
                run('pe', h)

            @block.scalar
            def _(h):
                run('act', h)

            @block.vector
            def _(h):
                run('dve', h)

            @block.gpsimd
            def _(h):
                run('pool', h)


A_Q, A_K, A_V, A_G = 0, 256, 512, 1024
B_Q, B_K, B_V, B_G = 1536, 2048, 2560, 3072
C_X, C_G = 3584, 4096
D_Q, D_FF, D_FB, D_I, D_G = 4608, 5120, 5632, 6144, 6656

R_CW, R_CB, R_BA, R_BX, R_LAM, R_LB, R_GN = 0, 4, 5, 7, 9, 11, 15
NR = 16
Q_C1, Q_C2, Q_LB, Q_OML = 16, 18, 20, 22
NQ = 24


class KB:
    def __init__(self, nc, nseq, nlayers, mask):
        self.nc = nc
        self.P = Prog(nc)
        self.nseq = nseq
        self.nlayers = nlayers
        self.mask = mask
        self.st = ExitStack()
        self.bank_rr = 0
        self.debug = False
        self.bank_set = list(range(8))
        self.dbg_outs = []

    def sb(self, name, shape, dt):
        return self.st.enter_context(self.nc.sbuf_tensor(name, shape, dt))

    def carve(self, shape, dt):
        n = int(np.prod(shape[1:]))
        nbytes = n * (4 if dt == F32 else 2)
        nbytes = (nbytes + 63) // 64 * 64
        w0 = self.aoff // 4
        assert self.aoff + nbytes <= self.arena_bytes, (self.aoff, nbytes, self.arena_bytes)
        self.aoff += nbytes
        ap = self.arena[0:shape[0], w0:w0 + nbytes // 4]
        if dt != F32:
            ap = ap.bitcast(dt)
        ap = ap[:, 0:n]
        if len(shape) == 3:
            ap = ap.rearrange("p (a b) -> p a b", a=shape[1])
        elif len(shape) == 4:
            ap = ap.rearrange("p (a b c) -> p a b c", a=shape[1], b=shape[2])
        return ap

    def phase(self):
        self.P.barrier()
        self.aoff = 0

    def dump(self, name, ap, bufs, dt=F32):
        if not getattr(self, 'debug', False):
            return
        shape = list(ap.shape)
        d = self.nc.dram_tensor("dbg_" + name, shape, dt, kind="ExternalOutput").ap()
        self.dbg_outs.append(self.dma(d, ap, r=bufs))

    def bank(self):
        bs = self.bank_set
        self.bank_rr = (self.bank_rr + 1) % len(bs)
        i = bs[self.bank_rr]
        return self.psum[i], self.psum_bf[i], self.pbuf[i]

    def bankx(self, i):
        return self.psum[i], self.psum_bf[i], self.pbuf[i]

    def declare(self):
        nc = self.nc
        ns = self.nseq

        def din(name, shape):
            return nc.dram_tensor(name, list(shape), F32, kind="ExternalInput").ap()

        self.x = din("x", (ns, T, DM))
        self.p = din("p", (NL, ns, T, PLE))
        self.norm_mix = din("norm_mix", (NL, DM))
        self.w_in = din("w_in", (NL, DM, W_IN))
        self.ret_logit = din("ret_decay_logit", (NL, 2, 4))
        self.rpb = din("nat_rpb", (NL, 8, 15, 31))
        self.conv_w = din("lru_conv_w", (NL, 4, 512))
        self.conv_b = din("lru_conv_b", (NL, 512))
        self.wa = din("lru_wa", (NL, 2, 8, 64, 64))
        self.ba = din("lru_ba", (NL, 2, 512))
        self.wx = din("lru_wx", (NL, 2, 8, 64, 64))
        self.bx = din("lru_bx", (NL, 2, 512))
        self.lam = din("lru_lambda", (NL, 2, 512))
        self.lbl = din("hgrn_lb_logits", (NL, 2, 512))
        self.gn = din("hgrn_norm", (NL, 512))
        self.w_br = din("w_branch", (NL, 4, 512, DM))
        self.w_mg = din("w_merge", (NL, 4, DM, DM))
        self.w_o = din("w_out", (NL, DM, DM))
        self.ple_norm = din("ple_norm", (NL, DM))
        self.w_pg = din("w_ple_gate", (NL, DM, DM))
        self.w_pp = din("w_ple_proj", (NL, PLE, DM))
        self.final_norm = din("final_norm", (DM,))
        self.c_ident = din("c_ident", (128, 128))
        self.c_maskF = din("c_maskF", (128, 128))
        self.c_cos = din("c_cos", (128, T))
        self.c_sin = din("c_sin", (128, T))
        self.c_ret = din("c_ret", (6, 128, 128))
        self.c_maskB = din("c_maskB", (128, 128))
        self.y = nc.dram_tensor("y", [ns, T, DM], F32, kind="ExternalOutput").ap()
        self.xs = nc.dram_tensor("xs", [ns, T, DM], F32, kind="Internal").ap()
        self.ebs = nc.dram_tensor("ebs", [NL, 128, 8, 14, 64], F32, kind="Internal").ap()
        self.B_ebs = Buf('ebs')

    def alloc(self):
        nc = self.nc
        self.ident_f = self.sb("ident_f", [128, 128], F32)
        self.ident_b = self.sb("ident_b", [128, 128], BF16)
        self.ones_f = self.sb("ones_f", [128, 128], F32)
        self.ones_b = self.sb("ones_b", [128, 128], BF16)
        self.prm = self.sb("prm", [128, NL, 4, NQ], F32)
        self.hT = self.sb("hT", [128, 8, T], BF16)
        self.merged = self.sb("merged", [128, 8, T], F32)
        self.brT = self.sb("brT", [128, 4, T], BF16)
        self.B_hT = Buf('hT')
        self.B_merged = [[Buf() for _ in range(NTQ)] for _ in range(8)]
        self.B_brT = [Buf() for _ in range(4)]
        self.B_const = Buf('const')
        self.B_prm = Buf('prm')
        used = 128 * 4 * 2 + 128 * 2 * 2 + NL * 4 * NQ * 4 + 8 * T * 2 + 8 * T * 4 + 4 * T * 2
        self.arena_bytes = (207 * 1024 - used) // 64 * 64
        self.arena = self.sb("arena", [128, self.arena_bytes // 4], F32)
        self.aoff = 0
        self.psum = []
        self.psum_bf = []
        self.pbuf = []
        for i in range(8):
            t = self.st.enter_context(nc.psum_tensor(f"ps{i}", [128, 512], F32))
            self.psum.append(t)
            self.psum_bf.append(t[:].bitcast(BF16))
            self.pbuf.append(Buf(f'ps{i}'))

    def mm(self, out, lhsT, rhs, start, stop, r, wb, **kw):
        d = {'w': [wb]} if start else {'wp': [wb]}
        return self.P.pe(lambda h: h.matmul(out, lhsT=lhsT, rhs=rhs, start=start, stop=stop, **kw), r=r, **d)

    def mm_group(self, out_ap, lhs_list, rhs_list, r, wb):
        n = len(lhs_list)
        for k in range(n):
            self.mm(out_ap, lhs_list[k], rhs_list[k], k == 0, k == n - 1, r, wb)

    def tr(self, out, in_, ident, r, wb, first=True):
        d = {'w': [wb]} if first else {'wp': [wb]}
        return self.P.pe(lambda h: h.transpose(out=out, in_=in_, identity=ident), r=r, **d)

    def actf(self, out, in_, func, r, w=(), wp=(), **kw):
        return self.P.act(lambda h: h.activation(out=out, in_=in_, func=func, **kw), r=r, w=w, wp=wp)

    def tt(self, out, in0, in1, op, r, w=(), wp=(), eng='dve'):
        return self.P.add(eng, lambda h: h.tensor_tensor(out=out, in0=in0, in1=in1, op=op), r=r, w=w, wp=wp)

    def ts(self, out, in0, s1, s2, op0, op1=None, r=(), w=(), wp=(), eng='dve'):
        if op1 is None:
            return self.P.add(eng, lambda h: h.tensor_scalar(out=out, in0=in0, scalar1=s1, scalar2=None, op0=op0), r=r, w=w, wp=wp)
        return self.P.add(eng, lambda h: h.tensor_scalar(out=out, in0=in0, scalar1=s1, scalar2=s2, op0=op0, op1=op1), r=r, w=w, wp=wp)

    def stt(self, out, in0, scalar, in1, op0, op1, r, w=(), wp=()):
        return self.P.dve(lambda h: h.scalar_tensor_tensor(out=out, in0=in0, scalar=scalar, in1=in1, op0=op0, op1=op1), r=r, w=w, wp=wp)

    def cp(self, out, in_, r, w=(), wp=(), eng='dve'):
        if eng == 'act':
            return self.P.act(lambda h: h.copy(out=out, in_=in_), r=r, w=w, wp=wp)
        return self.P.add(eng, lambda h: h.tensor_copy(out=out, in_=in_), r=r, w=w, wp=wp)

    def mset(self, ap, val, w=(), wp=(), eng='dve'):
        return self.P.add(eng, lambda h: h.memset(ap, val), w=w, wp=wp)

    def recip(self, out, in_, r, w=(), wp=()):
        return self.P.dve(lambda h: h.reciprocal(out=out, in_=in_), r=r, w=w, wp=wp)

    def scan(self, out, d0, d1, r, w=(), wp=()):
        return self.P.dve(lambda h: h.tensor_tensor_scan(out=out, data0=d0, data1=d1, initial=0.0, op0=ALU.mult, op1=ALU.add), r=r, w=w, wp=wp)

    def dma(self, out, in_, r=(), w=(), wp=(), eng='sp'):
        return self.P.dma(lambda h: h.dma_start(out=out, in_=in_), eng=eng, r=r, w=w, wp=wp)

    def mk_wpool(self, n, words):
        return {'bufs': [(self.carve([128, words], BF16), Buf()) for _ in range(n)], 'rr': 0}

    def wload(self, src, nkc, ncols=128, eng='pool', pool=None):
        if pool is None:
            pool = self.wpool
        i = pool['rr']
        pool['rr'] = (i + 1) % len(pool['bufs'])
        t, b = pool['bufs'][i]
        v = t[:, 0:nkc * ncols].rearrange("p (k c) -> p k c", k=nkc)
        self.dma(v, src.rearrange("(k p) c -> p k c", p=128), w=[b], eng=eng)
        return v, b

    def setup_wbufs(self, n, words):
        self.wpool = self.mk_wpool(n, words)

    def phase0(self):
        self.phase()
        self.dma(self.ident_f[:], self.c_ident[:, :], w=[self.B_const])
        self.dma(self.ident_b[:], self.c_ident[:, :], wp=[self.B_const], eng='pool')
        self.mset(self.ones_f[:], 1.0, wp=[self.B_const])
        self.mset(self.ones_b[:], 1.0, wp=[self.B_const])
        stg = self.carve([NR, 512], F32)
        Bs = Buf()
        for l in range(NL):
            rows = [(R_CW, self.conv_w[l], 4), (R_CB, self.conv_b[l:l + 1], 1), (R_BA, self.ba[l], 2),
                    (R_BX, self.bx[l], 2), (R_LAM, self.lam[l], 2), (R_LB, self.lbl[0], 2),
                    (R_LB + 2, self.lbl[1], 2), (R_GN, self.gn[l:l + 1], 1)]
            first = True
            for r0, src, n in rows:
                if first:
                    self.dma(stg[r0:r0 + n, :], src, w=[Bs])
                else:
                    self.dma(stg[r0:r0 + n, :], src, wp=[Bs])
                first = False
            for fc in range(4):
                ps, _, pb = self.bank()
                self.tr(ps[:, 0:NR], stg[0:NR, fc * 128:(fc + 1) * 128], self.ident_f[0:NR, 0:NR], [Bs, self.B_const], pb)
                self.cp(self.prm[:, l, fc, 0:NR], ps[:, 0:NR], r=[pb], wp=[self.B_prm])
        tmp = self.carve([128, NL, 4, 2], F32)
        tmp2 = self.carve([128, 4, 2], F32)
        Bt = Buf()
        self.softplus_neg(self.prm[:, :, :, R_LAM:R_LAM + 2], tmp, [128, NL, 4, 2], self.B_prm, Bt)
        self.ts(self.prm[:, :, :, Q_C1:Q_C1 + 2], tmp, -8.0, None, ALU.mult, r=[Bt], wp=[self.B_prm])
        self.ts(self.prm[:, :, :, Q_C2:Q_C2 + 2], tmp, -16.0, None, ALU.mult, r=[Bt], wp=[self.B_prm])
        self.mset(self.prm[:, 0, :, Q_LB:Q_LB + 2], 0.0, wp=[self.B_prm])
        self.mset(self.prm[:, 0, :, Q_OML:Q_OML + 2], 1.0, wp=[self.B_prm])
        if NL > 1:
            B2 = Buf()
            self.tt(tmp2, self.prm[:, 1, :, R_LB + 2:R_LB + 4], self.prm[:, 1, :, R_LB:R_LB + 2], ALU.subtract,
                    r=[self.B_prm], w=[B2])
            self.actf(self.prm[:, 1, :, Q_LB:Q_LB + 2], tmp2, AF.Sigmoid, r=[B2], wp=[self.B_prm])
            self.actf(self.prm[:, 1, :, Q_OML:Q_OML + 2], tmp2, AF.Sigmoid, r=[B2], wp=[self.B_prm], scale=-1.0)

    def build_nat_tables(self):
        self.phase()
        BT = self.carve([64, 120, 64], F32)
        ebt = [(self.carve([128, 8, 64], F32), Buf()) for _ in range(2)]
        BBT = Buf()
        first_out = True
        for l in range(NL):
            self.mset(BT, -30000.0, w=[BBT])
            src2 = self.rpb[l].rearrange("h r m -> (h r) m")
            for qc in range(64):
                ws = min(max(qc - 8, 0), 48)
                a = 15 - qc + ws
                s_ap = src2[:, a:a + 16]
                s_ap = bass.AP(s_ap.tensor, s_ap.offset, [[0, 1]] + [list(x) for x in s_ap.ap])
                self.dma(BT[qc:qc + 1, :, ws:ws + 16], s_ap, wp=[BBT])
            gi = 0
            for h in range(8):
                for p0 in (0, 8):
                    npair = 8 if p0 == 0 else 6
                    ps, _, pb = self.bank()
                    for pi in range(npair):
                        p = p0 + pi
                        in_ = BT[:, h * 15 + p:h * 15 + p + 2, :].rearrange("q a k -> q (a k)")
                        self.tr(ps[:, pi * 64:(pi + 1) * 64], in_, self.ident_f[0:64, 0:64], [BBT, self.B_const], pb, first=(pi == 0))
                    et, Bet = ebt[gi % 2]
                    gi += 1
                    self.actf(et[:, 0:npair, :], ps[:, 0:npair * 64].rearrange("p (a b) -> p a b", a=npair), AF.Exp, r=[pb], w=[Bet])
                    kw = {'w': [self.B_ebs]} if first_out else {'wp': [self.B_ebs]}
                    first_out = False
                    self.dma(self.ebs[l, :, h, p0:p0 + npair, :], et[:, 0:npair, :], r=[Bet], **kw)

    def softplus_neg(self, x_ap, out_ap, shape, Bx, Bo):
        e = self.carve(shape, F32)
        L = self.carve(shape, F32)
        u = self.carve(shape, F32)
        u2 = self.carve(shape, F32)
        q = self.carve(shape, F32)
        Be, BL, Bu, Bu2, Bq = Buf(), Buf(), Buf(), Buf(), Buf()
        self.actf(e, x_ap, AF.Exp, r=[Bx], w=[Be], scale=-1.0)
        self.actf(L, e, AF.Ln, r=[Be], w=[BL], bias=1.0)
        self.ts(u, e, 2.0, None, ALU.add, r=[Be], w=[Bu])
        self.recip(u, u, r=[Bu], w=[Bu])
        self.tt(u, u, e, ALU.mult, r=[Bu, Be], w=[Bu])
        self.tt(u2, u, u, ALU.mult, r=[Bu], w=[Bu2])
        self.ts(q, u2, 1.0 / 9, 1.0 / 7, ALU.mult, ALU.add, r=[Bu2], w=[Bq])
        for c in (1.0 / 5, 1.0 / 3, 1.0):
            self.tt(q, q, u2, ALU.mult, r=[Bq, Bu2], w=[Bq])
            self.ts(q, q, c, None, ALU.add, r=[Bq], w=[Bq])
        self.tt(q, q, u, ALU.mult, r=[Bq, Bu], w=[Bq])
        self.ts(q, q, 2.0, None, ALU.mult, r=[Bq], w=[Bq])
        self.tt(q, q, L, ALU.subtract, r=[Bq, BL], w=[Bq])
        self.ts(u2, e, 0.3, None, ALU.is_lt, r=[Be], w=[Bu2])
        self.tt(q, q, u2, ALU.mult, r=[Bq, Bu2], w=[Bq])
        self.tt(out_ap, q, L, ALU.add, r=[Bq, BL], w=[Bo])

    def rms_stats(self, xt, Bx, junk, Bj, ss, rs, Bss, Brs):
        self.actf(junk, xt, AF.Square, r=[Bx], w=[Bj, Bss], accum_out=ss)
        self.actf(rs, ss, AF.Sqrt, r=[Bss], w=[Brs], scale=1.0 / DM, bias=EPS)
        self.recip(rs, rs, r=[Brs], w=[Brs])

    def phase1(self, s, l):
        self.phase()
        xsrc = self.x[s] if l == 0 else self.xs[s]
        gbc = self.carve([128, DM], F32)
        Bg = Buf()
        self.dma(gbc, self.norm_mix[l:l + 1, :].broadcast_to([128, DM]), w=[Bg])
        junk = self.carve([128, DM], BF16)
        Bj = Buf()
        xts = [(self.carve([128, DM], F32), Buf()) for _ in range(3)]
        hns = [(self.carve([128, DM], BF16), Buf()) for _ in range(2)]
        sts = [(self.carve([128, 1], F32), self.carve([128, 1], F32), Buf(), Buf()) for _ in range(2)]
        for tt in range(NTT):
            xt, Bx = xts[tt % 3]
            hn, Bh = hns[tt % 2]
            ss, rs, Bss, Brs = sts[tt % 2]
            self.dma(xt, xsrc[tt * 128:(tt + 1) * 128, :], w=[Bx])
            self.rms_stats(xt, Bx, junk, Bj, ss, rs, Bss, Brs)
            self.stt(hn, xt, rs, gbc, ALU.mult, ALU.mult, r=[Bx, Brs, Bg], w=[Bh])
            ps, psb, pb = self.bank()
            for kc in range(8):
                self.tr(psb[:, kc * 128:(kc + 1) * 128], hn[:, kc * 128:(kc + 1) * 128], self.ident_b[:], [Bh, self.B_const], pb, first=(kc == 0))
            kw = {'w': [self.B_hT]} if tt == 0 else {'wp': [self.B_hT]}
            self.cp(self.hT[:, :, tt * 128:(tt + 1) * 128], psb.rearrange("p (k c) -> p k c", k=8), r=[pb], eng='act', **kw)
        self.dump("hT", self.hT[:], [self.B_hT], BF16)

    def gen_merge(self, l, j, first, src=None, Bsrc=None, nw=4, nt=2, res=None):
        if src is None:
            src, Bsrc = self.brT, self.B_brT
        if res is None:
            res = self.merge_res(nw, nt)
        wp, sgs, tmps = res
        i = 0
        for dmc in range(8):
            wg, Bwg = self.wload(self.w_mg[l, j][:, dmc * 128:(dmc + 1) * 128], 8, pool=wp)
            wb, Bwb = self.wload(self.w_br[l, j][:, dmc * 128:(dmc + 1) * 128], 4, pool=wp)
            for tq in range(NTQ):
                sl = slice(tq * 512, (tq + 1) * 512)
                psg, _, pbg = self.bank()
                psp, _, pbp = self.bank()
                self.mm_group(psg[:, :], [wg[:, k, :] for k in range(8)], [self.hT[:, k, sl] for k in range(8)], [Bwg, self.B_hT], pbg)
                self.mm_group(psp[:, :], [wb[:, k, :] for k in range(4)], [src[:, k, sl] for k in range(4)], [Bwb] + list(Bsrc), pbp)
                sg, Bsg = sgs[i % len(sgs)]
                tmp, Btmp = tmps[i % len(tmps)]
                i += 1
                self.actf(sg, psg[:, :], AF.Sigmoid, r=[pbg], w=[Bsg])
                Bm = self.B_merged[dmc][tq]
                if first:
                    self.tt(self.merged[:, dmc, sl], psp[:, :], sg, ALU.mult, r=[pbp, Bsg], w=[Bm])
                else:
                    self.tt(tmp, psp[:, :], sg, ALU.mult, r=[pbp, Bsg], w=[Btmp])
                    self.tt(self.merged[:, dmc, sl], self.merged[:, dmc, sl], tmp, ALU.add, r=[Bm, Btmp], w=[Bm], eng='pool')
                yield
        self.dump(f"merged{j}", self.merged[:], [b for row in self.B_merged for b in row])

    def merge_res(self, nw, nt):
        return (self.mk_wpool(nw, 1024), [(self.carve([128, 512], F32), Buf()) for _ in range(nt)],
                [(self.carve([128, 512], F32), Buf()) for _ in range(nt)])

    def phase_merge(self, l, j, first):
        self.phase()
        for _ in self.gen_merge(l, j, first):
            pass

    def phase3(self, s, l, have_merged):
        self.phase()
        last = (l == self.nlayers - 1)
        xsrc = self.x[s] if l == 0 else self.xs[s]
        wo = self.carve([128, 8, DM], BF16)
        wpg = self.carve([128, 8, DM], BF16)
        wpp = self.carve([128, 2, DM], BF16)
        Bwo, Bwpg, Bwpp = Buf(), Buf(), Buf()
        if have_merged:
            self.dma(wo, self.w_o[l].rearrange("(k p) c -> p k c", p=128), w=[Bwo], eng='pool')
        self.dma(wpg, self.w_pg[l].rearrange("(k p) c -> p k c", p=128), w=[Bwpg], eng='pool')
        self.dma(wpp, self.w_pp[l].rearrange("(k p) c -> p k c", p=128), w=[Bwpp], eng='pool')
        gple = self.carve([128, DM], F32)
        gfin = self.carve([128, DM], F32)
        Bg = Buf()
        self.dma(gple, self.ple_norm[l:l + 1, :].broadcast_to([128, DM]), w=[Bg])
        self.dma(gfin, self.final_norm.rearrange("(a c) -> a c", a=1).broadcast_to([128, DM]), wp=[Bg])
        NB = 3
        xts = [(self.carve([128, DM], F32), Buf()) for _ in range(NB)]
        mbs = [(self.carve([128, 8, 128], BF16), Buf()) for _ in range(NB)]
        n2s = [(self.carve([128, DM], BF16), Buf()) for _ in range(NB)]
        n2Ts = [(self.carve([128, 8, 128], BF16), Buf()) for _ in range(NB)]
        pts = [(self.carve([128, PLE], F32), Buf()) for _ in range(NB)]
        pbs = [(self.carve([128, PLE], BF16), Buf()) for _ in range(NB)]
        pTs = [(self.carve([128, 2, 128], BF16), Buf()) for _ in range(NB)]
        gts = [(self.carve([128, DM], F32), Buf()) for _ in range(NB)]
        sts = [[(self.carve([128, 1], F32), self.carve([128, 1], F32), Buf(), Buf()) for _ in range(2)] for _ in range(NB)]
        outs = []

        def s1(tt):
            b = tt % NB
            tsl = slice(tt * 128, (tt + 1) * 128)
            xt, Bx = xts[b]
            self.dma(xt, xsrc[tsl, :], w=[Bx])
            if have_merged:
                mb, Bmb = mbs[b]
                self.cp(mb, self.merged[:, :, tsl], r=[self.B_merged[d][tt // 4] for d in range(8)], w=[Bmb], eng='dve')
                for hf in range(2):
                    ps, _, pb = self.bank()
                    self.mm_group(ps[:, :], [mb[:, k, :] for k in range(8)], [wo[:, k, hf * 512:(hf + 1) * 512] for k in range(8)], [Bmb, Bwo], pb)
                    self.tt(xt[:, hf * 512:(hf + 1) * 512], xt[:, hf * 512:(hf + 1) * 512], ps[:, :], ALU.add, r=[Bx, pb], w=[Bx])
            ss, rs, Bss, Brs = sts[b][0]
            n2, Bn2 = n2s[b]
            self.rms_stats(xt, Bx, n2, Bn2, ss, rs, Bss, Brs)
            self.stt(n2, xt, rs, gple, ALU.mult, ALU.mult, r=[Bx, Brs, Bg], w=[Bn2])
            ps, psb, pb = self.bank()
            for kc in range(8):
                self.tr(psb[:, kc * 128:(kc + 1) * 128], n2[:, kc * 128:(kc + 1) * 128], self.ident_b[:], [Bn2, self.B_const], pb, first=(kc == 0))
            n2T, Bn2T = n2Ts[b]
            self.cp(n2T, psb.rearrange("p (k c) -> p k c", k=8), r=[pb], w=[Bn2T], eng='act')
            pt, Bpt = pts[b]
            pbf, Bpbf = pbs[b]
            pT, BpT = pTs[b]
            self.dma(pt, self.p[l, s, tsl, :], w=[Bpt])
            self.cp(pbf, pt, r=[Bpt], w=[Bpbf], eng='pool')
            ps2, psb2, pb2 = self.bank()
            for c in range(2):
                self.tr(psb2[:, c * 128:(c + 1) * 128], pbf[:, c * 128:(c + 1) * 128], self.ident_b[:], [Bpbf, self.B_const], pb2, first=(c == 0))
            self.cp(pT, psb2[:, 0:256].rearrange("p (k c) -> p k c", k=2), r=[pb2], w=[BpT], eng='act')
            return (tt, b, tsl, xt, Bx, n2T, Bn2T, pT, BpT)

        def s2(ctx):
            tt, b, tsl, xt, Bx, n2T, Bn2T, pT, BpT = ctx
            gt, Bgt = gts[b]
            for hf in range(2):
                hs = slice(hf * 512, (hf + 1) * 512)
                psg, _, pbg = self.bank()
                psp, _, pbp = self.bank()
                self.mm_group(psg[:, :], [n2T[:, k, :] for k in range(8)], [wpg[:, k, hs] for k in range(8)], [Bn2T, Bwpg], pbg)
                self.mm_group(psp[:, :], [pT[:, k, :] for k in range(2)], [wpp[:, k, hs] for k in range(2)], [BpT, Bwpp], pbp)
                self.actf(gt[:, hs], psg[:, :], AF.Sigmoid, r=[pbg], w=[Bgt] if hf == 0 else (), wp=() if hf == 0 else [Bgt])
                self.tt(gt[:, hs], gt[:, hs], psp[:, :], ALU.mult, r=[Bgt, pbp], w=[Bgt])
            self.tt(xt, xt, gt, ALU.add, r=[Bx, Bgt], w=[Bx])
            if last:
                ss, rs, Bss, Brs = sts[b][1]
                n2, Bn2 = n2s[b]
                self.rms_stats(xt, Bx, n2, Bn2, ss, rs, Bss, Brs)
                self.stt(xt, xt, rs, gfin, ALU.mult, ALU.mult, r=[Bx, Brs, Bg], w=[Bx])
                outs.append(self.dma(self.y[s, tsl, :], xt, r=[Bx]))
            else:
                outs.append(self.dma(self.xs[s, tsl, :], xt, r=[Bx]))

        prev = None
        for tt in range(NTT):
            ctx = s1(tt)
            if prev is not None:
                s2(prev)
            prev = ctx
        s2(prev)
        return outs

    def branch_C(self, l):
        self.phase()
        for _ in self.gen_C(l):
            pass

    def gen_C(self, l):
        self.setup_wbufs(3, 1024)
        X = self.carve([128, T + 4], F32)
        XC, R, I, A, S, HF = [self.carve([128, T], F32) for _ in range(6)]
        BX, BXC, BR, BI, BA, BS, BHF = [Buf() for _ in range(7)]
        Wd = [[(self.carve([128, 128], F32), Buf()) for _ in range(2)] for _ in range(2)]
        pr = self.prm
        for fc in range(4):
            for d in range(2):
                for g, wsrc in enumerate((self.wa, self.wx)):
                    wt, Bw_ = Wd[d][g]
                    self.mset(wt, 0.0, w=[Bw_])
                    self.dma(wt[0:64, 0:64], wsrc[l, d, 2 * fc], wp=[Bw_])
                    self.dma(wt[64:128, 64:128], wsrc[l, d, 2 * fc + 1], wp=[Bw_])
            w, Bw = self.wload(self.w_in[l][:, C_X + fc * 128:C_X + (fc + 1) * 128], 8)
            self.mset(X[:, 0:2], 0.0, w=[BX])
            self.mset(X[:, T + 2:T + 4], 0.0, wp=[BX])
            for tq in range(NTQ):
                sl = slice(tq * 512, (tq + 1) * 512)
                ps, _, pb = self.bank()
                self.mm_group(ps[:, :], [w[:, k, :] for k in range(8)], [self.hT[:, k, sl] for k in range(8)], [Bw, self.B_hT], pb)
                self.cp(X[:, 2 + tq * 512:2 + (tq + 1) * 512], ps[:, :], r=[pb], wp=[BX], eng='act')
            yield
            self.ts(XC, X[:, 0:T], pr[:, l, fc, R_CW:R_CW + 1], pr[:, l, fc, R_CB:R_CB + 1], ALU.mult, ALU.add, r=[BX, self.B_prm], w=[BXC])
            for j in range(1, 4):
                self.stt(XC, X[:, j:T + j], pr[:, l, fc, R_CW + j:R_CW + j + 1], XC, ALU.mult, ALU.add, r=[BX, BXC, self.B_prm], w=[BXC])
            yield
            for d in range(2):
                for g, (dst, Bd, rb) in enumerate(((R, BR, R_BA), (I, BI, R_BX))):
                    wt, Bw_ = Wd[d][g]
                    for tq in range(NTQ):
                        sl = slice(tq * 512, (tq + 1) * 512)
                        ps, _, pb = self.bank()
                        self.mm(ps[:, :], wt, XC[:, sl], True, True, [Bw_, BXC], pb)
                        kw = {'w': [Bd]} if tq == 0 else {'wp': [Bd]}
                        self.actf(dst[:, sl], ps[:, :], AF.Sigmoid, r=[pb, self.B_prm], bias=pr[:, l, fc, rb + d:rb + d + 1], **kw)
                yield
                self.actf(A, R, AF.Exp, r=[BR, self.B_prm], w=[BA], scale=pr[:, l, fc, Q_C1 + d:Q_C1 + d + 1])
                self.actf(S, R, AF.Exp, r=[BR, self.B_prm], w=[BS], scale=pr[:, l, fc, Q_C2 + d:Q_C2 + d + 1])
                self.actf(S, S, AF.Sqrt, r=[BS], w=[BS], scale=-1.0, bias=1.0)
                self.tt(I, I, XC, ALU.mult, r=[BI, BXC], w=[BI])
                self.tt(I, I, S, ALU.mult, r=[BI, BS], w=[BI])
                yield
                if d == 0:
                    self.scan(HF, A, I, r=[BA, BI], w=[BHF])
                else:
                    self.scan(S[:, ::-1], A[:, ::-1], I[:, ::-1], r=[BA, BI], w=[BS])
                    self.tt(HF, HF, S, ALU.add, r=[BHF, BS], w=[BHF])
            yield
            if fc == 0:
                assert getattr(self, 'brT_free', True), "pending merge still reads brT"
            if fc == 3:
                self.dump("C_R", R, [BR])
                self.dump("C_B", I, [BI])
                self.dump("prm", self.prm[:], [self.B_prm])
            w, Bw = self.wload(self.w_in[l][:, C_G + fc * 128:C_G + (fc + 1) * 128], 8)
            for tq in range(NTQ):
                sl = slice(tq * 512, (tq + 1) * 512)
                ps, _, pb = self.bank()
                self.mm_group(ps[:, :], [w[:, k, :] for k in range(8)], [self.hT[:, k, sl] for k in range(8)], [Bw, self.B_hT], pb)
                kw = {'w': [BR]} if tq == 0 else {'wp': [BR]}
                self.actf(R[:, sl], ps[:, :], AF.Silu, r=[pb], **kw)
            self.tt(self.brT[:, fc, :], HF, R, ALU.mult, r=[BHF, BR], w=[self.B_brT[fc]])
            if fc == 3:
                self.dump("C_XC", XC, [BXC])
                self.dump("C_HF", HF, [BHF])
                self.dump("C_A", A, [BA])
                self.dump("C_X", X, [BX])
        self.dump("C_brT", self.brT[:], self.B_brT, BF16)

    def branch_D(self, l):
        self.phase()
        self.bank_set = [4, 5, 6, 7]
        self.setup_wbufs(3, 1024)
        pr = self.prm
        maskF = self.carve([128, 128], F32)
        maskB = self.carve([128, 128], F32)
        M0 = self.carve([128, 512], BF16)
        M1 = self.carve([128, 512], BF16)
        Bk = Buf()
        self.dma(maskF, self.c_maskF[:, :], w=[Bk])
        self.dma(maskB, self.c_maskB[:, :], wp=[Bk])
        self.mset(M0, 1.0, wp=[Bk])
        self.mset(M0.rearrange("p (c j) -> p c j", j=32)[:, :, 0:1], 0.0, wp=[Bk])
        self.mset(M1, 1.0, wp=[Bk])
        self.mset(M1.rearrange("p (c j) -> p c j", j=32)[:, :, 31:32], 0.0, wp=[Bk])
        Q, E, G, Bc, O = [self.carve([128, T], F32) for _ in range(5)]
        qt, kh = [self.carve([128, T], BF16) for _ in range(2)]
        khtok = self.carve([128, NTT, 128], BF16)
        vtok = self.carve([128, NTT, 128], BF16)
        Sall = self.carve([128, 65, 128], BF16)
        Sm2 = [self.carve([128, 128], F32) for _ in range(2)]
        bl = self.carve([128, 64], F32)
        ac = self.carve([128, 64], F32)
        attms = [(self.carve([128, 512], BF16), Buf()) for _ in range(2)]
        BQ, BE, BG, BBc, BO, Bqt, Bkh, Bkhtok, Bvtok, BSall, BSm, Bbl, Bac = [Buf() for _ in range(13)]
        BSm2 = [Buf(), Buf()]
        BE2, BG2, BBc2, Bkh2, Bqt2, Bkhtok2, Bbl2, Bac2 = [[Buf(), Buf()] for _ in range(8)]
        bl_bc = bass.AP(bl.tensor, bl.offset, [list(bl.ap[0]), [1, 64], [0, 32]])
        v3 = lambda a: a.rearrange("p (c j) -> p c j", j=32)
        for h in range(4):
            cs = slice(h * 128, (h + 1) * 128)
            w, Bw = self.wload(self.w_in[l][:, D_Q + h * 128:D_Q + (h + 1) * 128], 8)
            for tq in range(NTQ):
                sl = slice(tq * 512, (tq + 1) * 512)
                ps, _, pb = self.bank()
                self.mm_group(ps[:, :], [w[:, k, :] for k in range(8)], [self.hT[:, k, sl] for k in range(8)], [Bw, self.B_hT], pb)
                kw = {'w': [BQ]} if tq == 0 else {'wp': [BQ]}
                self.actf(Q[:, sl], ps[:, :], AF.Silu, r=[pb], **kw)
            w, Bw = self.wload(self.w_in[l][:, D_I + h * 128:D_I + (h + 1) * 128], 8)
            for g4 in range(4):
                ps, _, pb = self.bank()
                for i4 in range(4):
                    tt = g4 * 4 + i4
                    for k in range(8):
                        self.mm(ps[:, i4 * 128:(i4 + 1) * 128], self.hT[:, k, tt * 128:(tt + 1) * 128], w[:, k, :], k == 0, k == 7,
                                [Bw, self.B_hT], pb) if (i4 == 0 and k == 0) else \
                            self.P.pe(lambda hh, o_=ps[:, i4 * 128:(i4 + 1) * 128], a_=self.hT[:, k, tt * 128:(tt + 1) * 128], b_=w[:, k, :], s_=(k == 0), e_=(k == 7):
                                      hh.matmul(o_, lhsT=a_, rhs=b_, start=s_, stop=e_), r=[Bw, self.B_hT], wp=[pb])
                kw = {'w': [Bvtok]} if g4 == 0 else {'wp': [Bvtok]}
                self.cp(vtok[:, g4 * 4:(g4 + 1) * 4, :], ps[:, :].rearrange("p (a b) -> p a b", a=4), r=[pb], eng='act', **kw)
            for d in range(2):
                zoff = (D_FF if d == 0 else D_FB) + h * 128
                w, Bw = self.wload(self.w_in[l][:, zoff:zoff + 128], 8)
                HS = [slice(0, 1024), slice(1024, 2048)]
                for tq in range(NTQ):
                    sl = slice(tq * 512, (tq + 1) * 512)
                    hf = tq // 2
                    ps, _, pb = self.bank()
                    self.mm_group(ps[:, :], [w[:, k, :] for k in range(8)], [self.hT[:, k, sl] for k in range(8)], [Bw, self.B_hT], pb)
                    kw = {'w': [BE2[hf], BE]} if tq % 2 == 0 else {'wp': [BE2[hf]]}
                    self.actf(E[:, sl], ps[:, :], AF.Sigmoid, r=[pb], **kw)
                for hf in range(2):
                    self.ts(E[:, HS[hf]], E[:, HS[hf]], pr[:, l, h, Q_OML + d:Q_OML + d + 1], pr[:, l, h, Q_LB + d:Q_LB + d + 1], ALU.mult, ALU.add,
                            r=[BE2[hf], self.B_prm], w=[BE2[hf]])
                for hf in range(2):
                    self.actf(G[:, HS[hf]], E[:, HS[hf]], AF.Ln, r=[BE2[hf]], w=[BG2[hf], BG])
                for hf in range(2):
                    self.ts(E[:, HS[hf]], E[:, HS[hf]], -1.0, 1.0, ALU.mult, ALU.add, r=[BE2[hf]], w=[BE2[hf]])
                for tq in range(NTQ):
                    sl = slice(tq * 512, (tq + 1) * 512)
                    hf = tq // 2
                    kw = {'w': [BBc2[hf]]} if tq % 2 == 0 else {'wp': [BBc2[hf]]}
                    if d == 0:
                        self.scan(Bc[:, sl], M0, G[:, sl], r=[Bk, BG2[hf]], **kw)
                    else:
                        self.scan(Bc[:, sl][:, ::-1], M1[:, ::-1], G[:, sl][:, ::-1], r=[Bk, BG2[hf]], **kw)
                edge = 31 if d == 0 else 0
                CS = [slice(0, 32), slice(32, 64)]
                for hf in range(2):
                    self.cp(bl[:, CS[hf]], v3(Bc)[:, CS[hf], edge], r=[BBc2[hf]], w=[Bbl2[hf]])
                for hf in range(2):
                    self.actf(ac[:, CS[hf]], bl[:, CS[hf]], AF.Exp, r=[Bbl2[hf]], w=[Bac2[hf]])
                for hf in range(2):
                    blh = bl[:, CS[hf]]
                    blh_bc = bass.AP(blh.tensor, blh.offset, [list(blh.ap[0]), [1, 32], [0, 32]])
                    self.tt(v3(G)[:, CS[hf], :], blh_bc, v3(Bc)[:, CS[hf], :], ALU.subtract, r=[Bbl2[hf], BBc2[hf], BG2[hf]], w=[BG2[hf]])
                for hf in range(2):
                    self.actf(G[:, HS[hf]], G[:, HS[hf]], AF.Exp, r=[BG2[hf]], w=[BG2[hf]])
                for hf in range(2):
                    self.tt(kh[:, HS[hf]], E[:, HS[hf]], G[:, HS[hf]], ALU.mult, r=[BE2[hf], BG2[hf]], w=[Bkh2[hf]])
                for g8 in range(2):
                    ps, psb, pb = self.bank()
                    for i8 in range(8):
                        tt = g8 * 8 + i8
                        self.tr(psb[:, i8 * 128:(i8 + 1) * 128], kh[:, tt * 128:(tt + 1) * 128], self.ident_b[:], [Bkh2[g8], self.B_const], pb, first=(i8 == 0))
                    self.cp(khtok[:, g8 * 8:(g8 + 1) * 8, :], psb.rearrange("p (a b) -> p a b", a=8), r=[pb], eng='act', w=[Bkhtok2[g8]])
                for hf in range(2):
                    self.actf(G[:, HS[hf]], Bc[:, HS[hf]], AF.Exp, r=[BBc2[hf], BG2[hf]], w=[BG2[hf]])
                for hf in range(2):
                    self.tt(qt[:, HS[hf]], Q[:, HS[hf]], G[:, HS[hf]], ALU.mult, r=[BQ, BG2[hf]], w=[Bqt2[hf]])
                for hf in range(2):
                    self.actf(G[:, HS[hf]], Bc[:, HS[hf]], AF.Exp, r=[BBc2[hf], BG2[hf]], w=[BG2[hf]], scale=-1.0)
                for hf in range(2):
                    self.tt(kh[:, HS[hf]], E[:, HS[hf]], G[:, HS[hf]], ALU.mult, r=[BE2[hf], BG2[hf], Bkhtok2[hf]], w=[Bkh2[hf]])
                self.mset(Sm2[0], 0.0, w=[BSm2[0]])
                step = 0
                s0 = 0 if d == 0 else 64
                self.mset(Sall[:, s0, :], 0.0, w=[BSall])
                for rnd in range(4):
                    tiles = [rnd * 4 + i for i in range(4)] if d == 0 else [15 - rnd * 4 - i for i in range(4)]
                    corder = [0, 1, 2, 3] if d == 0 else [3, 2, 1, 0]
                    for bi, tt in enumerate(tiles):
                        for j in corder:
                            psj, _, pbj = self.bankx(j)
                            o_ = psj[:, bi * 128:(bi + 1) * 128]
                            a_ = khtok[32 * j:32 * j + 32, tt, :]
                            b_ = vtok[32 * j:32 * j + 32, tt, :]
                            kw = {'w': [pbj]} if bi == 0 else {'wp': [pbj]}
                            self.P.pe(lambda hh, o_=o_, a_=a_, b_=b_, j=j: hh.matmul(o_, lhsT=a_, rhs=b_, start=True, stop=True, tile_position=(32 * j, 0)),
                                      r=Bkhtok2 + [Bvtok], **kw)
                    for bi, tt in enumerate(tiles):
                        for j in corder:
                            c = tt * 4 + j
                            psj, _, pbj = self.bankx(j)
                            s_src, s_dst = Sm2[step % 2], Sm2[(step + 1) % 2]
                            Bs_src, Bs_dst = BSm2[step % 2], BSm2[(step + 1) % 2]
                            step += 1
                            self.stt(s_dst, s_src, ac[:, c:c + 1], psj[:, bi * 128:(bi + 1) * 128], ALU.mult, ALU.add, r=[Bs_src, pbj] + Bac2, w=[Bs_dst])
                            nxt = c + 1 if d == 0 else c
                            self.cp(Sall[:, nxt, :], s_dst, r=[Bs_dst], wp=[BSall], eng='act')
                mask = maskF if d == 0 else maskB
                mask_bc = bass.AP(mask.tensor, mask.offset, [list(mask.ap[0]), [0, 4], [1, 128]])
                def o1(tq):
                    sl = slice(tq * 512, (tq + 1) * 512)
                    psA, _, pbA = self.bank()
                    for i4 in range(4):
                        tsl = slice(tq * 512 + i4 * 128, tq * 512 + (i4 + 1) * 128)
                        kw = {'w': [pbA]} if i4 == 0 else {'wp': [pbA]}
                        self.P.pe(lambda hh, o_=psA[:, i4 * 128:(i4 + 1) * 128], a_=kh[:, tsl], b_=qt[:, tsl]: hh.matmul(o_, lhsT=a_, rhs=b_, start=True, stop=True),
                                  r=Bkh2 + Bqt2, **kw)
                    attm, Battm = attms[tq % 2]
                    self.tt(attm.rearrange("p (a b) -> p a b", a=4), psA[:, :].rearrange("p (a b) -> p a b", a=4), mask_bc, ALU.mult,
                            r=[pbA, Bk], w=[Battm])
                    return (tq, sl, attm, Battm)

                def o2(ctx):
                    tq, sl, attm, Battm = ctx
                    psO, _, pbO = self.bank()
                    for i4 in range(4):
                        tt = tq * 4 + i4
                        osl = slice(i4 * 128, (i4 + 1) * 128)
                        kw = {'w': [pbO]} if i4 == 0 else {'wp': [pbO]}
                        self.P.pe(lambda hh, o_=psO[:, osl], a_=vtok[:, tt, :], b_=attm[:, osl]: hh.matmul(o_, lhsT=a_, rhs=b_, start=True, stop=False),
                                  r=[Bvtok, Battm], **kw)
                        for j in range(4):
                            c = tt * 4 + j
                            slot = c if d == 0 else c + 1
                            self.P.pe(lambda hh, o_=psO[:, i4 * 128 + j * 32:i4 * 128 + (j + 1) * 32], a_=Sall[:, slot, :], b_=qt[:, c * 32:(c + 1) * 32], e_=(j == 3):
                                      hh.matmul(o_, lhsT=a_, rhs=b_, start=False, stop=e_), r=[BSall] + Bqt2, wp=[pbO])
                    if d == 0:
                        kw = {'w': [BO]} if tq == 0 else {'wp': [BO]}
                        self.cp(O[:, sl], psO[:, :], r=[pbO], eng='act', **kw)
                    else:
                        self.tt(O[:, sl], O[:, sl], psO[:, :], ALU.add, r=[BO, pbO], wp=[BO])
                prev_o = None
                for tq in range(NTQ):
                    ctx_o = o1(tq)
                    if prev_o is not None:
                        o2(prev_o)
                    prev_o = ctx_o
                o2(prev_o)
            self.actf(G, O, AF.Square, r=[BO, BG] + BG2, w=[BG])
            for tq in range(NTQ):
                sl = slice(tq * 512, (tq + 1) * 512)
                ps, _, pb = self.bank()
                self.mm(ps[:, :], self.ones_f[:], G[:, sl], True, True, [BG, self.B_const], pb)
                kw = {'w': [BE]} if tq == 0 else {'wp': [BE]}
                kw_r = BE2
                self.actf(E[:, sl], ps[:, :], AF.Ln, r=[pb] + BE2, scale=1.0 / 128, bias=EPS, **kw)
            self.actf(E, E, AF.Exp, r=[BE], w=[BE], scale=-0.5)
            self.stt(O, O, pr[:, l, h, R_GN:R_GN + 1], E, ALU.mult, ALU.mult, r=[BO, BE, self.B_prm], w=[BO])
            w, Bw = self.wload(self.w_in[l][:, D_G + h * 128:D_G + (h + 1) * 128], 8)
            for tq in range(NTQ):
                sl = slice(tq * 512, (tq + 1) * 512)
                ps, _, pb = self.bank()
                self.mm_group(ps[:, :], [w[:, k, :] for k in range(8)], [self.hT[:, k, sl] for k in range(8)], [Bw, self.B_hT], pb)
                kw = {'w': [BG]} if tq == 0 else {'wp': [BG]}
                self.actf(G[:, sl], ps[:, :], AF.Silu, r=[pb], **kw)
            self.tt(self.brT[:, h, :], O, G, ALU.mult, r=[BO, BG], w=[self.B_brT[h]])
            if h == 3:
                self.dump("D_O", O, [BO])
                self.dump("D_qt", qt, [Bqt], BF16)
                self.dump("D_kh", kh, [Bkh], BF16)
                self.dump("D_Bc", Bc, [BBc])
                self.dump("D_Sall", Sall, [BSall], BF16)
        self.dump("D_brT", self.brT[:], self.B_brT, BF16)
        self.bank_set = list(range(8))

    def branch_A(self, l):
        self.phase()
        self.setup_wbufs(3, 1024)
        COS = self.carve([128, T], F32)
        SIN = self.carve([128, T], F32)
        CT = self.carve([128, 6, 128], F32)
        Bk = Buf()
        self.dma(COS, self.c_cos[:, :], w=[Bk])
        self.dma(SIN, self.c_sin[:, :], wp=[Bk])
        self.dma(CT, self.c_ret.rearrange("a p c -> p a c"), wp=[Bk])
        DF, UF, DB, UB, TQ, TK = [CT[:, i, :] for i in range(6)]
        lgt = self.carve([128, 8], F32)
        lg = self.carve([128, 8], F32)
        lgs = self.carve([128, 4], F32)
        cd = self.carve([128, 4], F32)
        Blg, Blgs = Buf(), Buf()
        self.dma(lgt, self.ret_logit[l].rearrange("d h -> (d h)").rearrange("(a c) -> a c", a=1).broadcast_to([128, 8]), w=[Blg])
        self.softplus_neg(lgt, lg, [128, 8], Blg, Blg)
        self.ts(lg, lg, -1.0, None, ALU.mult, r=[Blg], w=[Blg])
        self.cp(lgs[0:64, :], lg[0:64, 0:4], r=[Blg], w=[Blgs])
        self.cp(lgs[64:128, :], lg[64:128, 4:8], r=[Blg], wp=[Blgs])
        self.actf(cd, lgs, AF.Exp, r=[Blgs], wp=[Blgs], scale=128.0)
        MT = self.carve([128, 128], F32)
        WQ = self.carve([128, 128], F32)
        KW = self.carve([128, 128], F32)
        t1 = self.carve([128, 128], F32)
        BMT, BWQ, BKW, Bt1 = Buf(), Buf(), Buf(), Buf()
        qr, qh, kr, kst = [self.carve([128, T], BF16) for _ in range(4)]
        khtok = self.carve([128, NTT, 128], BF16)
        vtok = self.carve([128, NTT, 128], BF16)
        X = self.carve([128, NTT, 128], F32)
        prev = self.carve([128, NTT, 128], BF16)
        O = self.carve([128, T], F32)
        tmps = [(self.carve([128, 512], F32), Buf()) for _ in range(4)]
        attms = [(self.carve([128, 512], BF16), Buf()) for _ in range(2)]
        Bqr, Bqh, Bkr, Bkst, Bkhtok, Bvtok, BX, Bprev, BO = [Buf() for _ in range(9)]
        wd = self.carve([128, 8, 128], BF16)
        wsw = self.carve([128, 8, 128], BF16)
        Bw = Buf()
        c3 = lambda a: a.rearrange("p (n c) -> p n c", c=128)
        bc16 = lambda a: bass.AP(a.tensor, a.offset, [list(a.ap[0]), [0, NTT], [1, 128]])
        bc4 = lambda a: bass.AP(a.tensor, a.offset, [list(a.ap[0]), [0, 4], [1, 128]])
        ti = 0
        for h in range(4):
            lf = lg[:, h:h + 1]
            lb_ = lg[:, 4 + h:5 + h]
            self.actf(MT, DF, AF.Exp, r=[Bk, Blg], w=[BMT], scale=lf)
            self.tt(MT, MT, UF, ALU.mult, r=[BMT, Bk], w=[BMT])
            self.actf(t1, DB, AF.Exp, r=[Bk, Blg], w=[Bt1], scale=lb_)
            self.tt(t1, t1, UB, ALU.mult, r=[Bt1, Bk], w=[Bt1])
            self.tt(MT, MT, t1, ALU.add, r=[BMT, Bt1], w=[BMT])
            self.ts(MT, MT, 0.125, None, ALU.mult, r=[BMT], w=[BMT])
            self.actf(WQ, TQ, AF.Exp, r=[Bk, Blgs], w=[BWQ], scale=lgs[:, h:h + 1])
            self.actf(KW, TK, AF.Exp, r=[Bk, Blgs], w=[BKW], scale=lgs[:, h:h + 1])
            self.ts(KW, KW, 0.125, None, ALU.mult, r=[BKW], w=[BKW])
            for (c0, dst, Bd) in ((A_Q + h * 64, qr, Bqr), (A_K + h * 64, kr, Bkr)):
                w0, Bw0 = self.wload(self.w_in[l][:, c0:c0 + 64], 8, ncols=64)
                for a in range(2):
                    kw = {'w': [Bw]} if a == 0 else {'wp': [Bw]}
                    self.cp(wd[:, :, a * 64:(a + 1) * 64], w0, r=[Bw0], eng='pool', **kw)
                    for j2 in range(2):
                        self.cp(wsw[:, :, a * 64 + j2 * 32:a * 64 + (j2 + 1) * 32], w0[:, :, (1 - j2) * 32:(2 - j2) * 32], r=[Bw0], wp=[Bw], eng='pool')
                for tq in range(NTQ):
                    sl = slice(tq * 512, (tq + 1) * 512)
                    psn, _, pbn = self.bank()
                    pss, _, pbs = self.bank()
                    lh_n = [wd[:, k, :] for k in range(8)]
                    lh_s = [wsw[:, k, :] for k in range(8)]
                    rh = [self.hT[:, k, sl] for k in range(8)]
                    self.mm_group(psn[:, :], lh_n, rh, [Bw, self.B_hT], pbn)
                    self.mm_group(pss[:, :], lh_s, rh, [Bw, self.B_hT], pbs)
                    ta, Bta = tmps[ti % 4]
                    tb, Btb = tmps[(ti + 1) % 4]
                    ti += 2
                    self.tt(ta, psn[:, :], COS[:, sl], ALU.mult, r=[pbn, Bk], w=[Bta])
                    self.tt(tb, pss[:, :], SIN[:, sl], ALU.mult, r=[pbs, Bk], w=[Btb])
                    kw = {'w': [Bd]} if tq == 0 else {'wp': [Bd]}
                    self.tt(dst[:, sl], ta, tb, ALU.add, r=[Bta, Btb], eng='pool', **kw)
            self.tt(c3(qh), c3(qr), bc16(WQ), ALU.mult, r=[Bqr, BWQ], w=[Bqh])
            self.tt(c3(kst), c3(kr), bc16(KW), ALU.mult, r=[Bkr, BKW], w=[Bkst])
            for g8 in range(2):
                ps, psb, pb = self.bank()
                for i8 in range(8):
                    tt = g8 * 8 + i8
                    self.tr(psb[:, i8 * 128:(i8 + 1) * 128], kst[:, tt * 128:(tt + 1) * 128], self.ident_b[:], [Bkst, self.B_const], pb, first=(i8 == 0))
                kw = {'w': [Bkhtok]} if g8 == 0 else {'wp': [Bkhtok]}
                self.cp(khtok[:, g8 * 8:(g8 + 1) * 8, :], psb.rearrange("p (a b) -> p a b", a=8), r=[pb], eng='act', **kw)
            w, Bw = self.wload(self.w_in[l][:, A_V + h * 128:A_V + (h + 1) * 128], 8)
            for g4 in range(4):
                ps, _, pb = self.bank()
                for i4 in range(4):
                    tt = g4 * 4 + i4
                    for k in range(8):
                        kw = {'w': [pb]} if (i4 == 0 and k == 0) else {'wp': [pb]}
                        self.P.pe(lambda hh, o_=ps[:, i4 * 128:(i4 + 1) * 128], a_=self.hT[:, k, tt * 128:(tt + 1) * 128], b_=w[:, k, :], s_=(k == 0), e_=(k == 7):
                                  hh.matmul(o_, lhsT=a_, rhs=b_, start=s_, stop=e_), r=[Bw, self.B_hT], **kw)
                kw = {'w': [Bvtok]} if g4 == 0 else {'wp': [Bvtok]}
                self.cp(vtok[:, g4 * 4:(g4 + 1) * 4, :], ps[:, :].rearrange("p (a b) -> p a b", a=4), r=[pb], eng='act', **kw)
            for g4 in range(4):
                ps, _, pb = self.bank()
                for i4 in range(4):
                    n = g4 * 4 + i4
                    kw = {'w': [pb]} if i4 == 0 else {'wp': [pb]}
                    self.P.pe(lambda hh, o_=ps[:, i4 * 128:(i4 + 1) * 128], a_=khtok[:, n, :], b_=vtok[:, n, :]: hh.matmul(o_, lhsT=a_, rhs=b_, start=True, stop=True),
                              r=[Bkhtok, Bvtok], **kw)
                kw = {'w': [BX]} if g4 == 0 else {'wp': [BX]}
                self.cp(X[:, g4 * 4:(g4 + 1) * 4, :], ps[:, :].rearrange("p (a b) -> p a b", a=4), r=[pb], eng='act', **kw)
            for n in range(1, NTT):
                self.stt(X[0:64, n, :], X[0:64, n - 1, :], cd[0:64, h:h + 1], X[0:64, n, :], ALU.mult, ALU.add, r=[BX, Blgs], w=[BX])
            for n in range(NTT - 2, -1, -1):
                self.stt(X[64:128, n, :], X[64:128, n + 1, :], cd[64:128, h:h + 1], X[64:128, n, :], ALU.mult, ALU.add, r=[BX, Blgs], w=[BX])
            self.mset(prev[0:64, 0, :], 0.0, w=[Bprev], eng='pool')
            self.mset(prev[64:128, NTT - 1, :], 0.0, wp=[Bprev], eng='pool')
            self.cp(prev[0:64, 1:NTT, :], X[0:64, 0:NTT - 1, :], r=[BX], wp=[Bprev], eng='pool')
            self.cp(prev[64:128, 0:NTT - 1, :], X[64:128, 1:NTT, :], r=[BX], wp=[Bprev], eng='pool')
            MT_bc = bc4(MT)
            def a1(tq):
                sl = slice(tq * 512, (tq + 1) * 512)
                psS, _, pbS = self.bank()
                for i4 in range(4):
                    tsl = slice(tq * 512 + i4 * 128, tq * 512 + (i4 + 1) * 128)
                    kw = {'w': [pbS]} if i4 == 0 else {'wp': [pbS]}
                    self.P.pe(lambda hh, o_=psS[:, i4 * 128:(i4 + 1) * 128], a_=kr[0:64, tsl], b_=qr[0:64, tsl]: hh.matmul(o_, lhsT=a_, rhs=b_, start=True, stop=True),
                              r=[Bkr, Bqr], **kw)
                attm, Battm = attms[tq % 2]
                self.tt(attm.rearrange("p (a b) -> p a b", a=4), psS[:, :].rearrange("p (a b) -> p a b", a=4), MT_bc, ALU.mult, r=[pbS, BMT], w=[Battm])
                return (tq, sl, attm, Battm)

            def a2(ctx):
                tq, sl, attm, Battm = ctx
                psO, _, pbO = self.bank()
                for i4 in range(4):
                    n = tq * 4 + i4
                    osl = slice(i4 * 128, (i4 + 1) * 128)
                    tsl = slice(n * 128, (n + 1) * 128)
                    kw = {'w': [pbO]} if i4 == 0 else {'wp': [pbO]}
                    self.P.pe(lambda hh, o_=psO[:, osl], a_=vtok[:, n, :], b_=attm[:, osl]: hh.matmul(o_, lhsT=a_, rhs=b_, start=True, stop=False),
                              r=[Bvtok, Battm], **kw)
                    self.P.pe(lambda hh, o_=psO[:, osl], a_=prev[:, n, :], b_=qh[:, tsl]: hh.matmul(o_, lhsT=a_, rhs=b_, start=False, stop=True),
                              r=[Bprev, Bqh], wp=[pbO])
                kw = {'w': [BO]} if tq == 0 else {'wp': [BO]}
                self.cp(O[:, sl], psO[:, :], r=[pbO], eng='act', **kw)
            prev_a = None
            for tq in range(NTQ):
                ctx_a = a1(tq)
                if prev_a is not None:
                    a2(prev_a)
                prev_a = ctx_a
            a2(prev_a)
            SQ = X.rearrange("p a b -> p (a b)")
            self.actf(SQ, O, AF.Square, r=[BO, BX, Bprev], w=[BX])
            for tq in range(NTQ):
                sl = slice(tq * 512, (tq + 1) * 512)
                ps, _, pb = self.bank()
                self.mm(ps[:, :], self.ones_f[:], SQ[:, sl], True, True, [BX, self.B_const], pb)
                rt, Brt = tmps[tq]
                self.actf(rt, ps[:, :], AF.Ln, r=[pb], w=[Brt], scale=1.0 / 128, bias=EPS)
                self.actf(rt, rt, AF.Exp, r=[Brt], w=[Brt], scale=-0.5)
                self.tt(O[:, sl], O[:, sl], rt, ALU.mult, r=[BO, Brt], wp=[BO])
            w, Bw = self.wload(self.w_in[l][:, A_G + h * 128:A_G + (h + 1) * 128], 8)
            for tq in range(NTQ):
                sl = slice(tq * 512, (tq + 1) * 512)
                ps, _, pb = self.bank()
                self.mm_group(ps[:, :], [w[:, k, :] for k in range(8)], [self.hT[:, k, sl] for k in range(8)], [Bw, self.B_hT], pb)
                gt, Bgt = tmps[tq]
                self.actf(gt, ps[:, :], AF.Silu, r=[pb], w=[Bgt])
                kw = {'w': [self.B_brT[h]]} if tq == 0 else {'wp': [self.B_brT[h]]}
                self.tt(self.brT[:, h, sl], O[:, sl], gt, ALU.mult, r=[BO, Bgt], **kw)
            if h == 3:
                self.dump("A_O", O, [BO])
                self.dump("A_qr", qr, [Bqr], BF16)
                self.dump("A_kr", kr, [Bkr], BF16)
                self.dump("A_MT", MT, [BMT])
                self.dump("A_X", X, [BX])
                self.dump("A_lg", lg, [Blg])
        self.dump("A_brT", self.brT[:], self.B_brT, BF16)

    def branch_B(self, l, dst=None, Bdst=None):
        self.phase()
        if dst is None:
            odst, Bdst = self.brT, self.B_brT
        else:
            odst = self.carve([128, 4, T], BF16)
        self.bank_set = [4, 5, 6, 7]
        self.setup_wbufs(3, 1024)
        qT, kT = [self.carve([128, T], BF16) for _ in range(2)]
        Va = self.carve([128, 16, 128], BF16)
        Vb = self.carve([128, 16, 128], BF16)
        G = self.carve([128, T], F32)
        EB = self.carve([128, 2, 14, 64], F32)
        exs = [(self.carve([128, 512], F32), Buf()) for _ in range(2)]
        Ps = [(self.carve([128, 512], BF16), Buf()) for _ in range(2)]
        rds = [(self.carve([128, 512], F32), Buf()) for _ in range(2)]
        BqT, BkT, BVa, BVb, BG, BEB = [Buf() for _ in range(6)]
        it = 0
        for fc in range(4 if BST >= 1 else 0):
            self.dma(EB, self.ebs[l, :, 2 * fc:2 * fc + 2, :, :], r=[self.B_ebs], w=[BEB])
            for (c0, dst, Bd, fn) in ((B_Q, qT, BqT, None), (B_K, kT, BkT, None), (B_G, G, BG, AF.Silu)):
                w, Bw = self.wload(self.w_in[l][:, c0 + fc * 128:c0 + (fc + 1) * 128], 8)
                for tq in range(NTQ):
                    sl = slice(tq * 512, (tq + 1) * 512)
                    ps, _, pb = self.bank()
                    self.mm_group(ps[:, :], [w[:, k, :] for k in range(8)], [self.hT[:, k, sl] for k in range(8)], [Bw, self.B_hT], pb)
                    kw = {'w': [Bd]} if tq == 0 else {'wp': [Bd]}
                    if fn is None:
                        self.cp(dst[:, sl], ps[:, :], r=[pb], eng='act', **kw)
                    else:
                        self.actf(dst[:, sl], ps[:, :], fn, r=[pb], **kw)
            w, Bw = self.wload(self.w_in[l][:, B_V + fc * 128:B_V + (fc + 1) * 128], 8)
            for (Vt, BV, off, ntile) in ((Va, BVa, 0, 16), (Vb, BVb, 64, 15)):
                for g4 in range(4):
                    n4 = min(4, ntile - g4 * 4)
                    ps, _, pb = self.bank()
                    for i4 in range(n4):
                        t0 = off + (g4 * 4 + i4) * 128
                        for k in range(8):
                            kw = {'w': [pb]} if (i4 == 0 and k == 0) else {'wp': [pb]}
                            self.P.pe(lambda hh, o_=ps[:, i4 * 128:(i4 + 1) * 128], a_=self.hT[:, k, t0:t0 + 128], b_=w[:, k, :], s_=(k == 0), e_=(k == 7):
                                      hh.matmul(o_, lhsT=a_, rhs=b_, start=s_, stop=e_), r=[Bw, self.B_hT], **kw)
                    kw = {'w': [BV]} if g4 == 0 else {'wp': [BV]}
                    self.cp(Vt[:, g4 * 4:g4 * 4 + n4, :], ps[:, 0:n4 * 128].rearrange("p (a b) -> p a b", a=n4), r=[pb], eng='act', **kw)
            def s1(r):
                nonlocal it
                rs = min(max(r - 4, 0), 24)
                o = r - rs
                ex, Bex = exs[it % 2]
                Pt, BP = Ps[it % 2]
                it += 1
                for hh in range(2):
                    hb = hh * 64
                    psS, _, pbS = self.bank()
                    for i in range(4):
                        k0 = (rs + 2 * i) * 64
                        kw = {'w': [pbS]} if i == 0 else {'wp': [pbS]}
                        self.P.pe(lambda h_, o_=psS[:, i * 64:(i + 1) * 64], a_=kT[hb:hb + 64, k0:k0 + 128], b_=qT[hb:hb + 64, r * 64:(r + 1) * 64]:
                                  h_.matmul(o_, lhsT=a_, rhs=b_, start=True, stop=True), r=[BkT, BqT], **kw)
                    kw = {'w': [Bex]} if hh == 0 else {'wp': [Bex]}
                    self.actf(ex[:, hh * 256:(hh + 1) * 256], psS[:, 0:256], AF.Exp, r=[pbS], scale=0.125, **kw)
                eb = EB[:, :, 7 - o:7 - o + 7:2, :]
                self.tt(Pt.rearrange("p (h i q) -> p h i q", h=2, i=4), ex.rearrange("p (h i q) -> p h i q", h=2, i=4), eb, ALU.mult,
                        r=[Bex, BEB], w=[BP])
                return (r, rs, Pt, BP)

            def s2(ctx):
                r, rs, Pt, BP = ctx
                rg, r4 = r // 4, r % 4
                a = rs % 2
                Vt, BV = (Va, BVa) if a == 0 else (Vb, BVb)
                psN, _, pbN = self.bankx(2 * (rg % 2))
                psD, _, pbD = self.bankx(2 * (rg % 2) + 1)
                for hh in range(2):
                    oc = hh * 256 + r4 * 64
                    for (psX, pbX, isnum) in ((psN, pbN, True), (psD, pbD, False)):
                        for i in range(4):
                            ti = (rs + 2 * i - a) // 2
                            lh = Vt[:, ti, :] if isnum else self.ones_b[:, :]
                            kw = {'w': [pbX]} if (r4 == 0 and hh == 0 and i == 0) else {'wp': [pbX]}
                            self.P.pe(lambda h_, o_=psX[:, oc:oc + 64], a_=lh, b_=Pt[:, (hh * 4 + i) * 64:(hh * 4 + i + 1) * 64], s_=(i == 0), e_=(i == 3):
                                      h_.matmul(o_, lhsT=a_, rhs=b_, start=s_, stop=e_), r=[BV, BP, self.B_const], **kw)
                if r4 != 3:
                    return
                rd, Brd = rds[rg % 2]
                sl = slice(rg * 256, (rg + 1) * 256)
                for hh in range(2):
                    pr_ = slice(hh * 64, (hh + 1) * 64)
                    cs_ = slice(hh * 256, (hh + 1) * 256)
                    kw = {'w': [Brd]} if hh == 0 else {'wp': [Brd]}
                    self.actf(rd[pr_, 0:256], psD[pr_, cs_], AF.Ln, r=[pbD], **kw)
                    self.actf(rd[pr_, 0:256], rd[pr_, 0:256], AF.Exp, r=[Brd], wp=[Brd], scale=-1.0)
                    self.tt(rd[pr_, 0:256], rd[pr_, 0:256], psN[pr_, cs_], ALU.mult, r=[Brd, pbN], wp=[Brd])
                kw = {'w': [Bdst[fc]]} if rg == 0 else {'wp': [Bdst[fc]]}
                self.tt(odst[:, fc, sl], rd[:, 0:256], G[:, sl], ALU.mult, r=[Brd, BG], **kw)

            prev = None
            for r in range(32):
                ctx = s1(r)
                if prev is not None:
                    s2(prev)
                prev = ctx
            s2(prev)
        self.dump("B_brT", odst, Bdst, BF16)
        self.bank_set = list(range(8))

    def build(self):
        self.declare()
        self.alloc()
        self.phase0()
        if self.mask[1]:
            self.build_nat_tables()
        outs = []
        self.B_brT2 = [Buf() for _ in range(4)]
        for s in range(self.nseq):
            for l in range(self.nlayers):
                self.phase1(s, l)
                if all(self.mask):
                    self.branch_A(l)
                    self.phase_merge(l, 0, True)
                    self.branch_D(l)
                    self.branch_B(l, dst='arena', Bdst=self.B_brT2)
                    self.phase()
                    brT2 = self.carve([128, 4, T], BF16)
                    self.brT_free = False
                    mres = self.merge_res(3, 1)
                    gm = self.gen_merge(l, 3, False, res=mres)
                    gm2 = self.gen_merge(l, 1, False, src=brT2, Bsrc=self.B_brT2, res=mres)
                    gc = self.gen_C(l)
                    nmd = 0
                    cstep = 0
                    c_done = False
                    md_done = False
                    while not md_done:
                        if not c_done and cstep < 7:
                            try:
                                next(gc)
                                cstep += 1
                            except StopIteration:
                                c_done = True
                        for _ in range(5):
                            try:
                                next(gm)
                            except StopIteration:
                                md_done = True
                                break
                    self.brT_free = True
                    mb_done = False
                    while not (c_done and mb_done):
                        if not c_done:
                            try:
                                next(gc)
                            except StopIteration:
                                c_done = True
                        if not mb_done:
                            try:
                                next(gm2)
                            except StopIteration:
                                mb_done = True
                    self.phase_merge(l, 2, False)
                    first = False
                else:
                    first = True
                    for j, fn in enumerate((self.branch_A, self.branch_B, self.branch_C, self.branch_D)):
                        if not self.mask[j] or fn is None:
                            continue
                        fn(l)
                        self.phase_merge(l, j, first)
                        first = False
                o = self.phase3(s, l, not first)
                if l == self.nlayers - 1:
                    outs += o
        self.P.emit(final_wait=outs + self.dbg_outs)
        self.st.close()


_CACHE = {}


def get_nc(nseq, nlayers, mask):
    key = (nseq, nlayers, tuple(mask))
    if key not in _CACHE:
        nc = bass.Bass("TRN2", target_bir_lowering=False)
        kb = KB(nc, nseq, nlayers, mask)
        kb.build()
        _CACHE[key] = nc
    return _CACHE[key]


WNAMES = ['norm_mix', 'w_in', 'ret_decay_logit', 'nat_rpb', 'lru_conv_w', 'lru_conv_b', 'lru_wa', 'lru_ba', 'lru_wx',
          'lru_bx', 'lru_lambda', 'hgrn_lb_logits', 'hgrn_norm', 'w_branch', 'w_merge', 'w_out', 'ple_norm',
          'w_ple_gate', 'w_ple_proj', 'final_norm']


def host_consts():
    s = np.arange(128)[:, None]
    t = np.arange(128)[None, :]
    same = (s // 32) == (t // 32)
    half = 32
    inv = (np.float32(10000.0) ** (-(np.arange(half, dtype=np.float32) / np.float32(half)))).astype(np.float32)
    ang = (np.arange(T, dtype=np.float32)[None, :] * inv[:, None]).astype(np.float32)
    cs = np.cos(ang.astype(np.float64)).astype(np.float32)
    sn = np.sin(ang.astype(np.float64)).astype(np.float32)
    cos64 = np.concatenate([cs, cs], 0)
    sin64 = np.concatenate([-sn, sn], 0)
    c_cos = np.concatenate([cos64, cos64], 0)
    c_sin = np.concatenate([sin64, sin64], 0)
    tau = np.arange(128, dtype=np.float32)
    DF = np.maximum(t - s, 0).astype(np.float32)
    UF = (t >= s).astype(np.float32)
    DB = np.maximum(s - t, 0).astype(np.float32)
    UB = (s >= t).astype(np.float32)
    TQ = np.concatenate([np.tile(tau + 1, (64, 1)), np.tile(128 - tau, (64, 1))], 0)
    TK = np.concatenate([np.tile(127 - tau, (64, 1)), np.tile(tau, (64, 1))], 0)
    c_ret = np.stack([DF, UF, DB, UB, TQ, TK]).astype(np.float32)
    return {'c_ident': np.eye(128, dtype=np.float32), 'c_cos': c_cos, 'c_sin': c_sin, 'c_ret': c_ret,
            'c_maskF': (same & (s <= t)).astype(np.float32),
            'c_maskB': (same & (s >= t)).astype(np.float32)}


def run_seqs(xs, ps, weights, nlayers=NL, mask=(1, 1, 1, 1), ncores=8):
    n = xs.shape[0]
    per = n // ncores
    nc = get_nc(per, nlayers, mask)
    consts = host_consts()
    in_maps = []
    for c in range(ncores):
        m = {'x': np.ascontiguousarray(xs[c * per:(c + 1) * per]),
             'p': np.ascontiguousarray(ps[:, c * per:(c + 1) * per])}
        for k in WNAMES:
            m[k] = weights[k]
        m.update(consts)
        in_maps.append(m)
    res = run_bass_kernel_spmd(nc, in_maps, core_ids=list(range(ncores)))
    return np.concatenate([r['y'] for r in res.results], axis=0)


def kernel(**inputs):
    inputs = {k: np.asarray(v) for k, v in inputs.items()}
    xs = np.concatenate([inputs['x_prompt'], inputs['x_sample']], axis=0)
    ps = np.concatenate([inputs['p_prompt'], inputs['p_sample']], axis=1)
    weights = {k: np.ascontiguousarray(inputs[k], dtype=np.float32) for k in WNAMES}
    y = run_seqs(xs.astype(np.float32, copy=False), ps.astype(np.float32, copy=False), weights)
    nb = inputs['x_prompt'].shape[0]
    return (np.ascontiguousarray(y[:nb]), np.ascontiguousarray(y[nb:]))
```

```python
import numpy as np
from contextlib import ExitStack
import concourse.bass as bass
import concourse.mybir as mybir
from concourse.bass_utils import run_bass_kernel_spmd
from concourse.alu_op_type import AluOpType as ALU

AF = mybir.ActivationFunctionType
F32 = mybir.dt.float32
BF16 = mybir.dt.bfloat16
AX = mybir.AxisListType

T = 2048
DM = 1024
NL = 2
PLE = 256
W_IN = 7168
EPS = 1e-6
NTT = T // 128
NTQ = T // 512

import os
BST = int(os.environ.get('BST', '3'))
ENGS = ['pe', 'act', 'dve', 'pool', 'sp']
SEM_LIM = 20000
N_EPOCH = 6
N_DMA_SEMS = 32


class Buf:
    __slots__ = ('name', 'writers', 'readers', 'round_deps')

    def __init__(self, name=''):
        self.name = name
        self.writers = []
        self.readers = []
        self.round_deps = []


class Op:
    __slots__ = ('eng', 'fn', 'deps', 'sig', 'idx', 'dma', 'sem', 'val', 'prev_dma', 'gidx')

    def __init__(self, eng, fn, dma):
        self.eng = eng
        self.fn = fn
        self.deps = set()
        self.sig = False
        self.dma = dma
        self.sem = None
        self.val = 0
        self.prev_dma = None


class Prog:
    def __init__(self, nc):
        self.nc = nc
        self.ops = {e: [] for e in ENGS}
        self.all = []
        self.dma_last = [None] * N_DMA_SEMS
        self.dma_cnt = [0] * N_DMA_SEMS
        self.dma_rr = 0
        self.dma_rr_sw = 0
        self.bar_deps = {e: set() for e in ENGS}

    def add(self, eng, fn, r=(), w=(), wp=(), dma=False):
        o = Op(eng, fn, dma)
        o.idx = len(self.ops[eng])
        o.gidx = len(self.all)
        deps = set(self.bar_deps[eng])
        self.bar_deps[eng] = set()
        for b in r:
            deps.update(b.writers)
        for b in w:
            d = set(b.writers) | set(b.readers)
            deps.update(d)
            b.round_deps = list(d)
        for b in wp:
            deps.update(b.round_deps)
            deps.update(b.readers)
            if b.writers:
                deps.add(b.writers[0])
        for b in r:
            b.readers.append(o)
        for b in w:
            b.writers = [o]
            b.readers = []
        for b in wp:
            b.writers.append(o)
        o.deps = deps
        if dma:
            half = N_DMA_SEMS // 2
            if eng == 'pool':
                s = half + self.dma_rr_sw
                self.dma_rr_sw = (self.dma_rr_sw + 1) % half
            else:
                s = self.dma_rr
                self.dma_rr = (self.dma_rr + 1) % half
            o.sem = ('dma', s)
            self.dma_cnt[s] += 16
            o.val = self.dma_cnt[s]
            o.prev_dma = self.dma_last[s]
            self.dma_last[s] = o
        self.ops[eng].append(o)
        self.all.append(o)
        return o

    def barrier(self):
        last = set()
        for e in ENGS:
            lst = [o for o in self.ops[e] if not o.dma]
            if lst:
                last.add(lst[-1])
        for s in range(N_DMA_SEMS):
            if self.dma_last[s] is not None:
                last.add(self.dma_last[s])
        for e in ENGS:
            self.bar_deps[e] = set(last)

    def pe(self, fn, **k):
        return self.add('pe', fn, **k)

    def act(self, fn, **k):
        return self.add('act', fn, **k)

    def dve(self, fn, **k):
        return self.add('dve', fn, **k)

    def pool(self, fn, **k):
        return self.add('pool', fn, **k)

    def dma(self, fn, eng='sp', **k):
        return self.add(eng, fn, dma=True, **k)

    def emit(self, final_wait=()):
        nc = self.nc
        for o in self.all:
            nd = set()
            for d in o.deps:
                if d.dma:
                    nd.add(d)
                    continue
                if d.eng == o.eng and o.eng == 'pe' and not o.dma:
                    continue
                nd.add(d)
            o.deps = nd
            for d in nd:
                d.sig = True
        with ExitStack() as st:
            csem = {}
            for e in ['pe', 'act', 'dve', 'pool']:
                csem[e] = [st.enter_context(nc.semaphore(f"c_{e}{i}")) for i in range(N_EPOCH)]
            dsem = [st.enter_context(nc.semaphore(f"d{i}")) for i in range(N_DMA_SEMS)]
            for e in ['pe', 'act', 'dve', 'pool']:
                k = 0
                for o in self.ops[e]:
                    if o.dma:
                        continue
                    if o.sig:
                        o.sem = ('c', e, k // SEM_LIM)
                        o.val = k % SEM_LIM + 1
                        k += 1
                assert k < SEM_LIM * N_EPOCH, (e, k)

            def semh(s):
                return dsem[s[1]] if s[0] == 'dma' else csem[s[1]][s[2]]

            block = st.enter_context(nc.Block())
            fw = list(final_wait)

            def run(eng_name, h):
                waited = {}
                for o in self.ops[eng_name]:
                    need = {}
                    deps = list(o.deps)
                    if o.dma and o.prev_dma is not None:
                        deps.append(o.prev_dma)
                    for d in deps:
                        if d.sem is None:
                            continue
                        if need.get(d.sem, 0) < d.val:
                            need[d.sem] = d.val
                    for s, v in need.items():
                        if waited.get(s, 0) < v:
                            h.wait_ge(semh(s), v)
                            waited[s] = v
                    ins = o.fn(h)
                    if o.dma:
                        ins.then_inc(semh(o.sem), 16)
                    elif o.sig:
                        ins.then_inc(semh(o.sem), 1)
                if eng_name == 'sp':
                    for o in fw:
                        if waited.get(o.sem, 0) < o.val:
                            h.wait_ge(semh(o.sem), o.val)
                            waited[o.sem] = o.val

            @block.sync
            def _(h):
                run('sp', h)

            @block.tensor
            def _(h):
                run('pe', h)

            @block.scalar
            def _(h):
                run('act', h)

            @block.vector
            def _(h):
                run('dve', h)

            @block.gpsimd
            def _(h):
                run('pool', h)


A_Q, A_K, A_V, A_G = 0, 256, 512, 1024
B_Q, B_K, B_V, B_G = 1536, 2048, 2560, 3072
C_X, C_G = 3584, 4096
D_Q, D_FF, D_FB, D_I, D_G = 4608, 5120, 5632, 6144, 6656

R_CW, R_CB, R_BA, R_BX, R_LAM, R_LB, R_GN = 0, 4, 5, 7, 9, 11, 15
NR = 16
Q_C1, Q_C2, Q_LB, Q_OML = 16, 18, 20, 22
NQ = 24


class KB:
    def __init__(self, nc, nseq, nlayers, mask):
        self.nc = nc
        self.P = Prog(nc)
        self.nseq = nseq
        self.nlayers = nlayers
        self.mask = mask
        self.st = ExitStack()
        self.bank_rr = 0
        self.debug = False
        self.bank_set = list(range(8))
        self.dbg_outs = []

    def sb(self, name, shape, dt):
        return self.st.enter_context(self.nc.sbuf_tensor(name, shape, dt))

    def carve(self, shape, dt):
        n = int(np.prod(shape[1:]))
        nbytes = n * (4 if dt == F32 else 2)
        nbytes = (nbytes + 63) // 64 * 64
        w0 = self.aoff // 4
        assert self.aoff + nbytes <= self.arena_bytes, (self.aoff, nbytes, self.arena_bytes)
        self.aoff += nbytes
        ap = self.arena[0:shape[0], w0:w0 + nbytes // 4]
        if dt != F32:
            ap = ap.bitcast(dt)
        ap = ap[:, 0:n]
        if len(shape) == 3:
            ap = ap.rearrange("p (a b) -> p a b", a=shape[1])
        elif len(shape) == 4:
            ap = ap.rearrange("p (a b c) -> p a b c", a=shape[1], b=shape[2])
        return ap

    def phase(self):
        self.P.barrier()
        self.aoff = 0

    def dump(self, name, ap, bufs, dt=F32):
        if not getattr(self, 'debug', False):
            return
        shape = list(ap.shape)
        d = self.nc.dram_tensor("dbg_" + name, shape, dt, kind="ExternalOutput").ap()
        self.dbg_outs.append(self.dma(d, ap, r=bufs))

    def bank(self):
        bs = self.bank_set
        self.bank_rr = (self.bank_rr + 1) % len(bs)
        i = bs[self.bank_rr]
        return self.psum[i], self.psum_bf[i], self.pbuf[i]

    def bankx(self, i):
        return self.psum[i], self.psum_bf[i], self.pbuf[i]

    def declare(self):
        nc = self.nc
        ns = self.nseq

        def din(name, shape):
            return nc.dram_tensor(name, list(shape), F32, kind="ExternalInput").ap()

        self.x = din("x", (ns, T, DM))
        self.p = din("p", (NL, ns, T, PLE))
        self.norm_mix = din("norm_mix", (NL, DM))
        self.w_in = din("w_in", (NL, DM, W_IN))
        self.ret_logit = din("ret_decay_logit", (NL, 2, 4))
        self.rpb = din("nat_rpb", (NL, 8, 15, 31))
        self.conv_w = din("lru_conv_w", (NL, 4, 512))
        self.conv_b = din("lru_conv_b", (NL, 512))
        self.wa = din("lru_wa", (NL, 2, 8, 64, 64))
        self.ba = din("lru_ba", (NL, 2, 512))
        self.wx = din("lru_wx", (NL, 2, 8, 64, 64))
        self.bx = din("lru_bx", (NL, 2, 512))
        self.lam = din("lru_lambda", (NL, 2, 512))
        self.lbl = din("hgrn_lb_logits", (NL, 2, 512))
        self.gn = din("hgrn_norm", (NL, 512))
        self.w_br = din("w_branch", (NL, 4, 512, DM))
        self.w_mg = din("w_merge", (NL, 4, DM, DM))
        self.w_o = din("w_out", (NL, DM, DM))
        self.ple_norm = din("ple_norm", (NL, DM))
        self.w_pg = din("w_ple_gate", (NL, DM, DM))
        self.w_pp = din("w_ple_proj", (NL, PLE, DM))
        self.final_norm = din("final_norm", (DM,))
        self.c_ident = din("c_ident", (128, 128))
        self.c_maskF = din("c_maskF", (128, 128))
        self.c_cos = din("c_cos", (128, T))
        self.c_sin = din("c_sin", (128, T))
        self.c_ret = din("c_ret", (6, 128, 128))
        self.c_maskB = din("c_maskB", (128, 128))
        self.y = nc.dram_tensor("y", [ns, T, DM], F32, kind="ExternalOutput").ap()
        self.xs = nc.dram_tensor("xs", [ns, T, DM], F32, kind="Internal").ap()
        self.ebs = nc.dram_tensor("ebs", [NL, 128, 8, 14, 64], F32, kind="Internal").ap()
        self.B_ebs = Buf('ebs')

    def alloc(self):
        nc = self.nc
        self.ident_f = self.sb("ident_f", [128, 128], F32)
        self.ident_b = self.sb("ident_b", [128, 128], BF16)
        self.ones_f = self.sb("ones_f", [128, 128], F32)
        self.ones_b = self.sb("ones_b", [128, 128], BF16)
        self.prm = self.sb("prm", [128, NL, 4, NQ], F32)
        self.hT = self.sb("hT", [128, 8, T], BF16)
        self.merged = self.sb("merged", [128, 8, T], F32)
        self.brT = self.sb("brT", [128, 4, T], BF16)
        self.B_hT = Buf('hT')
        self.B_merged = [[Buf() for _ in range(NTQ)] for _ in range(8)]
        self.B_brT = [Buf() for _ in range(4)]
        self.B_const = Buf('const')
        self.B_prm = Buf('prm')
        used = 128 * 4 * 2 + 128 * 2 * 2 + NL * 4 * NQ * 4 + 8 * T * 2 + 8 * T * 4 + 4 * T * 2
        self.arena_bytes = (207 * 1024 - used) // 64 * 64
        self.arena = self.sb("arena", [128, self.arena_bytes // 4], F32)
        self.aoff = 0
        self.psum = []
        self.psum_bf = []
        self.pbuf = []
        for i in range(8):
            t = self.st.enter_context(nc.psum_tensor(f"ps{i}", [128, 512], F32))
            self.psum.append(t)
            self.psum_bf.append(t[:].bitcast(BF16))
            self.pbuf.append(Buf(f'ps{i}'))

    def mm(self, out, lhsT, rhs, start, stop, r, wb, **kw):
        d = {'w': [wb]} if start else {'wp': [wb]}
        return self.P.pe(lambda h: h.matmul(out, lhsT=lhsT, rhs=rhs, start=start, stop=stop, **kw), r=r, **d)

    def mm_group(self, out_ap, lhs_list, rhs_list, r, wb):
        n = len(lhs_list)
        for k in range(n):
            self.mm(out_ap, lhs_list[k], rhs_list[k], k == 0, k == n - 1, r, wb)

    def tr(self, out, in_, ident, r, wb, first=True):
        d = {'w': [wb]} if first else {'wp': [wb]}
        return self.P.pe(lambda h: h.transpose(out=out, in_=in_, identity=ident), r=r, **d)

    def actf(self, out, in_, func, r, w=(), wp=(), **kw):
        return self.P.act(lambda h: h.activation(out=out, in_=in_, func=func, **kw), r=r, w=w, wp=wp)

    def tt(self, out, in0, in1, op, r, w=(), wp=(), eng='dve'):
        return self.P.add(eng, lambda h: h.tensor_tensor(out=out, in0=in0, in1=in1, op=op), r=r, w=w, wp=wp)

    def ts(self, out, in0, s1, s2, op0, op1=None, r=(), w=(), wp=(), eng='dve'):
        if op1 is None:
            return self.P.add(eng, lambda h: h.tensor_scalar(out=out, in0=in0, scalar1=s1, scalar2=None, op0=op0), r=r, w=w, wp=wp)
        return self.P.add(eng, lambda h: h.tensor_scalar(out=out, in0=in0, scalar1=s1, scalar2=s2, op0=op0, op1=op1), r=r, w=w, wp=wp)

    def stt(self, out, in0, scalar, in1, op0, op1, r, w=(), wp=()):
        return self.P.dve(lambda h: h.scalar_tensor_tensor(out=out, in0=in0, scalar=scalar, in1=in1, op0=op0, op1=op1), r=r, w=w, wp=wp)

    def cp(self, out, in_, r, w=(), wp=(), eng='dve'):
        if eng == 'act':
            return self.P.act(lambda h: h.copy(out=out, in_=in_), r=r, w=w, wp=wp)
        return self.P.add(eng, lambda h: h.tensor_copy(out=out, in_=in_), r=r, w=w, wp=wp)

    def mset(self, ap, val, w=(), wp=(), eng='dve'):
        return self.P.add(eng, lambda h: h.memset(ap, val), w=w, wp=wp)

    def recip(self, out, in_, r, w=(), wp=()):
        return self.P.dve(lambda h: h.reciprocal(out=out, in_=in_), r=r, w=w, wp=wp)

    def scan(self, out, d0, d1, r, w=(), wp=()):
        return self.P.dve(lambda h: h.tensor_tensor_scan(out=out, data0=d0, data1=d1, initial=0.0, op0=ALU.mult, op1=ALU.add), r=r, w=w, wp=wp)

    def dma(self, out, in_, r=(), w=(), wp=(), eng='sp'):
        return self.P.dma(lambda h: h.dma_start(out=out, in_=in_), eng=eng, r=r, w=w, wp=wp)

    def mk_wpool(self, n, words):
        return {'bufs': [(self.carve([128, words], BF16), Buf()) for _ in range(n)], 'rr': 0}

    def wload(self, src, nkc, ncols=128, eng='pool', pool=None):
        if pool is None:
            pool = self.wpool
        i = pool['rr']
        pool['rr'] = (i + 1) % len(pool['bufs'])
        t, b = pool['bufs'][i]
        v = t[:, 0:nkc * ncols].rearrange("p (k c) -> p k c", k=nkc)
        self.dma(v, src.rearrange("(k p) c -> p k c", p=128), w=[b], eng=eng)
        return v, b

    def setup_wbufs(self, n, words):
        self.wpool = self.mk_wpool(n, words)

    def phase0(self):
        self.phase()
        self.dma(self.ident_f[:], self.c_ident[:, :], w=[self.B_const])
        self.dma(self.ident_b[:], self.c_ident[:, :], wp=[self.B_const], eng='pool')
        self.mset(self.ones_f[:], 1.0, wp=[self.B_const])
        self.mset(self.ones_b[:], 1.0, wp=[self.B_const])
        stg = self.carve([NR, 512], F32)
        Bs = Buf()
        for l in range(NL):
            rows = [(R_CW, self.conv_w[l], 4), (R_CB, self.conv_b[l:l + 1], 1), (R_BA, self.ba[l], 2),
                    (R_BX, self.bx[l], 2), (R_LAM, self.lam[l], 2), (R_LB, self.lbl[0], 2),
                    (R_LB + 2, self.lbl[1], 2), (R_GN, self.gn[l:l + 1], 1)]
            first = True
            for r0, src, n in rows:
                if first:
                    self.dma(stg[r0:r0 + n, :], src, w=[Bs])
                else:
                    self.dma(stg[r0:r0 + n, :], src, wp=[Bs])
                first = False
            for fc in range(4):
                ps, _, pb = self.bank()
                self.tr(ps[:, 0:NR], stg[0:NR, fc * 128:(fc + 1) * 128], self.ident_f[0:NR, 0:NR], [Bs, self.B_const], pb)
                self.cp(self.prm[:, l, fc, 0:NR], ps[:, 0:NR], r=[pb], wp=[self.B_prm])
        tmp = self.carve([128, NL, 4, 2], F32)
        tmp2 = self.carve([128, 4, 2], F32)
        Bt = Buf()
        self.softplus_neg(self.prm[:, :, :, R_LAM:R_LAM + 2], tmp, [128, NL, 4, 2], self.B_prm, Bt)
        self.ts(self.prm[:, :, :, Q_C1:Q_C1 + 2], tmp, -8.0, None, ALU.mult, r=[Bt], wp=[self.B_prm])
        self.ts(self.prm[:, :, :, Q_C2:Q_C2 + 2], tmp, -16.0, None, ALU.mult, r=[Bt], wp=[self.B_prm])
        self.mset(self.prm[:, 0, :, Q_LB:Q_LB + 2], 0.0, wp=[self.B_prm])
        self.mset(self.prm[:, 0, :, Q_OML:Q_OML + 2], 1.0, wp=[self.B_prm])
        if NL > 1:
            B2 = Buf()
            self.tt(tmp2, self.prm[:, 1, :, R_LB + 2:R_LB + 4], self.prm[:, 1, :, R_LB:R_LB + 2], ALU.subtract,
                    r=[self.B_prm], w=[B2])
            self.actf(self.prm[:, 1, :, Q_LB:Q_LB + 2], tmp2, AF.Sigmoid, r=[B2], wp=[self.B_prm])
            self.actf(self.prm[:, 1, :, Q_OML:Q_OML + 2], tmp2, AF.Sigmoid, r=[B2], wp=[self.B_prm], scale=-1.0)

    def build_nat_tables(self):
        self.phase()
        BT = self.carve([64, 120, 64], F32)
        ebt = [(self.carve([128, 8, 64], F32), Buf()) for _ in range(2)]
        BBT = Buf()
        first_out = True
        for l in range(NL):
            self.mset(BT, -30000.0, w=[BBT])
            src2 = self.rpb[l].rearrange("h r m -> (h r) m")
            for qc in range(64):
                ws = min(max(qc - 8, 0), 48)
                a = 15 - qc + ws
                s_ap = src2[:, a:a + 16]
                s_ap = bass.AP(s_ap.tensor, s_ap.offset, [[0, 1]] + [list(x) for x in s_ap.ap])
                self.dma(BT[qc:qc + 1, :, ws:ws + 16], s_ap, wp=[BBT])
            gi = 0
            for h in range(8):
                for p0 in (0, 8):
                    npair = 8 if p0 == 0 else 6
                    ps, _, pb = self.bank()
                    for pi in range(npair):
                        p = p0 + pi
                        in_ = BT[:, h * 15 + p:h * 15 + p + 2, :].rearrange("q a k -> q (a k)")
                        self.tr(ps[:, pi * 64:(pi + 1) * 64], in_, self.ident_f[0:64, 0:64], [BBT, self.B_const], pb, first=(pi == 0))
                    et, Bet = ebt[gi % 2]
                    gi += 1
                    self.actf(et[:, 0:npair, :], ps[:, 0:npair * 64].rearrange("p (a b) -> p a b", a=npair), AF.Exp, r=[pb], w=[Bet])
                    kw = {'w': [self.B_ebs]} if first_out else {'wp': [self.B_ebs]}
                    first_out = False
                    self.dma(self.ebs[l, :, h, p0:p0 + npair, :], et[:, 0:npair, :], r=[Bet], **kw)

    def softplus_neg(self, x_ap, out_ap, shape, Bx, Bo):
        e = self.carve(shape, F32)
        L = self.carve(shape, F32)
        u = self.carve(shape, F32)
        u2 = self.carve(shape, F32)
        q = self.carve(shape, F32)
        Be, BL, Bu, Bu2, Bq = Buf(), Buf(), Buf(), Buf(), Buf()
        self.actf(e, x_ap, AF.Exp, r=[Bx], w=[Be], scale=-1.0)
        self.actf(L, e, AF.Ln, r=[Be], w=[BL], bias=1.0)
        self.ts(u, e, 2.0, None, ALU.add, r=[Be], w=[Bu])
        self.recip(u, u, r=[Bu], w=[Bu])
        self.tt(u, u, e, ALU.mult, r=[Bu, Be], w=[Bu])
        self.tt(u2, u, u, ALU.mult, r=[Bu], w=[Bu2])
        self.ts(q, u2, 1.0 / 9, 1.0 / 7, ALU.mult, ALU.add, r=[Bu2], w=[Bq])
        for c in (1.0 / 5, 1.0 / 3, 1.0):
            self.tt(q, q, u2, ALU.mult, r=[Bq, Bu2], w=[Bq])
            self.ts(q, q, c, None, ALU.add, r=[Bq], w=[Bq])
        self.tt(q, q, u, ALU.mult, r=[Bq, Bu], w=[Bq])
        self.ts(q, q, 2.0, None, ALU.mult, r=[Bq], w=[Bq])
        self.tt(q, q, L, ALU.subtract, r=[Bq, BL], w=[Bq])
        self.ts(u2, e, 0.3, None, ALU.is_lt, r=[Be], w=[Bu2])
        self.tt(q, q, u2, ALU.mult, r=[Bq, Bu2], w=[Bq])
        self.tt(out_ap, q, L, ALU.add, r=[Bq, BL], w=[Bo])

    def rms_stats(self, xt, Bx, junk, Bj, ss, rs, Bss, Brs):
        self.actf(junk, xt, AF.Square, r=[Bx], w=[Bj, Bss], accum_out=ss)
        self.actf(rs, ss, AF.Sqrt, r=[Bss], w=[Brs], scale=1.0 / DM, bias=EPS)
        self.recip(rs, rs, r=[Brs], w=[Brs])

    def phase1(self, s, l):
        self.phase()
        xsrc = self.x[s] if l == 0 else self.xs[s]
        gbc = self.carve([128, DM], F32)
        Bg = Buf()
        self.dma(gbc, self.norm_mix[l:l + 1, :].broadcast_to([128, DM]), w=[Bg])
        junk = self.carve([128, DM], BF16)
        Bj = Buf()
        xts = [(self.carve([128, DM], F32), Buf()) for _ in range(3)]
        hns = [(self.carve([128, DM], BF16), Buf()) for _ in range(2)]
        sts = [(self.carve([128, 1], F32), self.carve([128, 1], F32), Buf(), Buf()) for _ in range(2)]
        for tt in range(NTT):
            xt, Bx = xts[tt % 3]
            hn, Bh = hns[tt % 2]
            ss, rs, Bss, Brs = sts[tt % 2]
            self.dma(xt, xsrc[tt * 128:(tt + 1) * 128, :], w=[Bx])
            self.rms_stats(xt, Bx, junk, Bj, ss, rs, Bss, Brs)
            self.stt(hn, xt, rs, gbc, ALU.mult, ALU.mult, r=[Bx, Brs, Bg], w=[Bh])
            ps, psb, pb = self.bank()
            for kc in range(8):
                self.tr(psb[:, kc * 128:(kc + 1) * 128], hn[:, kc * 128:(kc + 1) * 128], self.ident_b[:], [Bh, self.B_const], pb, first=(kc == 0))
            kw = {'w': [self.B_hT]} if tt == 0 else {'wp': [self.B_hT]}
            self.cp(self.hT[:, :, tt * 128:(tt + 1) * 128], psb.rearrange("p (k c) -> p k c", k=8), r=[pb], eng='act', **kw)
        self.dump("hT", self.hT[:], [self.B_hT], BF16)

    def gen_merge(self, l, j, first, src=None, Bsrc=None, nw=4, nt=2, res=None):
        if src is None:
            src, Bsrc = self.brT, self.B_brT
        if res is None:
            res = self.merge_res(nw, nt)
        wp, sgs, tmps = res
        i = 0
        for dmc in range(8):
            wg, Bwg = self.wload(self.w_mg[l, j][:, dmc * 128:(dmc + 1) * 128], 8, pool=wp)
            wb, Bwb = self.wload(self.w_br[l, j][:, dmc * 128:(dmc + 1) * 128], 4, pool=wp)
            for tq in range(NTQ):
                sl = slice(tq * 512, (tq + 1) * 512)
                psg, _, pbg = self.bank()
                psp, _, pbp = self.bank()
                self.mm_group(psg[:, :], [wg[:, k, :] for k in range(8)], [self.hT[:, k, sl] for k in range(8)], [Bwg, self.B_hT], pbg)
                self.mm_group(psp[:, :], [wb[:, k, :] for k in range(4)], [src[:, k, sl] for k in range(4)], [Bwb] + list(Bsrc), pbp)
                sg, Bsg = sgs[i % len(sgs)]
                tmp, Btmp = tmps[i % len(tmps)]
                i += 1
                self.actf(sg, psg[:, :], AF.Sigmoid, r=[pbg], w=[Bsg])
                Bm = self.B_merged[dmc][tq]
                if first:
                    self.tt(self.merged[:, dmc, sl], psp[:, :], sg, ALU.mult, r=[pbp, Bsg], w=[Bm])
                else:
                    self.tt(tmp, psp[:, :], sg, ALU.mult, r=[pbp, Bsg], w=[Btmp])
                    self.tt(self.merged[:, dmc, sl], self.merged[:, dmc, sl], tmp, ALU.add, r=[Bm, Btmp], w=[Bm], eng='pool')
                yield
        self.dump(f"merged{j}", self.merged[:], [b for row in self.B_merged for b in row])

    def merge_res(self, nw, nt):
        return (self.mk_wpool(nw, 1024), [(self.carve([128, 512], F32), Buf()) for _ in range(nt)],
                [(self.carve([128, 512], F32), Buf()) for _ in range(nt)])

    def phase_merge(self, l, j, first):
        self.phase()
        for _ in self.gen_merge(l, j, first):
            pass

    def phase3(self, s, l, have_merged):
        self.phase()
        last = (l == self.nlayers - 1)
        xsrc = self.x[s] if l == 0 else self.xs[s]
        wo = self.carve([128, 8, DM], BF16)
        wpg = self.carve([128, 8, DM], BF16)
        wpp = self.carve([128, 2, DM], BF16)
        Bwo, Bwpg, Bwpp = Buf(), Buf(), Buf()
        if have_merged:
            self.dma(wo, self.w_o[l].rearrange("(k p) c -> p k c", p=128), w=[Bwo], eng='pool')
        self.dma(wpg, self.w_pg[l].rearrange("(k p) c -> p k c", p=128), w=[Bwpg], eng='pool')
        self.dma(wpp, self.w_pp[l].rearrange("(k p) c -> p k c", p=128), w=[Bwpp], eng='pool')
        gple = self.carve([128, DM], F32)
        gfin = self.carve([128, DM], F32)
        Bg = Buf()
        self.dma(gple, self.ple_norm[l:l + 1, :].broadcast_to([128, DM]), w=[Bg])
        self.dma(gfin, self.final_norm.rearrange("(a c) -> a c", a=1).broadcast_to([128, DM]), wp=[Bg])
        NB = 3
        xts = [(self.carve([128, DM], F32), Buf()) for _ in range(NB)]
        mbs = [(self.carve([128, 8, 128], BF16), Buf()) for _ in range(NB)]
        n2s = [(self.carve([128, DM], BF16), Buf()) for _ in range(NB)]
        n2Ts = [(self.carve([128, 8, 128], BF16), Buf()) for _ in range(NB)]
        pts = [(self.carve([128, PLE], F32), Buf()) for _ in range(NB)]
        pbs = [(self.carve([128, PLE], BF16), Buf()) for _ in range(NB)]
        pTs = [(self.carve([128, 2, 128], BF16), Buf()) for _ in range(NB)]
        gts = [(self.carve([128, DM], F32), Buf()) for _ in range(NB)]
        sts = [[(self.carve([128, 1], F32), self.carve([128, 1], F32), Buf(), Buf()) for _ in range(2)] for _ in range(NB)]
        outs = []

        def s1(tt):
            b = tt % NB
            tsl = slice(tt * 128, (tt + 1) * 128)
            xt, Bx = xts[b]
            self.dma(xt, xsrc[tsl, :], w=[Bx])
            if have_merged:
                mb, Bmb = mbs[b]
                self.cp(mb, self.merged[:, :, tsl], r=[self.B_merged[d][tt // 4] for d in range(8)], w=[Bmb], eng='dve')
                for hf in range(2):
                    ps, _, pb = self.bank()
                    self.mm_group(ps[:, :], [mb[:, k, :] for k in range(8)], [wo[:, k, hf * 512:(hf + 1) * 512] for k in range(8)], [Bmb, Bwo], pb)
                    self.tt(xt[:, hf * 512:(hf + 1) * 512], xt[:, hf * 512:(hf + 1) * 512], ps[:, :], ALU.add, r=[Bx, pb], w=[Bx])
            ss, rs, Bss, Brs = sts[b][0]
            n2, Bn2 = n2s[b]
            self.rms_stats(xt, Bx, n2, Bn2, ss, rs, Bss, Brs)
            self.stt(n2, xt, rs, gple, ALU.mult, ALU.mult, r=[Bx, Brs, Bg], w=[Bn2])
            ps, psb, pb = self.bank()
            for kc in range(8):
                self.tr(psb[:, kc * 128:(kc + 1) * 128], n2[:, kc * 128:(kc + 1) * 128], self.ident_b[:], [Bn2, self.B_const], pb, first=(kc == 0))
            n2T, Bn2T = n2Ts[b]
            self.cp(n2T, psb.rearrange("p (k c) -> p k c", k=8), r=[pb], w=[Bn2T], eng='act')
            pt, Bpt = pts[b]
            pbf, Bpbf = pbs[b]
            pT, BpT = pTs[b]
            self.dma(pt, self.p[l, s, tsl, :], w=[Bpt])
            self.cp(pbf, pt, r=[Bpt], w=[Bpbf], eng='pool')
            ps2, psb2, pb2 = self.bank()
            for c in range(2):
                self.tr(psb2[:, c * 128:(c + 1) * 128], pbf[:, c * 128:(c + 1) * 128], self.ident_b[:], [Bpbf, self.B_const], pb2, first=(c == 0))
            self.cp(pT, psb2[:, 0:256].rearrange("p (k c) -> p k c", k=2), r=[pb2], w=[BpT], eng='act')
            return (tt, b, tsl, xt, Bx, n2T, Bn2T, pT, BpT)

        def s2(ctx):
            tt, b, tsl, xt, Bx, n2T, Bn2T, pT, BpT = ctx
            gt, Bgt = gts[b]
            for hf in range(2):
                hs = slice(hf * 512, (hf + 1) * 512)
                psg, _, pbg = self.bank()
                psp, _, pbp = self.bank()
                self.mm_group(psg[:, :], [n2T[:, k, :] for k in range(8)], [wpg[:, k, hs] for k in range(8)], [Bn2T, Bwpg], pbg)
                self.mm_group(psp[:, :], [pT[:, k, :] for k in range(2)], [wpp[:, k, hs] for k in range(2)], [BpT, Bwpp], pbp)
                self.actf(gt[:, hs], psg[:, :], AF.Sigmoid, r=[pbg], w=[Bgt] if hf == 0 else (), wp=() if hf == 0 else [Bgt])
                self.tt(gt[:, hs], gt[:, hs], psp[:, :], ALU.mult, r=[Bgt, pbp], w=[Bgt])
            self.tt(xt, xt, gt, ALU.add, r=[Bx, Bgt], w=[Bx])
            if last:
                ss, rs, Bss, Brs = sts[b][1]
                n2, Bn2 = n2s[b]
                self.rms_stats(xt, Bx, n2, Bn2, ss, rs, Bss, Brs)
                self.stt(xt, xt, rs, gfin, ALU.mult, ALU.mult, r=[Bx, Brs, Bg], w=[Bx])
                outs.append(self.dma(self.y[s, tsl, :], xt, r=[Bx]))
            else:
                outs.append(self.dma(self.xs[s, tsl, :], xt, r=[Bx]))

        prev = None
        for tt in range(NTT):
            ctx = s1(tt)
            if prev is not None:
                s2(prev)
            prev = ctx
        s2(prev)
        return outs

    def branch_C(self, l):
        self.phase()
        for _ in self.gen_C(l):
            pass

    def gen_C(self, l):
        self.setup_wbufs(3, 1024)
        X = self.carve([128, T + 4], F32)
        XC, R, I, A, S, HF = [self.carve([128, T], F32) for _ in range(6)]
        BX, BXC, BR, BI, BA, BS, BHF = [Buf() for _ in range(7)]
        Wd = [[(self.carve([128, 128], F32), Buf()) for _ in range(2)] for _ in range(2)]
        pr = self.prm
        for fc in range(4):
            for d in range(2):
                for g, wsrc in enumerate((self.wa, self.wx)):
                    wt, Bw_ = Wd[d][g]
                    self.mset(wt, 0.0, w=[Bw_])
                    self.dma(wt[0:64, 0:64], wsrc[l, d, 2 * fc], wp=[Bw_])
                    self.dma(wt[64:128, 64:128], wsrc[l, d, 2 * fc + 1], wp=[Bw_])
            w, Bw = self.wload(self.w_in[l][:, C_X + fc * 128:C_X + (fc + 1) * 128], 8)
            self.mset(X[:, 0:2], 0.0, w=[BX])
            self.mset(X[:, T + 2:T + 4], 0.0, wp=[BX])
            for tq in range(NTQ):
                sl = slice(tq * 512, (tq + 1) * 512)
                ps, _, pb = self.bank()
                self.mm_group(ps[:, :], [w[:, k, :] for k in range(8)], [self.hT[:, k, sl] for k in range(8)], [Bw, self.B_hT], pb)
                self.cp(X[:, 2 + tq * 512:2 + (tq + 1) * 512], ps[:, :], r=[pb], wp=[BX], eng='act')
            yield
            self.ts(XC, X[:, 0:T], pr[:, l, fc, R_CW:R_CW + 1], pr[:, l, fc, R_CB:R_CB + 1], ALU.mult, ALU.add, r=[BX, self.B_prm], w=[BXC])
            for j in range(1, 4):
                self.stt(XC, X[:, j:T + j], pr[:, l, fc, R_CW + j:R_CW + j + 1], XC, ALU.mult, ALU.add, r=[BX, BXC, self.B_prm], w=[BXC])
            yield
            for d in range(2):
                for g, (dst, Bd, rb) in enumerate(((R, BR, R_BA), (I, BI, R_BX))):
                    wt, Bw_ = Wd[d][g]
                    for tq in range(NTQ):
                        sl = slice(tq * 512, (tq + 1) * 512)
                        ps, _, pb = self.bank()
                        self.mm(ps[:, :], wt, XC[:, sl], True, True, [Bw_, BXC], pb)
                        kw = {'w': [Bd]} if tq == 0 else {'wp': [Bd]}
                        self.actf(dst[:, sl], ps[:, :], AF.Sigmoid, r=[pb, self.B_prm], bias=pr[:, l, fc, rb + d:rb + d + 1], **kw)
                yield
                self.actf(A, R, AF.Exp, r=[BR, self.B_prm], w=[BA], scale=pr[:, l, fc, Q_C1 + d:Q_C1 + d + 1])
                self.actf(S, R, AF.Exp, r=[BR, self.B_prm], w=[BS], scale=pr[:, l, fc, Q_C2 + d:Q_C2 + d + 1])
                self.actf(S, S, AF.Sqrt, r=[BS], w=[BS], scale=-1.0, bias=1.0)
                self.tt(I, I, XC, ALU.mult, r=[BI, BXC], w=[BI])
                self.tt(I, I, S, ALU.mult, r=[BI, BS], w=[BI])
                yield
                if d == 0:
                    self.scan(HF, A, I, r=[BA, BI], w=[BHF])
                else:
                    self.scan(S[:, ::-1], A[:, ::-1], I[:, ::-1], r=[BA, BI], w=[BS])
                    self.tt(HF, HF, S, ALU.add, r=[BHF, BS], w=[BHF])
            yield
            if fc == 0:
                assert getattr(self, 'brT_free', True), "pending merge still reads brT"
            if fc == 3:
                self.dump("C_R", R, [BR])
                self.dump("C_B", I, [BI])
                self.dump("prm", self.prm[:], [self.B_prm])
            w, Bw = self.wload(self.w_in[l][:, C_G + fc * 128:C_G + (fc + 1) * 128], 8)
            for tq in range(NTQ):
                sl = slice(tq * 512, (tq + 1) * 512)
                ps, _, pb = self.bank()
                self.mm_group(ps[:, :], [w[:, k, :] for k in range(8)], [self.hT[:, k, sl] for k in range(8)], [Bw, self.B_hT], pb)
                kw = {'w': [BR]} if tq == 0 else {'wp': [BR]}
                self.actf(R[:, sl], ps[:, :], AF.Silu, r=[pb], **kw)
            self.tt(self.brT[:, fc, :], HF, R, ALU.mult, r=[BHF, BR], w=[self.B_brT[fc]])
            if fc == 3:
                self.dump("C_XC", XC, [BXC])
                self.dump("C_HF", HF, [BHF])
                self.dump("C_A", A, [BA])
                self.dump("C_X", X, [BX])
        self.dump("C_brT", self.brT[:], self.B_brT, BF16)

    def branch_D(self, l):
        self.phase()
        self.bank_set = [4, 5, 6, 7]
        self.setup_wbufs(3, 1024)
        pr = self.prm
        maskF = self.carve([128, 128], F32)
        maskB = self.carve([128, 128], F32)
        M0 = self.carve([128, 512], BF16)
        M1 = self.carve([128, 512], BF16)
        Bk = Buf()
        self.dma(maskF, self.c_maskF[:, :], w=[Bk])
        self.dma(maskB, self.c_maskB[:, :], wp=[Bk])
        self.mset(M0, 1.0, wp=[Bk])
        self.mset(M0.rearrange("p (c j) -> p c j", j=32)[:, :, 0:1], 0.0, wp=[Bk])
        self.mset(M1, 1.0, wp=[Bk])
        self.mset(M1.rearrange("p (c j) -> p c j", j=32)[:, :, 31:32], 0.0, wp=[Bk])
        Q, E, G, Bc, O = [self.carve([128, T], F32) for _ in range(5)]
        qt, kh = [self.carve([128, T], BF16) for _ in range(2)]
        khtok = self.carve([128, NTT, 128], BF16)
        vtok = self.carve([128, NTT, 128], BF16)
        Sall = self.carve([128, 65, 128], BF16)
        Sm2 = [self.carve([128, 128], F32) for _ in range(2)]
        bl = self.carve([128, 64], F32)
        ac = self.carve([128, 64], F32)
        attms = [(self.carve([128, 512], BF16), Buf()) for _ in range(2)]
        BQ, BE, BG, BBc, BO, Bqt, Bkh, Bkhtok, Bvtok, BSall, BSm, Bbl, Bac = [Buf() for _ in range(13)]
        BSm2 = [Buf(), Buf()]
        BE2, BG2, BBc2, Bkh2, Bqt2, Bkhtok2, Bbl2, Bac2 = [[Buf(), Buf()] for _ in range(8)]
        bl_bc = bass.AP(bl.tensor, bl.offset, [list(bl.ap[0]), [1, 64], [0, 32]])
        v3 = lambda a: a.rearrange("p (c j) -> p c j", j=32)
        def d_prologue(h):
                cs = slice(h * 128, (h + 1) * 128)
                w, Bw = self.wload(self.w_in[l][:, D_Q + h * 128:D_Q + (h + 1) * 128], 8)
                for tq in range(NTQ):
                    sl = slice(tq * 512, (tq + 1) * 512)
                    ps, _, pb = self.bank()
                    self.mm_group(ps[:, :], [w[:, k, :] for k in range(8)], [self.hT[:, k, sl] for k in range(8)], [Bw, self.B_hT], pb)
                    kw = {'w': [BQ]} if tq == 0 else {'wp': [BQ]}
                    self.actf(Q[:, sl], ps[:, :], AF.Silu, r=[pb], **kw)
                w, Bw = self.wload(self.w_in[l][:, D_I + h * 128:D_I + (h + 1) * 128], 8)
                for g4 in range(4):
                    ps, _, pb = self.bank()
                    for i4 in range(4):
                        tt = g4 * 4 + i4
                        for k in range(8):
                            self.mm(ps[:, i4 * 128:(i4 + 1) * 128], self.hT[:, k, tt * 128:(tt + 1) * 128], w[:, k, :], k == 0, k == 7,
                                    [Bw, self.B_hT], pb) if (i4 == 0 and k == 0) else \
                                self.P.pe(lambda hh, o_=ps[:, i4 * 128:(i4 + 1) * 128], a_=self.hT[:, k, tt * 128:(tt + 1) * 128], b_=w[:, k, :], s_=(k == 0), e_=(k == 7):
                                          hh.matmul(o_, lhsT=a_, rhs=b_, start=s_, stop=e_), r=[Bw, self.B_hT], wp=[pb])
                    kw = {'w': [Bvtok]} if g4 == 0 else {'wp': [Bvtok]}
                    self.cp(vtok[:, g4 * 4:(g4 + 1) * 4, :], ps[:, :].rearrange("p (a b) -> p a b", a=4), r=[pb], eng='act', **kw)

        d_prologue(0)
        for h in range(4):
            cs = slice(h * 128, (h + 1) * 128)
            for d in range(2):
                zoff = (D_FF if d == 0 else D_FB) + h * 128
                w, Bw = self.wload(self.w_in[l][:, zoff:zoff + 128], 8)
                HS = [slice(0, 1024), slice(1024, 2048)]
                for tq in range(NTQ):
                    sl = slice(tq * 512, (tq + 1) * 512)
                    hf = tq // 2
                    ps, _, pb = self.bank()
                    self.mm_group(ps[:, :], [w[:, k, :] for k in range(8)], [self.hT[:, k, sl] for k in range(8)], [Bw, self.B_hT], pb)
                    kw = {'w': [BE2[hf], BE]} if tq % 2 == 0 else {'wp': [BE2[hf]]}
                    self.actf(E[:, sl], ps[:, :], AF.Sigmoid, r=[pb], **kw)
                for hf in range(2):
                    self.ts(E[:, HS[hf]], E[:, HS[hf]], pr[:, l, h, Q_OML + d:Q_OML + d + 1], pr[:, l, h, Q_LB + d:Q_LB + d + 1], ALU.mult, ALU.add,
                            r=[BE2[hf], self.B_prm], w=[BE2[hf]])
                for hf in range(2):
                    self.actf(G[:, HS[hf]], E[:, HS[hf]], AF.Ln, r=[BE2[hf]], w=[BG2[hf], BG])
                for hf in range(2):
                    self.ts(E[:, HS[hf]], E[:, HS[hf]], -1.0, 1.0, ALU.mult, ALU.add, r=[BE2[hf]], w=[BE2[hf]])
                for tq in range(NTQ):
                    sl = slice(tq * 512, (tq + 1) * 512)
                    hf = tq // 2
                    kw = {'w': [BBc2[hf]]} if tq % 2 == 0 else {'wp': [BBc2[hf]]}
                    if d == 0:
                        self.scan(Bc[:, sl], M0, G[:, sl], r=[Bk, BG2[hf]], **kw)
                    else:
                        self.scan(Bc[:, sl][:, ::-1], M1[:, ::-1], G[:, sl][:, ::-1], r=[Bk, BG2[hf]], **kw)
                edge = 31 if d == 0 else 0
                CS = [slice(0, 32), slice(32, 64)]
                for hf in range(2):
                    self.cp(bl[:, CS[hf]], v3(Bc)[:, CS[hf], edge], r=[BBc2[hf]], w=[Bbl2[hf]])
                for hf in range(2):
                    self.actf(ac[:, CS[hf]], bl[:, CS[hf]], AF.Exp, r=[Bbl2[hf]], w=[Bac2[hf]])
                for hf in range(2):
                    blh = bl[:, CS[hf]]
                    blh_bc = bass.AP(blh.tensor, blh.offset, [list(blh.ap[0]), [1, 32], [0, 32]])
                    self.tt(v3(G)[:, CS[hf], :], blh_bc, v3(Bc)[:, CS[hf], :], ALU.subtract, r=[Bbl2[hf], BBc2[hf], BG2[hf]], w=[BG2[hf]])
                for hf in range(2):
                    self.actf(G[:, HS[hf]], G[:, HS[hf]], AF.Exp, r=[BG2[hf]], w=[BG2[hf]])
                for hf in range(2):
                    self.tt(kh[:, HS[hf]], E[:, HS[hf]], G[:, HS[hf]], ALU.mult, r=[BE2[hf], BG2[hf]], w=[Bkh2[hf]])
                for g8 in range(2):
                    ps, psb, pb = self.bank()
                    for i8 in range(8):
                        tt = g8 * 8 + i8
                        self.tr(psb[:, i8 * 128:(i8 + 1) * 128], kh[:, tt * 128:(tt + 1) * 128], self.ident_b[:], [Bkh2[g8], self.B_const], pb, first=(i8 == 0))
                    self.cp(khtok[:, g8 * 8:(g8 + 1) * 8, :], psb.rearrange("p (a b) -> p a b", a=8), r=[pb], eng='act', w=[Bkhtok2[g8]])
                for hf in range(2):
                    self.actf(G[:, HS[hf]], Bc[:, HS[hf]], AF.Exp, r=[BBc2[hf], BG2[hf]], w=[BG2[hf]])
                for hf in range(2):
                    self.tt(qt[:, HS[hf]], Q[:, HS[hf]], G[:, HS[hf]], ALU.mult, r=[BQ, BG2[hf]], w=[Bqt2[hf]])
                for hf in range(2):
                    self.actf(G[:, HS[hf]], Bc[:, HS[hf]], AF.Exp, r=[BBc2[hf], BG2[hf]], w=[BG2[hf]], scale=-1.0)
                for hf in range(2):
                    self.tt(kh[:, HS[hf]], E[:, HS[hf]], G[:, HS[hf]], ALU.mult, r=[BE2[hf], BG2[hf], Bkhtok2[hf]], w=[Bkh2[hf]])
                self.mset(Sm2[0], 0.0, w=[BSm2[0]])
                step = 0
                s0 = 0 if d == 0 else 64
                self.mset(Sall[:, s0, :], 0.0, w=[BSall])
                for rnd in range(4):
                    tiles = [rnd * 4 + i for i in range(4)] if d == 0 else [15 - rnd * 4 - i for i in range(4)]
                    corder = [0, 1, 2, 3] if d == 0 else [3, 2, 1, 0]
                    for bi, tt in enumerate(tiles):
                        for j in corder:
                            psj, _, pbj = self.bankx(j)
                            o_ = psj[:, bi * 128:(bi + 1) * 128]
                            a_ = khtok[32 * j:32 * j + 32, tt, :]
                            b_ = vtok[32 * j:32 * j + 32, tt, :]
                            kw = {'w': [pbj]} if bi == 0 else {'wp': [pbj]}
                            self.P.pe(lambda hh, o_=o_, a_=a_, b_=b_, j=j: hh.matmul(o_, lhsT=a_, rhs=b_, start=True, stop=True, tile_position=(32 * j, 0)),
                                      r=Bkhtok2 + [Bvtok], **kw)
                    for bi, tt in enumerate(tiles):
                        for j in corder:
                            c = tt * 4 + j
                            psj, _, pbj = self.bankx(j)
                            s_src, s_dst = Sm2[step % 2], Sm2[(step + 1) % 2]
                            Bs_src, Bs_dst = BSm2[step % 2], BSm2[(step + 1) % 2]
                            step += 1
                            self.stt(s_dst, s_src, ac[:, c:c + 1], psj[:, bi * 128:(bi + 1) * 128], ALU.mult, ALU.add, r=[Bs_src, pbj] + Bac2, w=[Bs_dst])
                            nxt = c + 1 if d == 0 else c
                            self.cp(Sall[:, nxt, :], s_dst, r=[Bs_dst], wp=[BSall], eng='act')
                mask = maskF if d == 0 else maskB
                mask_bc = bass.AP(mask.tensor, mask.offset, [list(mask.ap[0]), [0, 4], [1, 128]])
                def o1(tq):
                    sl = slice(tq * 512, (tq + 1) * 512)
                    psA, _, pbA = self.bank()
                    for i4 in range(4):
                        tsl = slice(tq * 512 + i4 * 128, tq * 512 + (i4 + 1) * 128)
                        kw = {'w': [pbA]} if i4 == 0 else {'wp': [pbA]}
                        self.P.pe(lambda hh, o_=psA[:, i4 * 128:(i4 + 1) * 128], a_=kh[:, tsl], b_=qt[:, tsl]: hh.matmul(o_, lhsT=a_, rhs=b_, start=True, stop=True),
                                  r=Bkh2 + Bqt2, **kw)
                    attm, Battm = attms[tq % 2]
                    self.tt(attm.rearrange("p (a b) -> p a b", a=4), psA[:, :].rearrange("p (a b) -> p a b", a=4), mask_bc, ALU.mult,
                            r=[pbA, Bk], w=[Battm])
                    return (tq, sl, attm, Battm)

                def o2(ctx):
                    tq, sl, attm, Battm = ctx
                    psO, _, pbO = self.bank()
                    for i4 in range(4):
                        tt = tq * 4 + i4
                        osl = slice(i4 * 128, (i4 + 1) * 128)
                        kw = {'w': [pbO]} if i4 == 0 else {'wp': [pbO]}
                        self.P.pe(lambda hh, o_=psO[:, osl], a_=vtok[:, tt, :], b_=attm[:, osl]: hh.matmul(o_, lhsT=a_, rhs=b_, start=True, stop=False),
                                  r=[Bvtok, Battm], **kw)
                        for j in range(4):
                            c = tt * 4 + j
                            slot = c if d == 0 else c + 1
                            self.P.pe(lambda hh, o_=psO[:, i4 * 128 + j * 32:i4 * 128 + (j + 1) * 32], a_=Sall[:, slot, :], b_=qt[:, c * 32:(c + 1) * 32], e_=(j == 3):
                                      hh.matmul(o_, lhsT=a_, rhs=b_, start=False, stop=e_), r=[BSall] + Bqt2, wp=[pbO])
                    if d == 0:
                        kw = {'w': [BO]} if tq == 0 else {'wp': [BO]}
                        self.cp(O[:, sl], psO[:, :], r=[pbO], eng='act', **kw)
                    else:
                        self.tt(O[:, sl], O[:, sl], psO[:, :], ALU.add, r=[BO, pbO], wp=[BO])
                prev_o = None
                for tq in range(NTQ):
                    ctx_o = o1(tq)
                    if prev_o is not None:
                        o2(prev_o)
                    prev_o = ctx_o
                o2(prev_o)
            if h < 3:
                d_prologue(h + 1)
            self.actf(G, O, AF.Square, r=[BO, BG] + BG2, w=[BG])
            for tq in range(NTQ):
                sl = slice(tq * 512, (tq + 1) * 512)
                ps, _, pb = self.bank()
                self.mm(ps[:, :], self.ones_f[:], G[:, sl], True, True, [BG, self.B_const], pb)
                kw = {'w': [BE]} if tq == 0 else {'wp': [BE]}
                kw_r = BE2
                self.actf(E[:, sl], ps[:, :], AF.Ln, r=[pb] + BE2, scale=1.0 / 128, bias=EPS, **kw)
            self.actf(E, E, AF.Exp, r=[BE], w=[BE], scale=-0.5)
            self.stt(O, O, pr[:, l, h, R_GN:R_GN + 1], E, ALU.mult, ALU.mult, r=[BO, BE, self.B_prm], w=[BO])
            w, Bw = self.wload(self.w_in[l][:, D_G + h * 128:D_G + (h + 1) * 128], 8)
            for tq in range(NTQ):
                sl = slice(tq * 512, (tq + 1) * 512)
                ps, _, pb = self.bank()
                self.mm_group(ps[:, :], [w[:, k, :] for k in range(8)], [self.hT[:, k, sl] for k in range(8)], [Bw, self.B_hT], pb)
                kw = {'w': [BG]} if tq == 0 else {'wp': [BG]}
                self.actf(G[:, sl], ps[:, :], AF.Silu, r=[pb], **kw)
            self.tt(self.brT[:, h, :], O, G, ALU.mult, r=[BO, BG], w=[self.B_brT[h]])
            if h == 3:
                self.dump("D_O", O, [BO])
                self.dump("D_qt", qt, [Bqt], BF16)
                self.dump("D_kh", kh, [Bkh], BF16)
                self.dump("D_Bc", Bc, [BBc])
                self.dump("D_Sall", Sall, [BSall], BF16)
        self.dump("D_brT", self.brT[:], self.B_brT, BF16)
        self.bank_set = list(range(8))

    def branch_A(self, l):
        self.phase()
        self.setup_wbufs(3, 1024)
        COS = self.carve([128, T], F32)
        SIN = self.carve([128, T], F32)
        CT = self.carve([128, 6, 128], F32)
        Bk = Buf()
        self.dma(COS, self.c_cos[:, :], w=[Bk])
        self.dma(SIN, self.c_sin[:, :], wp=[Bk])
        self.dma(CT, self.c_ret.rearrange("a p c -> p a c"), wp=[Bk])
        DF, UF, DB, UB, TQ, TK = [CT[:, i, :] for i in range(6)]
        lgt = self.carve([128, 8], F32)
        lg = self.carve([128, 8], F32)
        lgs = self.carve([128, 4], F32)
        cd = self.carve([128, 4], F32)
        Blg, Blgs = Buf(), Buf()
        self.dma(lgt, self.ret_logit[l].rearrange("d h -> (d h)").rearrange("(a c) -> a c", a=1).broadcast_to([128, 8]), w=[Blg])
        self.softplus_neg(lgt, lg, [128, 8], Blg, Blg)
        self.ts(lg, lg, -1.0, None, ALU.mult, r=[Blg], w=[Blg])
        self.cp(lgs[0:64, :], lg[0:64, 0:4], r=[Blg], w=[Blgs])
        self.cp(lgs[64:128, :], lg[64:128, 4:8], r=[Blg], wp=[Blgs])
        self.actf(cd, lgs, AF.Exp, r=[Blgs], wp=[Blgs], scale=128.0)
        MT = self.carve([128, 128], F32)
        WQ = self.carve([128, 128], F32)
        KW = self.carve([128, 128], F32)
        t1 = self.carve([128, 128], F32)
        BMT, BWQ, BKW, Bt1 = Buf(), Buf(), Buf(), Buf()
        qr, qh, kr, kst = [self.carve([128, T], BF16) for _ in range(4)]
        khtok = self.carve([128, NTT, 128], BF16)
        vtok = self.carve([128, NTT, 128], BF16)
        X = self.carve([128, NTT, 128], F32)
        prev = self.carve([128, NTT, 128], BF16)
        O = self.carve([128, T], F32)
        tmps = [(self.carve([128, 512], F32), Buf()) for _ in range(4)]
        attms = [(self.carve([128, 512], BF16), Buf()) for _ in range(2)]
        Bqr, Bqh, Bkr, Bkst, Bkhtok, Bvtok, BX, Bprev, BO = [Buf() for _ in range(9)]
        wd = self.carve([128, 8, 128], BF16)
        wsw = self.carve([128, 8, 128], BF16)
        Bwd = Buf()
        c3 = lambda a: a.rearrange("p (n c) -> p n c", c=128)
        bc16 = lambda a: bass.AP(a.tensor, a.offset, [list(a.ap[0]), [0, NTT], [1, 128]])
        bc4 = lambda a: bass.AP(a.tensor, a.offset, [list(a.ap[0]), [0, 4], [1, 128]])
        ti = 0
        def a_prologue(h):
                nonlocal ti
                lf = lg[:, h:h + 1]
                lb_ = lg[:, 4 + h:5 + h]
                self.actf(MT, DF, AF.Exp, r=[Bk, Blg], w=[BMT], scale=lf)
                self.tt(MT, MT, UF, ALU.mult, r=[BMT, Bk], w=[BMT])
                self.actf(t1, DB, AF.Exp, r=[Bk, Blg], w=[Bt1], scale=lb_)
                self.tt(t1, t1, UB, ALU.mult, r=[Bt1, Bk], w=[Bt1])
                self.tt(MT, MT, t1, ALU.add, r=[BMT, Bt1], w=[BMT])
                self.ts(MT, MT, 0.125, None, ALU.mult, r=[BMT], w=[BMT])
                self.actf(WQ, TQ, AF.Exp, r=[Bk, Blgs], w=[BWQ], scale=lgs[:, h:h + 1])
                self.actf(KW, TK, AF.Exp, r=[Bk, Blgs], w=[BKW], scale=lgs[:, h:h + 1])
                self.ts(KW, KW, 0.125, None, ALU.mult, r=[BKW], w=[BKW])
                for (c0, dst, Bd) in ((A_Q + h * 64, qr, Bqr), (A_K + h * 64, kr, Bkr)):
                    w0, Bw0 = self.wload(self.w_in[l][:, c0:c0 + 64], 8, ncols=64)
                    for a in range(2):
                        kw = {'w': [Bwd]} if a == 0 else {'wp': [Bwd]}
                        self.cp(wd[:, :, a * 64:(a + 1) * 64], w0, r=[Bw0], eng='pool', **kw)
                        for j2 in range(2):
                            self.cp(wsw[:, :, a * 64 + j2 * 32:a * 64 + (j2 + 1) * 32], w0[:, :, (1 - j2) * 32:(2 - j2) * 32], r=[Bw0], wp=[Bwd], eng='pool')
                    for tq in range(NTQ):
                        sl = slice(tq * 512, (tq + 1) * 512)
                        psn, _, pbn = self.bank()
                        pss, _, pbs = self.bank()
                        lh_n = [wd[:, k, :] for k in range(8)]
                        lh_s = [wsw[:, k, :] for k in range(8)]
                        rh = [self.hT[:, k, sl] for k in range(8)]
                        self.mm_group(psn[:, :], lh_n, rh, [Bwd, self.B_hT], pbn)
                        self.mm_group(pss[:, :], lh_s, rh, [Bwd, self.B_hT], pbs)
                        ta, Bta = tmps[ti % 4]
                        tb, Btb = tmps[(ti + 1) % 4]
                        ti += 2
                        self.tt(ta, psn[:, :], COS[:, sl], ALU.mult, r=[pbn, Bk], w=[Bta])
                        self.tt(tb, pss[:, :], SIN[:, sl], ALU.mult, r=[pbs, Bk], w=[Btb])
                        kw = {'w': [Bd]} if tq == 0 else {'wp': [Bd]}
                        self.tt(dst[:, sl], ta, tb, ALU.add, r=[Bta, Btb], eng='pool', **kw)
                self.tt(c3(qh), c3(qr), bc16(WQ), ALU.mult, r=[Bqr, BWQ], w=[Bqh])
                self.tt(c3(kst), c3(kr), bc16(KW), ALU.mult, r=[Bkr, BKW], w=[Bkst])
                for g8 in range(2):
                    ps, psb, pb = self.bank()
                    for i8 in range(8):
                        tt = g8 * 8 + i8
                        self.tr(psb[:, i8 * 128:(i8 + 1) * 128], kst[:, tt * 128:(tt + 1) * 128], self.ident_b[:], [Bkst, self.B_const], pb, first=(i8 == 0))
                    kw = {'w': [Bkhtok]} if g8 == 0 else {'wp': [Bkhtok]}
                    self.cp(khtok[:, g8 * 8:(g8 + 1) * 8, :], psb.rearrange("p (a b) -> p a b", a=8), r=[pb], eng='act', **kw)
                w, Bw = self.wload(self.w_in[l][:, A_V + h * 128:A_V + (h + 1) * 128], 8)
                for g4 in range(4):
                    ps, _, pb = self.bank()
                    for i4 in range(4):
                        tt = g4 * 4 + i4
                        for k in range(8):
                            kw = {'w': [pb]} if (i4 == 0 and k == 0) else {'wp': [pb]}
                            self.P.pe(lambda hh, o_=ps[:, i4 * 128:(i4 + 1) * 128], a_=self.hT[:, k, tt * 128:(tt + 1) * 128], b_=w[:, k, :], s_=(k == 0), e_=(k == 7):
                                      hh.matmul(o_, lhsT=a_, rhs=b_, start=s_, stop=e_), r=[Bw, self.B_hT], **kw)
                    kw = {'w': [Bvtok]} if g4 == 0 else {'wp': [Bvtok]}
                    self.cp(vtok[:, g4 * 4:(g4 + 1) * 4, :], ps[:, :].rearrange("p (a b) -> p a b", a=4), r=[pb], eng='act', **kw)

        a_prologue(0)
        for h in range(4):
            for g4 in range(4):
                ps, _, pb = self.bank()
                for i4 in range(4):
                    n = g4 * 4 + i4
                    kw = {'w': [pb]} if i4 == 0 else {'wp': [pb]}
                    self.P.pe(lambda hh, o_=ps[:, i4 * 128:(i4 + 1) * 128], a_=khtok[:, n, :], b_=vtok[:, n, :]: hh.matmul(o_, lhsT=a_, rhs=b_, start=True, stop=True),
                              r=[Bkhtok, Bvtok], **kw)
                kw = {'w': [BX]} if g4 == 0 else {'wp': [BX]}
                self.cp(X[:, g4 * 4:(g4 + 1) * 4, :], ps[:, :].rearrange("p (a b) -> p a b", a=4), r=[pb], eng='act', **kw)
            for n in range(1, NTT):
                self.stt(X[0:64, n, :], X[0:64, n - 1, :], cd[0:64, h:h + 1], X[0:64, n, :], ALU.mult, ALU.add, r=[BX, Blgs], w=[BX])
            for n in range(NTT - 2, -1, -1):
                self.stt(X[64:128, n, :], X[64:128, n + 1, :], cd[64:128, h:h + 1], X[64:128, n, :], ALU.mult, ALU.add, r=[BX, Blgs], w=[BX])
            self.mset(prev[0:64, 0, :], 0.0, w=[Bprev], eng='pool')
            self.mset(prev[64:128, NTT - 1, :], 0.0, wp=[Bprev], eng='pool')
            self.cp(prev[0:64, 1:NTT, :], X[0:64, 0:NTT - 1, :], r=[BX], wp=[Bprev], eng='pool')
            self.cp(prev[64:128, 0:NTT - 1, :], X[64:128, 1:NTT, :], r=[BX], wp=[Bprev], eng='pool')
            MT_bc = bc4(MT)
            def a1(tq):
                sl = slice(tq * 512, (tq + 1) * 512)
                psS, _, pbS = self.bank()
                for i4 in range(4):
                    tsl = slice(tq * 512 + i4 * 128, tq * 512 + (i4 + 1) * 128)
                    kw = {'w': [pbS]} if i4 == 0 else {'wp': [pbS]}
                    self.P.pe(lambda hh, o_=psS[:, i4 * 128:(i4 + 1) * 128], a_=kr[0:64, tsl], b_=qr[0:64, tsl]: hh.matmul(o_, lhsT=a_, rhs=b_, start=True, stop=True),
                              r=[Bkr, Bqr], **kw)
                attm, Battm = attms[tq % 2]
                self.tt(attm.rearrange("p (a b) -> p a b", a=4), psS[:, :].rearrange("p (a b) -> p a b", a=4), MT_bc, ALU.mult, r=[pbS, BMT], w=[Battm])
                return (tq, sl, attm, Battm)

            def a2(ctx):
                tq, sl, attm, Battm = ctx
                psO, _, pbO = self.bank()
                for i4 in range(4):
                    n = tq * 4 + i4
                    osl = slice(i4 * 128, (i4 + 1) * 128)
                    tsl = slice(n * 128, (n + 1) * 128)
                    kw = {'w': [pbO]} if i4 == 0 else {'wp': [pbO]}
                    self.P.pe(lambda hh, o_=psO[:, osl], a_=vtok[:, n, :], b_=attm[:, osl]: hh.matmul(o_, lhsT=a_, rhs=b_, start=True, stop=False),
                              r=[Bvtok, Battm], **kw)
                    self.P.pe(lambda hh, o_=psO[:, osl], a_=prev[:, n, :], b_=qh[:, tsl]: hh.matmul(o_, lhsT=a_, rhs=b_, start=False, stop=True),
                              r=[Bprev, Bqh], wp=[pbO])
                kw = {'w': [BO]} if tq == 0 else {'wp': [BO]}
                self.cp(O[:, sl], psO[:, :], r=[pbO], eng='act', **kw)
            prev_a = None
            for tq in range(NTQ):
                ctx_a = a1(tq)
                if prev_a is not None:
                    a2(prev_a)
                prev_a = ctx_a
            a2(prev_a)
            if h < 3:
                a_prologue(h + 1)
            SQ = X.rearrange("p a b -> p (a b)")
            self.actf(SQ, O, AF.Square, r=[BO, BX, Bprev], w=[BX])
            for tq in range(NTQ):
                sl = slice(tq * 512, (tq + 1) * 512)
                ps, _, pb = self.bank()
                self.mm(ps[:, :], self.ones_f[:], SQ[:, sl], True, True, [BX, self.B_const], pb)
                rt, Brt = tmps[tq]
                self.actf(rt, ps[:, :], AF.Ln, r=[pb], w=[Brt], scale=1.0 / 128, bias=EPS)
                self.actf(rt, rt, AF.Exp, r=[Brt], w=[Brt], scale=-0.5)
                self.tt(O[:, sl], O[:, sl], rt, ALU.mult, r=[BO, Brt], wp=[BO])
            w, Bw = self.wload(self.w_in[l][:, A_G + h * 128:A_G + (h + 1) * 128], 8)
            for tq in range(NTQ):
                sl = slice(tq * 512, (tq + 1) * 512)
                ps, _, pb = self.bank()
                self.mm_group(ps[:, :], [w[:, k, :] for k in range(8)], [self.hT[:, k, sl] for k in range(8)], [Bw, self.B_hT], pb)
                gt, Bgt = tmps[tq]
                self.actf(gt, ps[:, :], AF.Silu, r=[pb], w=[Bgt])
                kw = {'w': [self.B_brT[h]]} if tq == 0 else {'wp': [self.B_brT[h]]}
                self.tt(self.brT[:, h, sl], O[:, sl], gt, ALU.mult, r=[BO, Bgt], **kw)
            if h == 3:
                self.dump("A_O", O, [BO])
                self.dump("A_qr", qr, [Bqr], BF16)
                self.dump("A_kr", kr, [Bkr], BF16)
                self.dump("A_MT", MT, [BMT])
                self.dump("A_X", X, [BX])
                self.dump("A_lg", lg, [Blg])
        self.dump("A_brT", self.brT[:], self.B_brT, BF16)

    def branch_B(self, l, dst=None, Bdst=None):
        self.phase()
        if dst is None:
            odst, Bdst = self.brT, self.B_brT
        else:
            odst = self.carve([128, 4, T], BF16)
        self.bank_set = [4, 5, 6, 7]
        self.setup_wbufs(3, 1024)
        qT, kT = [self.carve([128, T], BF16) for _ in range(2)]
        Va = self.carve([128, 16, 128], BF16)
        Vb = self.carve([128, 16, 128], BF16)
        G = self.carve([128, T], F32)
        EB = self.carve([128, 2, 14, 64], F32)
        exs = [(self.carve([128, 512], F32), Buf()) for _ in range(2)]
        Ps = [(self.carve([128, 512], BF16), Buf()) for _ in range(2)]
        rds = [(self.carve([128, 512], F32), Buf()) for _ in range(2)]
        BqT, BkT, BVa, BVb, BG, BEB = [Buf() for _ in range(6)]
        it = 0
        for fc in range(4 if BST >= 1 else 0):
            self.dma(EB, self.ebs[l, :, 2 * fc:2 * fc + 2, :, :], r=[self.B_ebs], w=[BEB])
            for (c0, dst, Bd, fn) in ((B_Q, qT, BqT, None), (B_K, kT, BkT, None), (B_G, G, BG, AF.Silu)):
                w, Bw = self.wload(self.w_in[l][:, c0 + fc * 128:c0 + (fc + 1) * 128], 8)
                for tq in range(NTQ):
                    sl = slice(tq * 512, (tq + 1) * 512)
                    ps, _, pb = self.bank()
                    self.mm_group(ps[:, :], [w[:, k, :] for k in range(8)], [self.hT[:, k, sl] for k in range(8)], [Bw, self.B_hT], pb)
                    kw = {'w': [Bd]} if tq == 0 else {'wp': [Bd]}
                    if fn is None:
                        self.cp(dst[:, sl], ps[:, :], r=[pb], eng='act', **kw)
                    else:
                        self.actf(dst[:, sl], ps[:, :], fn, r=[pb], **kw)
            w, Bw = self.wload(self.w_in[l][:, B_V + fc * 128:B_V + (fc + 1) * 128], 8)
            for (Vt, BV, off, ntile) in ((Va, BVa, 0, 16), (Vb, BVb, 64, 15)):
                for g4 in range(4):
                    n4 = min(4, ntile - g4 * 4)
                    ps, _, pb = self.bank()
                    for i4 in range(n4):
                        t0 = off + (g4 * 4 + i4) * 128
                        for k in range(8):
                            kw = {'w': [pb]} if (i4 == 0 and k == 0) else {'wp': [pb]}
                            self.P.pe(lambda hh, o_=ps[:, i4 * 128:(i4 + 1) * 128], a_=self.hT[:, k, t0:t0 + 128], b_=w[:, k, :], s_=(k == 0), e_=(k == 7):
                                      hh.matmul(o_, lhsT=a_, rhs=b_, start=s_, stop=e_), r=[Bw, self.B_hT], **kw)
                    kw = {'w': [BV]} if g4 == 0 else {'wp': [BV]}
                    self.cp(Vt[:, g4 * 4:g4 * 4 + n4, :], ps[:, 0:n4 * 128].rearrange("p (a b) -> p a b", a=n4), r=[pb], eng='act', **kw)
            def s1(r):
                nonlocal it
                rs = min(max(r - 4, 0), 24)
                o = r - rs
                ex, Bex = exs[it % 2]
                Pt, BP = Ps[it % 2]
                it += 1
                for hh in range(2):
                    hb = hh * 64
                    psS, _, pbS = self.bank()
                    for i in range(4):
                        k0 = (rs + 2 * i) * 64
                        kw = {'w': [pbS]} if i == 0 else {'wp': [pbS]}
                        self.P.pe(lambda h_, o_=psS[:, i * 64:(i + 1) * 64], a_=kT[hb:hb + 64, k0:k0 + 128], b_=qT[hb:hb + 64, r * 64:(r + 1) * 64]:
                                  h_.matmul(o_, lhsT=a_, rhs=b_, start=True, stop=True), r=[BkT, BqT], **kw)
                    kw = {'w': [Bex]} if hh == 0 else {'wp': [Bex]}
                    self.actf(ex[:, hh * 256:(hh + 1) * 256], psS[:, 0:256], AF.Exp, r=[pbS], scale=0.125, **kw)
                eb = EB[:, :, 7 - o:7 - o + 7:2, :]
                self.tt(Pt.rearrange("p (h i q) -> p h i q", h=2, i=4), ex.rearrange("p (h i q) -> p h i q", h=2, i=4), eb, ALU.mult,
                        r=[Bex, BEB], w=[BP])
                return (r, rs, Pt, BP)

            def s2(ctx):
                r, rs, Pt, BP = ctx
                rg, r4 = r // 4, r % 4
                a = rs % 2
                Vt, BV = (Va, BVa) if a == 0 else (Vb, BVb)
                psN, _, pbN = self.bankx(2 * (rg % 2))
                psD, _, pbD = self.bankx(2 * (rg % 2) + 1)
                for hh in range(2):
                    oc = hh * 256 + r4 * 64
                    for (psX, pbX, isnum) in ((psN, pbN, True), (psD, pbD, False)):
                        for i in range(4):
                            ti = (rs + 2 * i - a) // 2
                            lh = Vt[:, ti, :] if isnum else self.ones_b[:, :]
                            kw = {'w': [pbX]} if (r4 == 0 and hh == 0 and i == 0) else {'wp': [pbX]}
                            self.P.pe(lambda h_, o_=psX[:, oc:oc + 64], a_=lh, b_=Pt[:, (hh * 4 + i) * 64:(hh * 4 + i + 1) * 64], s_=(i == 0), e_=(i == 3):
                                      h_.matmul(o_, lhsT=a_, rhs=b_, start=s_, stop=e_), r=[BV, BP, self.B_const], **kw)
                if r4 != 3:
                    return
                rd, Brd = rds[rg % 2]
                sl = slice(rg * 256, (rg + 1) * 256)
                for hh in range(2):
                    pr_ = slice(hh * 64, (hh + 1) * 64)
                    cs_ = slice(hh * 256, (hh + 1) * 256)
                    kw = {'w': [Brd]} if hh == 0 else {'wp': [Brd]}
                    self.actf(rd[pr_, 0:256], psD[pr_, cs_], AF.Ln, r=[pbD], **kw)
                    self.actf(rd[pr_, 0:256], rd[pr_, 0:256], AF.Exp, r=[Brd], wp=[Brd], scale=-1.0)
                    self.tt(rd[pr_, 0:256], rd[pr_, 0:256], psN[pr_, cs_], ALU.mult, r=[Brd, pbN], wp=[Brd])
                kw = {'w': [Bdst[fc]]} if rg == 0 else {'wp': [Bdst[fc]]}
                self.tt(odst[:, fc, sl], rd[:, 0:256], G[:, sl], ALU.mult, r=[Brd, BG], **kw)

            prev = None
            for r in range(32):
                ctx = s1(r)
                if prev is not None:
                    s2(prev)
                prev = ctx
            s2(prev)
        self.dump("B_brT", odst, Bdst, BF16)
        self.bank_set = list(range(8))

    def build(self):
        self.declare()
        self.alloc()
        self.phase0()
        if self.mask[1]:
            self.build_nat_tables()
        outs = []
        self.B_brT2 = [Buf() for _ in range(4)]
        for s in range(self.nseq):
            for l in range(self.nlayers):
                self.phase1(s, l)
                if all(self.mask):
                    self.branch_A(l)
                    self.phase_merge(l, 0, True)
                    self.branch_D(l)
                    self.branch_B(l, dst='arena', Bdst=self.B_brT2)
                    self.phase()
                    brT2 = self.carve([128, 4, T], BF16)
                    self.brT_free = False
                    mres = self.merge_res(3, 1)
                    gm = self.gen_merge(l, 3, False, res=mres)
                    gm2 = self.gen_merge(l, 1, False, src=brT2, Bsrc=self.B_brT2, res=mres)
                    gc = self.gen_C(l)
                    nmd = 0
                    cstep = 0
                    c_done = False
                    md_done = False
                    while not md_done:
                        if not c_done and cstep < 7:
                            try:
                                next(gc)
                                cstep += 1
                            except StopIteration:
                                c_done = True
                        for _ in range(5):
                            try:
                                next(gm)
                            except StopIteration:
                                md_done = True
                                break
                    self.brT_free = True
                    mb_done = False
                    while not (c_done and mb_done):
                        if not c_done:
                            try:
                                next(gc)
                            except StopIteration:
                                c_done = True
                        if not mb_done:
                            try:
                                next(gm2)
                            except StopIteration:
                                mb_done = True
                    self.phase_merge(l, 2, False)
                    first = False
                else:
                    first = True
                    for j, fn in enumerate((self.branch_A, self.branch_B, self.branch_C, self.branch_D)):
                        if not self.mask[j] or fn is None:
                            continue
                        fn(l)
                        self.phase_merge(l, j, first)
                        first = False
                o = self.phase3(s, l, not first)
                if l == self.nlayers - 1:
                    outs += o
        self.P.emit(final_wait=outs + self.dbg_outs)
        self.st.close()


_CACHE = {}


def get_nc(nseq, nlayers, mask):
    key = (nseq, nlayers, tuple(mask))
    if key not in _CACHE:
        nc = bass.Bass("TRN2", target_bir_lowering=False)
        kb = KB(nc, nseq, nlayers, mask)
        kb.build()
        _CACHE[key] = nc
    return _CACHE[key]


WNAMES = ['norm_mix', 'w_in', 'ret_decay_logit', 'nat_rpb', 'lru_conv_w', 'lru_conv_b', 'lru_wa', 'lru_ba', 'lru_wx',
          'lru_bx', 'lru_lambda', 'hgrn_lb_logits', 'hgrn_norm', 'w_branch', 'w_merge', 'w_out', 'ple_norm',
          'w_ple_gate', 'w_ple_proj', 'final_norm']


def host_consts():
    s = np.arange(128)[:, None]
    t = np.arange(128)[None, :]
    same = (s // 32) == (t // 32)
    half = 32
    inv = (np.float32(10000.0) ** (-(np.arange(half, dtype=np.float32) / np.float32(half)))).astype(np.float32)
    ang = (np.arange(T, dtype=np.float32)[None, :] * inv[:, None]).astype(np.float32)
    cs = np.cos(ang.astype(np.float64)).astype(np.float32)
    sn = np.sin(ang.astype(np.float64)).astype(np.float32)
    cos64 = np.concatenate([cs, cs], 0)
    sin64 = np.concatenate([-sn, sn], 0)
    c_cos = np.concatenate([cos64, cos64], 0)
    c_sin = np.concatenate([sin64, sin64], 0)
    tau = np.arange(128, dtype=np.float32)
    DF = np.maximum(t - s, 0).astype(np.float32)
    UF = (t >= s).astype(np.float32)
    DB = np.maximum(s - t, 0).astype(np.float32)
    UB = (s >= t).astype(np.float32)
    TQ = np.concatenate([np.tile(tau + 1, (64, 1)), np.tile(128 - tau, (64, 1))], 0)
    TK = np.concatenate([np.tile(127 - tau, (64, 1)), np.tile(tau, (64, 1))], 0)
    c_ret = np.stack([DF, UF, DB, UB, TQ, TK]).astype(np.float32)
    return {'c_ident': np.eye(128, dtype=np.float32), 'c_cos': c_cos, 'c_sin': c_sin, 'c_ret': c_ret,
            'c_maskF': (same & (s <= t)).astype(np.float32),
            'c_maskB': (same & (s >= t)).astype(np.float32)}


def run_seqs(xs, ps, weights, nlayers=NL, mask=(1, 1, 1, 1), ncores=8):
    n = xs.shape[0]
    per = n // ncores
    nc = get_nc(per, nlayers, mask)
    consts = host_consts()
    in_maps = []
    for c in range(ncores):
        m = {'x': np.ascontiguousarray(xs[c * per:(c + 1) * per]),
             'p': np.ascontiguousarray(ps[:, c * per:(c + 1) * per])}
        for k in WNAMES:
            m[k] = weights[k]
        m.update(consts)
        in_maps.append(m)
    res = run_bass_kernel_spmd(nc, in_maps, core_ids=list(range(ncores)))
    return np.concatenate([r['y'] for r in res.results], axis=0)


def kernel(**inputs):
    inputs = {k: np.asarray(v) for k, v in inputs.items()}
    xs = np.concatenate([inputs['x_prompt'], inputs['x_sample']], axis=0)
    ps = np.concatenate([inputs['p_prompt'], inputs['p_sample']], axis=1)
    weights = {k: np.ascontiguousarray(inputs[k], dtype=np.float32) for k in WNAMES}
    y = run_seqs(xs.astype(np.float32, copy=False), ps.astype(np.float32, copy=False), weights)
    nb = inputs['x_prompt'].shape[0]
    return (np.ascontiguousarray(y[:nb]), np.ascontiguousarray(y[nb:]))
```

```python
import numpy as np
from contextlib import ExitStack
import concourse.bass as bass
import concourse.mybir as mybir
from concourse.bass_utils import run_bass_kernel_spmd
from concourse.alu_op_type import AluOpType as ALU

AF = mybir.ActivationFunctionType
F32 = mybir.dt.float32
BF16 = mybir.dt.bfloat16
AX = mybir.AxisListType

T = 2048
DM = 1024
NL = 2
PLE = 256
W_IN = 7168
EPS = 1e-6
NTT = T // 128
NTQ = T // 512

import os
BST = int(os.environ.get('BST', '3'))
ENGS = ['pe', 'act', 'dve', 'pool', 'sp']
SEM_LIM = 20000
N_EPOCH = 6
N_DMA_SEMS = 32


class Buf:
    __slots__ = ('name', 'writers', 'readers', 'round_deps')

    def __init__(self, name=''):
        self.name = name
        self.writers = []
        self.readers = []
        self.round_deps = []


class Op:
    __slots__ = ('eng', 'fn', 'deps', 'sig', 'idx', 'dma', 'sem', 'val', 'prev_dma', 'gidx')

    def __init__(self, eng, fn, dma):
        self.eng = eng
        self.fn = fn
        self.deps = set()
        self.sig = False
        self.dma = dma
        self.sem = None
        self.val = 0
        self.prev_dma = None


class Prog:
    def __init__(self, nc):
        self.nc = nc
        self.ops = {e: [] for e in ENGS}
        self.all = []
        self.dma_last = [None] * N_DMA_SEMS
        self.dma_cnt = [0] * N_DMA_SEMS
        self.dma_rr = 0
        self.dma_rr_sw = 0
        self.bar_deps = {e: set() for e in ENGS}

    def add(self, eng, fn, r=(), w=(), wp=(), dma=False):
        o = Op(eng, fn, dma)
        o.idx = len(self.ops[eng])
        o.gidx = len(self.all)
        deps = set(self.bar_deps[eng])
        self.bar_deps[eng] = set()
        for b in r:
            deps.update(b.writers)
        for b in w:
            d = set(b.writers) | set(b.readers)
            deps.update(d)
            b.round_deps = list(d)
        for b in wp:
            deps.update(b.round_deps)
            deps.update(b.readers)
            if b.writers:
                deps.add(b.writers[0])
        for b in r:
            b.readers.append(o)
        for b in w:
            b.writers = [o]
            b.readers = []
        for b in wp:
            b.writers.append(o)
        o.deps = deps
        if dma:
            half = N_DMA_SEMS // 2
            if eng == 'pool':
                s = half + self.dma_rr_sw
                self.dma_rr_sw = (self.dma_rr_sw + 1) % half
            else:
                s = self.dma_rr
                self.dma_rr = (self.dma_rr + 1) % half
            o.sem = ('dma', s)
            self.dma_cnt[s] += 16
            o.val = self.dma_cnt[s]
            o.prev_dma = self.dma_last[s]
            self.dma_last[s] = o
        self.ops[eng].append(o)
        self.all.append(o)
        return o

    def barrier(self):
        last = set()
        for e in ENGS:
            lst = [o for o in self.ops[e] if not o.dma]
            if lst:
                last.add(lst[-1])
        for s in range(N_DMA_SEMS):
            if self.dma_last[s] is not None:
                last.add(self.dma_last[s])
        for e in ENGS:
            self.bar_deps[e] = set(last)

    def pe(self, fn, **k):
        return self.add('pe', fn, **k)

    def act(self, fn, **k):
        return self.add('act', fn, **k)

    def dve(self, fn, **k):
        return self.add('dve', fn, **k)

    def pool(self, fn, **k):
        return self.add('pool', fn, **k)

    def dma(self, fn, eng='sp', **k):
        return self.add(eng, fn, dma=True, **k)

    def emit(self, final_wait=()):
        nc = self.nc
        for o in self.all:
            nd = set()
            for d in o.deps:
                if d.dma:
                    nd.add(d)
                    continue
                if d.eng == o.eng and o.eng == 'pe' and not o.dma:
                    continue
                nd.add(d)
            o.deps = nd
            for d in nd:
                d.sig = True
        with ExitStack() as st:
            csem = {}
            for e in ['pe', 'act', 'dve', 'pool']:
                csem[e] = [st.enter_context(nc.semaphore(f"c_{e}{i}")) for i in range(N_EPOCH)]
            dsem = [st.enter_context(nc.semaphore(f"d{i}")) for i in range(N_DMA_SEMS)]
            for e in ['pe', 'act', 'dve', 'pool']:
                k = 0
                for o in self.ops[e]:
                    if o.dma:
                        continue
                    if o.sig:
                        o.sem = ('c', e, k // SEM_LIM)
                        o.val = k % SEM_LIM + 1
                        k += 1
                assert k < SEM_LIM * N_EPOCH, (e, k)

            def semh(s):
                return dsem[s[1]] if s[0] == 'dma' else csem[s[1]][s[2]]

            block = st.enter_context(nc.Block())
            fw = list(final_wait)

            def run(eng_name, h):
                waited = {}
                for o in self.ops[eng_name]:
                    need = {}
                    deps = list(o.deps)
                    if o.dma and o.prev_dma is not None:
                        deps.append(o.prev_dma)
                    for d in deps:
                        if d.sem is None:
                            continue
                        if need.get(d.sem, 0) < d.val:
                            need[d.sem] = d.val
                    for s, v in need.items():
                        if waited.get(s, 0) < v:
                            h.wait_ge(semh(s), v)
                            waited[s] = v
                    ins = o.fn(h)
                    if o.dma:
                        ins.then_inc(semh(o.sem), 16)
                    elif o.sig:
                        ins.then_inc(semh(o.sem), 1)
                if eng_name == 'sp':
                    for o in fw:
                        if waited.get(o.sem, 0) < o.val:
                            h.wait_ge(semh(o.sem), o.val)
                            waited[o.sem] = o.val

            @block.sync
            def _(h):
                run('sp', h)

            @block.tensor
            def _(h):
                run('pe', h)

            @block.scalar
            def _(h):
                run('act', h)

            @block.vector
            def _(h):
                run('dve', h)

            @block.gpsimd
            def _(h):
                run('pool', h)


A_Q, A_K, A_V, A_G = 0, 256, 512, 1024
B_Q, B_K, B_V, B_G = 1536, 2048, 2560, 3072
C_X, C_G = 3584, 4096
D_Q, D_FF, D_FB, D_I, D_G = 4608, 5120, 5632, 6144, 6656

R_CW, R_CB, R_BA, R_BX, R_LAM, R_LB, R_GN = 0, 4, 5, 7, 9, 11, 15
NR = 16
Q_C1, Q_C2, Q_LB, Q_OML = 16, 18, 20, 22
NQ = 24


class KB:
    def __init__(self, nc, nseq, nlayers, mask):
        self.nc = nc
        self.P = Prog(nc)
        self.nseq = nseq
        self.nlayers = nlayers
        self.mask = mask
        self.st = ExitStack()
        self.bank_rr = 0
        self.debug = False
        self.bank_set = list(range(8))
        self.dbg_outs = []

    def sb(self, name, shape, dt):
        return self.st.enter_context(self.nc.sbuf_tensor(name, shape, dt))

    def carve(self, shape, dt):
        n = int(np.prod(shape[1:]))
        nbytes = n * (4 if dt == F32 else 2)
        nbytes = (nbytes + 63) // 64 * 64
        w0 = self.aoff // 4
        assert self.aoff + nbytes <= self.arena_bytes, (self.aoff, nbytes, self.arena_bytes)
        self.aoff += nbytes
        ap = self.arena[0:shape[0], w0:w0 + nbytes // 4]
        if dt != F32:
            ap = ap.bitcast(dt)
        ap = ap[:, 0:n]
        if len(shape) == 3:
            ap = ap.rearrange("p (a b) -> p a b", a=shape[1])
        elif len(shape) == 4:
            ap = ap.rearrange("p (a b c) -> p a b c", a=shape[1], b=shape[2])
        return ap

    def phase(self):
        self.P.barrier()
        self.aoff = 0

    def dump(self, name, ap, bufs, dt=F32):
        if not getattr(self, 'debug', False):
            return
        shape = list(ap.shape)
        d = self.nc.dram_tensor("dbg_" + name, shape, dt, kind="ExternalOutput").ap()
        self.dbg_outs.append(self.dma(d, ap, r=bufs))

    def bank(self):
        bs = self.bank_set
        self.bank_rr = (self.bank_rr + 1) % len(bs)
        i = bs[self.bank_rr]
        return self.psum[i], self.psum_bf[i], self.pbuf[i]

    def bankx(self, i):
        return self.psum[i], self.psum_bf[i], self.pbuf[i]

    def declare(self):
        nc = self.nc
        ns = self.nseq

        def din(name, shape):
            return nc.dram_tensor(name, list(shape), F32, kind="ExternalInput").ap()

        self.x = din("x", (ns, T, DM))
        self.p = din("p", (NL, ns, T, PLE))
        self.norm_mix = din("norm_mix", (NL, DM))
        self.w_in = din("w_in", (NL, DM, W_IN))
        self.ret_logit = din("ret_decay_logit", (NL, 2, 4))
        self.rpb = din("nat_rpb", (NL, 8, 15, 31))
        self.conv_w = din("lru_conv_w", (NL, 4, 512))
        self.conv_b = din("lru_conv_b", (NL, 512))
        self.wa = din("lru_wa", (NL, 2, 8, 64, 64))
        self.ba = din("lru_ba", (NL, 2, 512))
        self.wx = din("lru_wx", (NL, 2, 8, 64, 64))
        self.bx = din("lru_bx", (NL, 2, 512))
        self.lam = din("lru_lambda", (NL, 2, 512))
        self.lbl = din("hgrn_lb_logits", (NL, 2, 512))
        self.gn = din("hgrn_norm", (NL, 512))
        self.w_br = din("w_branch", (NL, 4, 512, DM))
        self.w_mg = din("w_merge", (NL, 4, DM, DM))
        self.w_o = din("w_out", (NL, DM, DM))
        self.ple_norm = din("ple_norm", (NL, DM))
        self.w_pg = din("w_ple_gate", (NL, DM, DM))
        self.w_pp = din("w_ple_proj", (NL, PLE, DM))
        self.final_norm = din("final_norm", (DM,))
        self.c_ident = din("c_ident", (128, 128))
        self.c_maskF = din("c_maskF", (128, 128))
        self.c_cos = din("c_cos", (128, T))
        self.c_sin = din("c_sin", (128, T))
        self.c_ret = din("c_ret", (6, 128, 128))
        self.c_maskB = din("c_maskB", (128, 128))
        self.y = nc.dram_tensor("y", [ns, T, DM], F32, kind="ExternalOutput").ap()
        self.xs = nc.dram_tensor("xs", [ns, T, DM], F32, kind="Internal").ap()
        self.ebs = nc.dram_tensor("ebs", [NL, 128, 8, 14, 64], F32, kind="Internal").ap()
        self.B_ebs = Buf('ebs')

    def alloc(self):
        nc = self.nc
        self.ident_f = self.sb("ident_f", [128, 128], F32)
        self.ident_b = self.sb("ident_b", [128, 128], BF16)
        self.ones_f = self.sb("ones_f", [128, 128], F32)
        self.ones_b = self.sb("ones_b", [128, 128], BF16)
        self.prm = self.sb("prm", [128, NL, 4, NQ], F32)
        self.hT = self.sb("hT", [128, 8, T], BF16)
        self.merged = self.sb("merged", [128, 8, T], F32)
        self.brT = self.sb("brT", [128, 4, T], BF16)
        self.B_hT = Buf('hT')
        self.B_merged = [[Buf() for _ in range(NTQ)] for _ in range(8)]
        self.B_brT = [Buf() for _ in range(4)]
        self.B_const = Buf('const')
        self.B_prm = Buf('prm')
        used = 128 * 4 * 2 + 128 * 2 * 2 + NL * 4 * NQ * 4 + 8 * T * 2 + 8 * T * 4 + 4 * T * 2
        self.arena_bytes = (207 * 1024 - used) // 64 * 64
        self.arena = self.sb("arena", [128, self.arena_bytes // 4], F32)
        self.aoff = 0
        self.psum = []
        self.psum_bf = []
        self.pbuf = []
        for i in range(8):
            t = self.st.enter_context(nc.psum_tensor(f"ps{i}", [128, 512], F32))
            self.psum.append(t)
            self.psum_bf.append(t[:].bitcast(BF16))
            self.pbuf.append(Buf(f'ps{i}'))

    def mm(self, out, lhsT, rhs, start, stop, r, wb, **kw):
        d = {'w': [wb]} if start else {'wp': [wb]}
        return self.P.pe(lambda h: h.matmul(out, lhsT=lhsT, rhs=rhs, start=start, stop=stop, **kw), r=r, **d)

    def mm_group(self, out_ap, lhs_list, rhs_list, r, wb):
        n = len(lhs_list)
        for k in range(n):
            self.mm(out_ap, lhs_list[k], rhs_list[k], k == 0, k == n - 1, r, wb)

    def tr(self, out, in_, ident, r, wb, first=True):
        d = {'w': [wb]} if first else {'wp': [wb]}
        return self.P.pe(lambda h: h.transpose(out=out, in_=in_, identity=ident), r=r, **d)

    def actf(self, out, in_, func, r, w=(), wp=(), **kw):
        return self.P.act(lambda h: h.activation(out=out, in_=in_, func=func, **kw), r=r, w=w, wp=wp)

    def tt(self, out, in0, in1, op, r, w=(), wp=(), eng='dve'):
        return self.P.add(eng, lambda h: h.tensor_tensor(out=out, in0=in0, in1=in1, op=op), r=r, w=w, wp=wp)

    def ts(self, out, in0, s1, s2, op0, op1=None, r=(), w=(), wp=(), eng='dve'):
        if op1 is None:
            return self.P.add(eng, lambda h: h.tensor_scalar(out=out, in0=in0, scalar1=s1, scalar2=None, op0=op0), r=r, w=w, wp=wp)
        return self.P.add(eng, lambda h: h.tensor_scalar(out=out, in0=in0, scalar1=s1, scalar2=s2, op0=op0, op1=op1), r=r, w=w, wp=wp)

    def stt(self, out, in0, scalar, in1, op0, op1, r, w=(), wp=()):
        return self.P.dve(lambda h: h.scalar_tensor_tensor(out=out, in0=in0, scalar=scalar, in1=in1, op0=op0, op1=op1), r=r, w=w, wp=wp)

    def cp(self, out, in_, r, w=(), wp=(), eng='dve'):
        if eng == 'act':
            return self.P.act(lambda h: h.copy(out=out, in_=in_), r=r, w=w, wp=wp)
        return self.P.add(eng, lambda h: h.tensor_copy(out=out, in_=in_), r=r, w=w, wp=wp)

    def mset(self, ap, val, w=(), wp=(), eng='dve'):
        return self.P.add(eng, lambda h: h.memset(ap, val), w=w, wp=wp)

    def recip(self, out, in_, r, w=(), wp=()):
        return self.P.dve(lambda h: h.reciprocal(out=out, in_=in_), r=r, w=w, wp=wp)

    def scan(self, out, d0, d1, r, w=(), wp=()):
        return self.P.dve(lambda h: h.tensor_tensor_scan(out=out, data0=d0, data1=d1, initial=0.0, op0=ALU.mult, op1=ALU.add), r=r, w=w, wp=wp)

    def dma(self, out, in_, r=(), w=(), wp=(), eng='sp'):
        return self.P.dma(lambda h: h.dma_start(out=out, in_=in_), eng=eng, r=r, w=w, wp=wp)

    def mk_wpool(self, n, words):
        return {'bufs': [(self.carve([128, words], BF16), Buf()) for _ in range(n)], 'rr': 0}

    def wload(self, src, nkc, ncols=128, eng='pool', pool=None):
        if pool is None:
            pool = self.wpool
        i = pool['rr']
        pool['rr'] = (i + 1) % len(pool['bufs'])
        t, b = pool['bufs'][i]
        v = t[:, 0:nkc * ncols].rearrange("p (k c) -> p k c", k=nkc)
        self.dma(v, src.rearrange("(k p) c -> p k c", p=128), w=[b], eng=eng)
        return v, b

    def setup_wbufs(self, n, words):
        self.wpool = self.mk_wpool(n, words)

    def phase0(self):
        self.phase()
        self.dma(self.ident_f[:], self.c_ident[:, :], w=[self.B_const])
        self.dma(self.ident_b[:], self.c_ident[:, :], wp=[self.B_const], eng='pool')
        self.mset(self.ones_f[:], 1.0, wp=[self.B_const])
        self.mset(self.ones_b[:], 1.0, wp=[self.B_const])
        stg = self.carve([NR, 512], F32)
        Bs = Buf()
        for l in range(NL):
            rows = [(R_CW, self.conv_w[l], 4), (R_CB, self.conv_b[l:l + 1], 1), (R_BA, self.ba[l], 2),
                    (R_BX, self.bx[l], 2), (R_LAM, self.lam[l], 2), (R_LB, self.lbl[0], 2),
                    (R_LB + 2, self.lbl[1], 2), (R_GN, self.gn[l:l + 1], 1)]
            first = True
            for r0, src, n in rows:
                if first:
                    self.dma(stg[r0:r0 + n, :], src, w=[Bs])
                else:
                    self.dma(stg[r0:r0 + n, :], src, wp=[Bs])
                first = False
            for fc in range(4):
                ps, _, pb = self.bank()
                self.tr(ps[:, 0:NR], stg[0:NR, fc * 128:(fc + 1) * 128], self.ident_f[0:NR, 0:NR], [Bs, self.B_const], pb)
                self.cp(self.prm[:, l, fc, 0:NR], ps[:, 0:NR], r=[pb], wp=[self.B_prm])
        tmp = self.carve([128, NL, 4, 2], F32)
        tmp2 = self.carve([128, 4, 2], F32)
        Bt = Buf()
        self.softplus_neg(self.prm[:, :, :, R_LAM:R_LAM + 2], tmp, [128, NL, 4, 2], self.B_prm, Bt)
        self.ts(self.prm[:, :, :, Q_C1:Q_C1 + 2], tmp, -8.0, None, ALU.mult, r=[Bt], wp=[self.B_prm])
        self.ts(self.prm[:, :, :, Q_C2:Q_C2 + 2], tmp, -16.0, None, ALU.mult, r=[Bt], wp=[self.B_prm])
        self.mset(self.prm[:, 0, :, Q_LB:Q_LB + 2], 0.0, wp=[self.B_prm])
        self.mset(self.prm[:, 0, :, Q_OML:Q_OML + 2], 1.0, wp=[self.B_prm])
        if NL > 1:
            B2 = Buf()
            self.tt(tmp2, self.prm[:, 1, :, R_LB + 2:R_LB + 4], self.prm[:, 1, :, R_LB:R_LB + 2], ALU.subtract,
                    r=[self.B_prm], w=[B2])
            self.actf(self.prm[:, 1, :, Q_LB:Q_LB + 2], tmp2, AF.Sigmoid, r=[B2], wp=[self.B_prm])
            self.actf(self.prm[:, 1, :, Q_OML:Q_OML + 2], tmp2, AF.Sigmoid, r=[B2], wp=[self.B_prm], scale=-1.0)

    def build_nat_tables(self):
        self.phase()
        BT = self.carve([64, 120, 64], F32)
        ebt = [(self.carve([128, 8, 64], F32), Buf()) for _ in range(2)]
        BBT = Buf()
        first_out = True
        for l in range(NL):
            self.mset(BT, -30000.0, w=[BBT])
            src2 = self.rpb[l].rearrange("h r m -> (h r) m")
            for qc in range(64):
                ws = min(max(qc - 8, 0), 48)
                a = 15 - qc + ws
                s_ap = src2[:, a:a + 16]
                s_ap = bass.AP(s_ap.tensor, s_ap.offset, [[0, 1]] + [list(x) for x in s_ap.ap])
                self.dma(BT[qc:qc + 1, :, ws:ws + 16], s_ap, wp=[BBT])
            gi = 0
            for h in range(8):
                for p0 in (0, 8):
                    npair = 8 if p0 == 0 else 6
                    ps, _, pb = self.bank()
                    for pi in range(npair):
                        p = p0 + pi
                        in_ = BT[:, h * 15 + p:h * 15 + p + 2, :].rearrange("q a k -> q (a k)")
                        self.tr(ps[:, pi * 64:(pi + 1) * 64], in_, self.ident_f[0:64, 0:64], [BBT, self.B_const], pb, first=(pi == 0))
                    et, Bet = ebt[gi % 2]
                    gi += 1
                    self.actf(et[:, 0:npair, :], ps[:, 0:npair * 64].rearrange("p (a b) -> p a b", a=npair), AF.Exp, r=[pb], w=[Bet])
                    kw = {'w': [self.B_ebs]} if first_out else {'wp': [self.B_ebs]}
                    first_out = False
                    self.dma(self.ebs[l, :, h, p0:p0 + npair, :], et[:, 0:npair, :], r=[Bet], **kw)

    def softplus_neg(self, x_ap, out_ap, shape, Bx, Bo):
        e = self.carve(shape, F32)
        L = self.carve(shape, F32)
        u = self.carve(shape, F32)
        u2 = self.carve(shape, F32)
        q = self.carve(shape, F32)
        Be, BL, Bu, Bu2, Bq = Buf(), Buf(), Buf(), Buf(), Buf()
        self.actf(e, x_ap, AF.Exp, r=[Bx], w=[Be], scale=-1.0)
        self.actf(L, e, AF.Ln, r=[Be], w=[BL], bias=1.0)
        self.ts(u, e, 2.0, None, ALU.add, r=[Be], w=[Bu])
        self.recip(u, u, r=[Bu], w=[Bu])
        self.tt(u, u, e, ALU.mult, r=[Bu, Be], w=[Bu])
        self.tt(u2, u, u, ALU.mult, r=[Bu], w=[Bu2])
        self.ts(q, u2, 1.0 / 9, 1.0 / 7, ALU.mult, ALU.add, r=[Bu2], w=[Bq])
        for c in (1.0 / 5, 1.0 / 3, 1.0):
            self.tt(q, q, u2, ALU.mult, r=[Bq, Bu2], w=[Bq])
            self.ts(q, q, c, None, ALU.add, r=[Bq], w=[Bq])
        self.tt(q, q, u, ALU.mult, r=[Bq, Bu], w=[Bq])
        self.ts(q, q, 2.0, None, ALU.mult, r=[Bq], w=[Bq])
        self.tt(q, q, L, ALU.subtract, r=[Bq, BL], w=[Bq])
        self.ts(u2, e, 0.3, None, ALU.is_lt, r=[Be], w=[Bu2])
        self.tt(q, q, u2, ALU.mult, r=[Bq, Bu2], w=[Bq])
        self.tt(out_ap, q, L, ALU.add, r=[Bq, BL], w=[Bo])

    def rms_stats(self, xt, Bx, junk, Bj, ss, rs, Bss, Brs):
        self.actf(junk, xt, AF.Square, r=[Bx], w=[Bj, Bss], accum_out=ss)
        self.actf(rs, ss, AF.Sqrt, r=[Bss], w=[Brs], scale=1.0 / DM, bias=EPS)
        self.recip(rs, rs, r=[Brs], w=[Brs])

    def phase1(self, s, l):
        self.phase()
        xsrc = self.x[s] if l == 0 else self.xs[s]
        gbc = self.carve([128, DM], F32)
        Bg = Buf()
        self.dma(gbc, self.norm_mix[l:l + 1, :].broadcast_to([128, DM]), w=[Bg])
        junk = self.carve([128, DM], BF16)
        Bj = Buf()
        xts = [(self.carve([128, DM], F32), Buf()) for _ in range(3)]
        hns = [(self.carve([128, DM], BF16), Buf()) for _ in range(2)]
        sts = [(self.carve([128, 1], F32), self.carve([128, 1], F32), Buf(), Buf()) for _ in range(2)]
        for tt in range(NTT):
            xt, Bx = xts[tt % 3]
            hn, Bh = hns[tt % 2]
            ss, rs, Bss, Brs = sts[tt % 2]
            self.dma(xt, xsrc[tt * 128:(tt + 1) * 128, :], w=[Bx])
            self.rms_stats(xt, Bx, junk, Bj, ss, rs, Bss, Brs)
            self.stt(hn, xt, rs, gbc, ALU.mult, ALU.mult, r=[Bx, Brs, Bg], w=[Bh])
            ps, psb, pb = self.bank()
            for kc in range(8):
                self.tr(psb[:, kc * 128:(kc + 1) * 128], hn[:, kc * 128:(kc + 1) * 128], self.ident_b[:], [Bh, self.B_const], pb, first=(kc == 0))
            kw = {'w': [self.B_hT]} if tt == 0 else {'wp': [self.B_hT]}
            self.cp(self.hT[:, :, tt * 128:(tt + 1) * 128], psb.rearrange("p (k c) -> p k c", k=8), r=[pb], eng='act', **kw)
        self.dump("hT", self.hT[:], [self.B_hT], BF16)

    def gen_merge(self, l, j, first, src=None, Bsrc=None, nw=4, nt=2, res=None):
        if src is None:
            src, Bsrc = self.brT, self.B_brT
        if res is None:
            res = self.merge_res(nw, nt)
        wp, sgs, tmps = res
        i = 0
        for dmc in range(8):
            wg, Bwg = self.wload(self.w_mg[l, j][:, dmc * 128:(dmc + 1) * 128], 8, pool=wp)
            wb, Bwb = self.wload(self.w_br[l, j][:, dmc * 128:(dmc + 1) * 128], 4, pool=wp)
            for tq in range(NTQ):
                sl = slice(tq * 512, (tq + 1) * 512)
                psg, _, pbg = self.bank()
                psp, _, pbp = self.bank()
                self.mm_group(psg[:, :], [wg[:, k, :] for k in range(8)], [self.hT[:, k, sl] for k in range(8)], [Bwg, self.B_hT], pbg)
                self.mm_group(psp[:, :], [wb[:, k, :] for k in range(4)], [src[:, k, sl] for k in range(4)], [Bwb] + list(Bsrc), pbp)
                sg, Bsg = sgs[i % len(sgs)]
                tmp, Btmp = tmps[i % len(tmps)]
                i += 1
                self.actf(sg, psg[:, :], AF.Sigmoid, r=[pbg], w=[Bsg])
                Bm = self.B_merged[dmc][tq]
                if first:
                    self.tt(self.merged[:, dmc, sl], psp[:, :], sg, ALU.mult, r=[pbp, Bsg], w=[Bm])
                else:
                    self.tt(tmp, psp[:, :], sg, ALU.mult, r=[pbp, Bsg], w=[Btmp])
                    self.tt(self.merged[:, dmc, sl], self.merged[:, dmc, sl], tmp, ALU.add, r=[Bm, Btmp], w=[Bm], eng='pool')
                yield
        self.dump(f"merged{j}", self.merged[:], [b for row in self.B_merged for b in row])

    def merge_res(self, nw, nt):
        return (self.mk_wpool(nw, 1024), [(self.carve([128, 512], F32), Buf()) for _ in range(nt)],
                [(self.carve([128, 512], F32), Buf()) for _ in range(nt)])

    def phase_merge(self, l, j, first):
        self.phase()
        for _ in self.gen_merge(l, j, first):
            pass

    def phase3(self, s, l, have_merged):
        self.phase()
        last = (l == self.nlayers - 1)
        xsrc = self.x[s] if l == 0 else self.xs[s]
        wo = self.carve([128, 8, DM], BF16)
        wpg = self.carve([128, 8, DM], BF16)
        wpp = self.carve([128, 2, DM], BF16)
        Bwo, Bwpg, Bwpp = Buf(), Buf(), Buf()
        if have_merged:
            self.dma(wo, self.w_o[l].rearrange("(k p) c -> p k c", p=128), w=[Bwo], eng='pool')
        self.dma(wpg, self.w_pg[l].rearrange("(k p) c -> p k c", p=128), w=[Bwpg], eng='pool')
        self.dma(wpp, self.w_pp[l].rearrange("(k p) c -> p k c", p=128), w=[Bwpp], eng='pool')
        gple = self.carve([128, DM], F32)
        gfin = self.carve([128, DM], F32)
        Bg = Buf()
        self.dma(gple, self.ple_norm[l:l + 1, :].broadcast_to([128, DM]), w=[Bg])
        self.dma(gfin, self.final_norm.rearrange("(a c) -> a c", a=1).broadcast_to([128, DM]), wp=[Bg])
        NB = 3
        xts = [(self.carve([128, DM], F32), Buf()) for _ in range(NB)]
        mbs = [(self.carve([128, 8, 128], BF16), Buf()) for _ in range(NB)]
        n2s = [(self.carve([128, DM], BF16), Buf()) for _ in range(NB)]
        n2Ts = [(self.carve([128, 8, 128], BF16), Buf()) for _ in range(NB)]
        pts = [(self.carve([128, PLE], F32), Buf()) for _ in range(NB)]
        pbs = [(self.carve([128, PLE], BF16), Buf()) for _ in range(NB)]
        pTs = [(self.carve([128, 2, 128], BF16), Buf()) for _ in range(NB)]
        gts = [(self.carve([128, DM], F32), Buf()) for _ in range(NB)]
        sts = [[(self.carve([128, 1], F32), self.carve([128, 1], F32), Buf(), Buf()) for _ in range(2)] for _ in range(NB)]
        outs = []

        def s1(tt):
            b = tt % NB
            tsl = slice(tt * 128, (tt + 1) * 128)
            xt, Bx = xts[b]
            self.dma(xt, xsrc[tsl, :], w=[Bx])
            if have_merged:
                mb, Bmb = mbs[b]
                self.cp(mb, self.merged[:, :, tsl], r=[self.B_merged[d][tt // 4] for d in range(8)], w=[Bmb], eng='dve')
                for hf in range(2):
                    ps, _, pb = self.bank()
                    self.mm_group(ps[:, :], [mb[:, k, :] for k in range(8)], [wo[:, k, hf * 512:(hf + 1) * 512] for k in range(8)], [Bmb, Bwo], pb)
                    self.tt(xt[:, hf * 512:(hf + 1) * 512], xt[:, hf * 512:(hf + 1) * 512], ps[:, :], ALU.add, r=[Bx, pb], w=[Bx])
            ss, rs, Bss, Brs = sts[b][0]
            n2, Bn2 = n2s[b]
            self.rms_stats(xt, Bx, n2, Bn2, ss, rs, Bss, Brs)
            self.stt(n2, xt, rs, gple, ALU.mult, ALU.mult, r=[Bx, Brs, Bg], w=[Bn2])
            ps, psb, pb = self.bank()
            for kc in range(8):
                self.tr(psb[:, kc * 128:(kc + 1) * 128], n2[:, kc * 128:(kc + 1) * 128], self.ident_b[:], [Bn2, self.B_const], pb, first=(kc == 0))
            n2T, Bn2T = n2Ts[b]
            self.cp(n2T, psb.rearrange("p (k c) -> p k c", k=8), r=[pb], w=[Bn2T], eng='act')
            pt, Bpt = pts[b]
            pbf, Bpbf = pbs[b]
            pT, BpT = pTs[b]
            self.dma(pt, self.p[l, s, tsl, :], w=[Bpt])
            self.cp(pbf, pt, r=[Bpt], w=[Bpbf], eng='pool')
            ps2, psb2, pb2 = self.bank()
            for c in range(2):
                self.tr(psb2[:, c * 128:(c + 1) * 128], pbf[:, c * 128:(c + 1) * 128], self.ident_b[:], [Bpbf, self.B_const], pb2, first=(c == 0))
            self.cp(pT, psb2[:, 0:256].rearrange("p (k c) -> p k c", k=2), r=[pb2], w=[BpT], eng='act')
            return (tt, b, tsl, xt, Bx, n2T, Bn2T, pT, BpT)

        def s2(ctx):
            tt, b, tsl, xt, Bx, n2T, Bn2T, pT, BpT = ctx
            gt, Bgt = gts[b]
            for hf in range(2):
                hs = slice(hf * 512, (hf + 1) * 512)
                psg, _, pbg = self.bank()
                psp, _, pbp = self.bank()
                self.mm_group(psg[:, :], [n2T[:, k, :] for k in range(8)], [wpg[:, k, hs] for k in range(8)], [Bn2T, Bwpg], pbg)
                self.mm_group(psp[:, :], [pT[:, k, :] for k in range(2)], [wpp[:, k, hs] for k in range(2)], [BpT, Bwpp], pbp)
                self.actf(gt[:, hs], psg[:, :], AF.Sigmoid, r=[pbg], w=[Bgt] if hf == 0 else (), wp=() if hf == 0 else [Bgt])
                self.tt(gt[:, hs], gt[:, hs], psp[:, :], ALU.mult, r=[Bgt, pbp], w=[Bgt])
            self.tt(xt, xt, gt, ALU.add, r=[Bx, Bgt], w=[Bx])
            if last:
                ss, rs, Bss, Brs = sts[b][1]
                n2, Bn2 = n2s[b]
                self.rms_stats(xt, Bx, n2, Bn2, ss, rs, Bss, Brs)
                self.stt(xt, xt, rs, gfin, ALU.mult, ALU.mult, r=[Bx, Brs, Bg], w=[Bx])
                outs.append(self.dma(self.y[s, tsl, :], xt, r=[Bx]))
            else:
                outs.append(self.dma(self.xs[s, tsl, :], xt, r=[Bx]))

        prev = None
        for tt in range(NTT):
            ctx = s1(tt)
            if prev is not None:
                s2(prev)
            prev = ctx
        s2(prev)
        return outs

    def branch_C(self, l):
        self.phase()
        for _ in self.gen_C(l):
            pass

    def gen_C(self, l):
        self.setup_wbufs(3, 1024)
        X = self.carve([128, T + 4], F32)
        XC, R, I, A, S, HF = [self.carve([128, T], F32) for _ in range(6)]
        BX, BXC, BR, BI, BA, BS, BHF = [Buf() for _ in range(7)]
        Wd = [[(self.carve([128, 128], F32), Buf()) for _ in range(2)] for _ in range(2)]
        pr = self.prm
        for fc in range(4):
            for d in range(2):
                for g, wsrc in enumerate((self.wa, self.wx)):
                    wt, Bw_ = Wd[d][g]
                    self.mset(wt, 0.0, w=[Bw_])
                    self.dma(wt[0:64, 0:64], wsrc[l, d, 2 * fc], wp=[Bw_])
                    self.dma(wt[64:128, 64:128], wsrc[l, d, 2 * fc + 1], wp=[Bw_])
            w, Bw = self.wload(self.w_in[l][:, C_X + fc * 128:C_X + (fc + 1) * 128], 8)
            self.mset(X[:, 0:2], 0.0, w=[BX])
            self.mset(X[:, T + 2:T + 4], 0.0, wp=[BX])
            for tq in range(NTQ):
                sl = slice(tq * 512, (tq + 1) * 512)
                ps, _, pb = self.bank()
                self.mm_group(ps[:, :], [w[:, k, :] for k in range(8)], [self.hT[:, k, sl] for k in range(8)], [Bw, self.B_hT], pb)
                self.cp(X[:, 2 + tq * 512:2 + (tq + 1) * 512], ps[:, :], r=[pb], wp=[BX], eng='act')
            yield
            self.ts(XC, X[:, 0:T], pr[:, l, fc, R_CW:R_CW + 1], pr[:, l, fc, R_CB:R_CB + 1], ALU.mult, ALU.add, r=[BX, self.B_prm], w=[BXC])
            for j in range(1, 4):
                self.stt(XC, X[:, j:T + j], pr[:, l, fc, R_CW + j:R_CW + j + 1], XC, ALU.mult, ALU.add, r=[BX, BXC, self.B_prm], w=[BXC])
            yield
            for d in range(2):
                for g, (dst, Bd, rb) in enumerate(((R, BR, R_BA), (I, BI, R_BX))):
                    wt, Bw_ = Wd[d][g]
                    for tq in range(NTQ):
                        sl = slice(tq * 512, (tq + 1) * 512)
                        ps, _, pb = self.bank()
                        self.mm(ps[:, :], wt, XC[:, sl], True, True, [Bw_, BXC], pb)
                        kw = {'w': [Bd]} if tq == 0 else {'wp': [Bd]}
                        self.actf(dst[:, sl], ps[:, :], AF.Sigmoid, r=[pb, self.B_prm], bias=pr[:, l, fc, rb + d:rb + d + 1], **kw)
                yield
                self.actf(A, R, AF.Exp, r=[BR, self.B_prm], w=[BA], scale=pr[:, l, fc, Q_C1 + d:Q_C1 + d + 1])
                self.actf(S, R, AF.Exp, r=[BR, self.B_prm], w=[BS], scale=pr[:, l, fc, Q_C2 + d:Q_C2 + d + 1])
                self.actf(S, S, AF.Sqrt, r=[BS], w=[BS], scale=-1.0, bias=1.0)
                self.tt(I, I, XC, ALU.mult, r=[BI, BXC], w=[BI])
                self.tt(I, I, S, ALU.mult, r=[BI, BS], w=[BI])
                yield
                if d == 0:
                    self.scan(HF, A, I, r=[BA, BI], w=[BHF])
                else:
                    self.scan(S[:, ::-1], A[:, ::-1], I[:, ::-1], r=[BA, BI], w=[BS])
                    self.tt(HF, HF, S, ALU.add, r=[BHF, BS], w=[BHF])
            yield
            if fc == 0:
                assert getattr(self, 'brT_free', True), "pending merge still reads brT"
            if fc == 3:
                self.dump("C_R", R, [BR])
                self.dump("C_B", I, [BI])
                self.dump("prm", self.prm[:], [self.B_prm])
            w, Bw = self.wload(self.w_in[l][:, C_G + fc * 128:C_G + (fc + 1) * 128], 8)
            for tq in range(NTQ):
                sl = slice(tq * 512, (tq + 1) * 512)
                ps, _, pb = self.bank()
                self.mm_group(ps[:, :], [w[:, k, :] for k in range(8)], [self.hT[:, k, sl] for k in range(8)], [Bw, self.B_hT], pb)
                kw = {'w': [BR]} if tq == 0 else {'wp': [BR]}
                self.actf(R[:, sl], ps[:, :], AF.Silu, r=[pb], **kw)
            self.tt(self.brT[:, fc, :], HF, R, ALU.mult, r=[BHF, BR], w=[self.B_brT[fc]])
            if fc == 3:
                self.dump("C_XC", XC, [BXC])
                self.dump("C_HF", HF, [BHF])
                self.dump("C_A", A, [BA])
                self.dump("C_X", X, [BX])
        self.dump("C_brT", self.brT[:], self.B_brT, BF16)

    def branch_D(self, l):
        self.phase()
        self.bank_set = [4, 5, 6, 7]
        self.setup_wbufs(3, 1024)
        pr = self.prm
        maskF = self.carve([128, 128], F32)
        maskB = self.carve([128, 128], F32)
        M0 = self.carve([128, 512], BF16)
        M1 = self.carve([128, 512], BF16)
        Bk = Buf()
        self.dma(maskF, self.c_maskF[:, :], w=[Bk])
        self.dma(maskB, self.c_maskB[:, :], wp=[Bk])
        BM0, BM1 = Buf(), Buf()
        self.mset(M0, 1.0, w=[BM0])
        self.mset(M0.rearrange("p (c j) -> p c j", j=32)[:, :, 0:1], 0.0, w=[BM0])
        self.mset(M1, 1.0, w=[BM1])
        self.mset(M1.rearrange("p (c j) -> p c j", j=32)[:, :, 31:32], 0.0, w=[BM1])
        Q, E, G, Bc, O = [self.carve([128, T], F32) for _ in range(5)]
        qt, kh = [self.carve([128, T], BF16) for _ in range(2)]
        khtok = self.carve([128, NTT, 128], BF16)
        vtok = self.carve([128, NTT, 128], BF16)
        Sall = self.carve([128, 65, 128], BF16)
        Sm2 = [self.carve([128, 128], F32) for _ in range(2)]
        bl = self.carve([128, 64], F32)
        ac = self.carve([128, 64], F32)
        attms = [(self.carve([128, 512], BF16), Buf()) for _ in range(2)]
        BQ, BE, BG, BBc, BO, Bqt, Bkh, Bkhtok, Bvtok, BSall, BSm, Bbl, Bac = [Buf() for _ in range(13)]
        BSm2 = [Buf(), Buf()]
        BE2, BG2, BBc2, Bkh2, Bqt2, Bkhtok2, Bbl2, Bac2 = [[Buf(), Buf()] for _ in range(8)]
        bl_bc = bass.AP(bl.tensor, bl.offset, [list(bl.ap[0]), [1, 64], [0, 32]])
        v3 = lambda a: a.rearrange("p (c j) -> p c j", j=32)
        def d_prologue(h):
                cs = slice(h * 128, (h + 1) * 128)
                w, Bw = self.wload(self.w_in[l][:, D_Q + h * 128:D_Q + (h + 1) * 128], 8)
                for tq in range(NTQ):
                    sl = slice(tq * 512, (tq + 1) * 512)
                    ps, _, pb = self.bank()
                    self.mm_group(ps[:, :], [w[:, k, :] for k in range(8)], [self.hT[:, k, sl] for k in range(8)], [Bw, self.B_hT], pb)
                    kw = {'w': [BQ]} if tq == 0 else {'wp': [BQ]}
                    self.actf(Q[:, sl], ps[:, :], AF.Silu, r=[pb], **kw)
                w, Bw = self.wload(self.w_in[l][:, D_I + h * 128:D_I + (h + 1) * 128], 8)
                for g4 in range(4):
                    ps, _, pb = self.bank()
                    for i4 in range(4):
                        tt = g4 * 4 + i4
                        for k in range(8):
                            self.mm(ps[:, i4 * 128:(i4 + 1) * 128], self.hT[:, k, tt * 128:(tt + 1) * 128], w[:, k, :], k == 0, k == 7,
                                    [Bw, self.B_hT], pb) if (i4 == 0 and k == 0) else \
                                self.P.pe(lambda hh, o_=ps[:, i4 * 128:(i4 + 1) * 128], a_=self.hT[:, k, tt * 128:(tt + 1) * 128], b_=w[:, k, :], s_=(k == 0), e_=(k == 7):
                                          hh.matmul(o_, lhsT=a_, rhs=b_, start=s_, stop=e_), r=[Bw, self.B_hT], wp=[pb])
                    kw = {'w': [Bvtok]} if g4 == 0 else {'wp': [Bvtok]}
                    self.cp(vtok[:, g4 * 4:(g4 + 1) * 4, :], ps[:, :].rearrange("p (a b) -> p a b", a=4), r=[pb], eng='act', **kw)

        d_prologue(0)
        for h in range(4):
            cs = slice(h * 128, (h + 1) * 128)
            for d in range(2):
                zoff = (D_FF if d == 0 else D_FB) + h * 128
                w, Bw = self.wload(self.w_in[l][:, zoff:zoff + 128], 8)
                HS = [slice(0, 1024), slice(1024, 2048)]
                for tq in range(NTQ):
                    sl = slice(tq * 512, (tq + 1) * 512)
                    hf = tq // 2
                    ps, _, pb = self.bank()
                    self.mm_group(ps[:, :], [w[:, k, :] for k in range(8)], [self.hT[:, k, sl] for k in range(8)], [Bw, self.B_hT], pb)
                    kw = {'w': [BE2[hf], BE]} if tq % 2 == 0 else {'wp': [BE2[hf]]}
                    self.actf(E[:, sl], ps[:, :], AF.Sigmoid, r=[pb], **kw)
                for hf in range(2):
                    self.ts(E[:, HS[hf]], E[:, HS[hf]], pr[:, l, h, Q_OML + d:Q_OML + d + 1], pr[:, l, h, Q_LB + d:Q_LB + d + 1], ALU.mult, ALU.add,
                            r=[BE2[hf], self.B_prm], w=[BE2[hf]])
                for hf in range(2):
                    self.actf(G[:, HS[hf]], E[:, HS[hf]], AF.Ln, r=[BE2[hf]], w=[BG2[hf], BG])
                for hf in range(2):
                    self.ts(E[:, HS[hf]], E[:, HS[hf]], -1.0, 1.0, ALU.mult, ALU.add, r=[BE2[hf]], w=[BE2[hf]])
                for tq in range(NTQ):
                    sl = slice(tq * 512, (tq + 1) * 512)
                    hf = tq // 2
                    kw = {'w': [BBc2[hf]]} if tq % 2 == 0 else {'wp': [BBc2[hf]]}
                    if d == 0:
                        self.scan(Bc[:, sl], M0, G[:, sl], r=[BM0, BG2[hf]], **kw)
                    else:
                        self.scan(Bc[:, sl][:, ::-1], M1[:, ::-1], G[:, sl][:, ::-1], r=[BM1, BG2[hf]], **kw)
                edge = 31 if d == 0 else 0
                CS = [slice(0, 32), slice(32, 64)]
                for hf in range(2):
                    self.cp(bl[:, CS[hf]], v3(Bc)[:, CS[hf], edge], r=[BBc2[hf]], w=[Bbl2[hf]])
                for hf in range(2):
                    self.actf(ac[:, CS[hf]], bl[:, CS[hf]], AF.Exp, r=[Bbl2[hf]], w=[Bac2[hf]])
                for hf in range(2):
                    blh = bl[:, CS[hf]]
                    blh_bc = bass.AP(blh.tensor, blh.offset, [list(blh.ap[0]), [1, 32], [0, 32]])
                    self.tt(v3(G)[:, CS[hf], :], blh_bc, v3(Bc)[:, CS[hf], :], ALU.subtract, r=[Bbl2[hf], BBc2[hf], BG2[hf]], w=[BG2[hf]])
                for hf in range(2):
                    self.actf(G[:, HS[hf]], G[:, HS[hf]], AF.Exp, r=[BG2[hf]], w=[BG2[hf]])
                for hf in range(2):
                    self.tt(kh[:, HS[hf]], E[:, HS[hf]], G[:, HS[hf]], ALU.mult, r=[BE2[hf], BG2[hf]], w=[Bkh2[hf]])
                for g8 in range(2):
                    ps, psb, pb = self.bank()
                    for i8 in range(8):
                        tt = g8 * 8 + i8
                        self.tr(psb[:, i8 * 128:(i8 + 1) * 128], kh[:, tt * 128:(tt + 1) * 128], self.ident_b[:], [Bkh2[g8], self.B_const], pb, first=(i8 == 0))
                    self.cp(khtok[:, g8 * 8:(g8 + 1) * 8, :], psb.rearrange("p (a b) -> p a b", a=8), r=[pb], eng='act', w=[Bkhtok2[g8]])
                for hf in range(2):
                    self.actf(G[:, HS[hf]], Bc[:, HS[hf]], AF.Exp, r=[BBc2[hf], BG2[hf]], w=[BG2[hf]])
                for hf in range(2):
                    self.tt(qt[:, HS[hf]], Q[:, HS[hf]], G[:, HS[hf]], ALU.mult, r=[BQ, BG2[hf]], w=[Bqt2[hf]])
                for hf in range(2):
                    self.actf(G[:, HS[hf]], Bc[:, HS[hf]], AF.Exp, r=[BBc2[hf], BG2[hf]], w=[BG2[hf]], scale=-1.0)
                for hf in range(2):
                    self.tt(kh[:, HS[hf]], E[:, HS[hf]], G[:, HS[hf]], ALU.mult, r=[BE2[hf], BG2[hf], Bkhtok2[hf]], w=[Bkh2[hf]])
                self.mset(Sm2[0], 0.0, w=[BSm2[0]])
                step = 0
                s0 = 0 if d == 0 else 64
                self.mset(Sall[:, s0, :], 0.0, w=[BSall])
                for rnd in range(4):
                    tiles = [rnd * 4 + i for i in range(4)] if d == 0 else [15 - rnd * 4 - i for i in range(4)]
                    corder = [0, 1, 2, 3] if d == 0 else [3, 2, 1, 0]
                    for bi, tt in enumerate(tiles):
                        for j in corder:
                            psj, _, pbj = self.bankx(j)
                            o_ = psj[:, bi * 128:(bi + 1) * 128]
                            a_ = khtok[32 * j:32 * j + 32, tt, :]
                            b_ = vtok[32 * j:32 * j + 32, tt, :]
                            kw = {'w': [pbj]} if bi == 0 else {'wp': [pbj]}
                            self.P.pe(lambda hh, o_=o_, a_=a_, b_=b_, j=j: hh.matmul(o_, lhsT=a_, rhs=b_, start=True, stop=True, tile_position=(32 * j, 0)),
                                      r=Bkhtok2 + [Bvtok], **kw)
                    for bi, tt in enumerate(tiles):
                        for j in corder:
                            c = tt * 4 + j
                            psj, _, pbj = self.bankx(j)
                            s_src, s_dst = Sm2[step % 2], Sm2[(step + 1) % 2]
                            Bs_src, Bs_dst = BSm2[step % 2], BSm2[(step + 1) % 2]
                            step += 1
                            self.stt(s_dst, s_src, ac[:, c:c + 1], psj[:, bi * 128:(bi + 1) * 128], ALU.mult, ALU.add, r=[Bs_src, pbj] + Bac2, w=[Bs_dst])
                            nxt = c + 1 if d == 0 else c
                            self.cp(Sall[:, nxt, :], s_dst, r=[Bs_dst], wp=[BSall], eng='act')
                mask = maskF if d == 0 else maskB
                mask_bc = bass.AP(mask.tensor, mask.offset, [list(mask.ap[0]), [0, 4], [1, 128]])
                def o1(tq):
                    sl = slice(tq * 512, (tq + 1) * 512)
                    psA, _, pbA = self.bank()
                    for i4 in range(4):
                        tsl = slice(tq * 512 + i4 * 128, tq * 512 + (i4 + 1) * 128)
                        kw = {'w': [pbA]} if i4 == 0 else {'wp': [pbA]}
                        self.P.pe(lambda hh, o_=psA[:, i4 * 128:(i4 + 1) * 128], a_=kh[:, tsl], b_=qt[:, tsl]: hh.matmul(o_, lhsT=a_, rhs=b_, start=True, stop=True),
                                  r=Bkh2 + Bqt2, **kw)
                    attm, Battm = attms[tq % 2]
                    self.tt(attm.rearrange("p (a b) -> p a b", a=4), psA[:, :].rearrange("p (a b) -> p a b", a=4), mask_bc, ALU.mult,
                            r=[pbA, Bk], w=[Battm])
                    return (tq, sl, attm, Battm)

                def o2(ctx):
                    tq, sl, attm, Battm = ctx
                    psO, _, pbO = self.bank()
                    for i4 in range(4):
                        tt = tq * 4 + i4
                        osl = slice(i4 * 128, (i4 + 1) * 128)
                        kw = {'w': [pbO]} if i4 == 0 else {'wp': [pbO]}
                        self.P.pe(lambda hh, o_=psO[:, osl], a_=vtok[:, tt, :], b_=attm[:, osl]: hh.matmul(o_, lhsT=a_, rhs=b_, start=True, stop=False),
                                  r=[Bvtok, Battm], **kw)
                        for j in range(4):
                            c = tt * 4 + j
                            slot = c if d == 0 else c + 1
                            self.P.pe(lambda hh, o_=psO[:, i4 * 128 + j * 32:i4 * 128 + (j + 1) * 32], a_=Sall[:, slot, :], b_=qt[:, c * 32:(c + 1) * 32], e_=(j == 3):
                                      hh.matmul(o_, lhsT=a_, rhs=b_, start=False, stop=e_), r=[BSall] + Bqt2, wp=[pbO])
                    if d == 0:
                        kw = {'w': [BO]} if tq == 0 else {'wp': [BO]}
                        self.cp(O[:, sl], psO[:, :], r=[pbO], eng='act', **kw)
                    else:
                        self.tt(O[:, sl], O[:, sl], psO[:, :], ALU.add, r=[BO, pbO], wp=[BO])
                prev_o = None
                for tq in range(NTQ):
                    ctx_o = o1(tq)
                    if prev_o is not None:
                        o2(prev_o)
                    prev_o = ctx_o
                o2(prev_o)
            if h < 3:
                d_prologue(h + 1)
            self.actf(G, O, AF.Square, r=[BO, BG] + BG2, w=[BG])
            for tq in range(NTQ):
                sl = slice(tq * 512, (tq + 1) * 512)
                ps, _, pb = self.bank()
                self.mm(ps[:, :], self.ones_f[:], G[:, sl], True, True, [BG, self.B_const], pb)
                kw = {'w': [BE]} if tq == 0 else {'wp': [BE]}
                kw_r = BE2
                self.actf(E[:, sl], ps[:, :], AF.Ln, r=[pb] + BE2, scale=1.0 / 128, bias=EPS, **kw)
            self.actf(E, E, AF.Exp, r=[BE], w=[BE], scale=-0.5)
            self.stt(O, O, pr[:, l, h, R_GN:R_GN + 1], E, ALU.mult, ALU.mult, r=[BO, BE, self.B_prm], w=[BO])
            w, Bw = self.wload(self.w_in[l][:, D_G + h * 128:D_G + (h + 1) * 128], 8)
            for tq in range(NTQ):
                sl = slice(tq * 512, (tq + 1) * 512)
                ps, _, pb = self.bank()
                self.mm_group(ps[:, :], [w[:, k, :] for k in range(8)], [self.hT[:, k, sl] for k in range(8)], [Bw, self.B_hT], pb)
                kw = {'w': [BG]} if tq == 0 else {'wp': [BG]}
                self.actf(G[:, sl], ps[:, :], AF.Silu, r=[pb], **kw)
            self.tt(self.brT[:, h, :], O, G, ALU.mult, r=[BO, BG], w=[self.B_brT[h]])
            if h == 3:
                self.dump("D_O", O, [BO])
                self.dump("D_qt", qt, [Bqt], BF16)
                self.dump("D_kh", kh, [Bkh], BF16)
                self.dump("D_Bc", Bc, [BBc])
                self.dump("D_Sall", Sall, [BSall], BF16)
        self.dump("D_brT", self.brT[:], self.B_brT, BF16)
        self.bank_set = list(range(8))

    def branch_A(self, l):
        self.phase()
        self.setup_wbufs(3, 1024)
        COS = self.carve([128, T], F32)
        SIN = self.carve([128, T], F32)
        CT = self.carve([128, 6, 128], F32)
        Bk = Buf()
        self.dma(COS, self.c_cos[:, :], w=[Bk])
        self.dma(SIN, self.c_sin[:, :], wp=[Bk])
        self.dma(CT, self.c_ret.rearrange("a p c -> p a c"), wp=[Bk])
        DF, UF, DB, UB, TQ, TK = [CT[:, i, :] for i in range(6)]
        lgt = self.carve([128, 8], F32)
        lg = self.carve([128, 8], F32)
        lgs = self.carve([128, 4], F32)
        cd = self.carve([128, 4], F32)
        Blg, Blgs = Buf(), Buf()
        self.dma(lgt, self.ret_logit[l].rearrange("d h -> (d h)").rearrange("(a c) -> a c", a=1).broadcast_to([128, 8]), w=[Blg])
        self.softplus_neg(lgt, lg, [128, 8], Blg, Blg)
        self.ts(lg, lg, -1.0, None, ALU.mult, r=[Blg], w=[Blg])
        self.cp(lgs[0:64, :], lg[0:64, 0:4], r=[Blg], w=[Blgs])
        self.cp(lgs[64:128, :], lg[64:128, 4:8], r=[Blg], wp=[Blgs])
        self.actf(cd, lgs, AF.Exp, r=[Blgs], wp=[Blgs], scale=128.0)
        MT = self.carve([128, 128], F32)
        WQ = self.carve([128, 128], F32)
        KW = self.carve([128, 128], F32)
        t1 = self.carve([128, 128], F32)
        BMT, BWQ, BKW, Bt1 = Buf(), Buf(), Buf(), Buf()
        qr, qh, kr, kst = [self.carve([128, T], BF16) for _ in range(4)]
        khtok = self.carve([128, NTT, 128], BF16)
        vtok = self.carve([128, NTT, 128], BF16)
        X = self.carve([128, NTT, 128], F32)
        prev = self.carve([128, NTT, 128], BF16)
        O = self.carve([128, T], F32)
        tmps = [(self.carve([128, 512], F32), Buf()) for _ in range(4)]
        attms = [(self.carve([128, 512], BF16), Buf()) for _ in range(2)]
        Bqr, Bqh, Bkr, Bkst, Bkhtok, Bvtok, BX, Bprev, BO = [Buf() for _ in range(9)]
        wd = self.carve([128, 8, 128], BF16)
        wsw = self.carve([128, 8, 128], BF16)
        Bwd = Buf()
        c3 = lambda a: a.rearrange("p (n c) -> p n c", c=128)
        bc16 = lambda a: bass.AP(a.tensor, a.offset, [list(a.ap[0]), [0, NTT], [1, 128]])
        bc4 = lambda a: bass.AP(a.tensor, a.offset, [list(a.ap[0]), [0, 4], [1, 128]])
        ti = 0
        def a_prologue(h):
                nonlocal ti
                lf = lg[:, h:h + 1]
                lb_ = lg[:, 4 + h:5 + h]
                self.actf(MT, DF, AF.Exp, r=[Bk, Blg], w=[BMT], scale=lf)
                self.tt(MT, MT, UF, ALU.mult, r=[BMT, Bk], w=[BMT])
                self.actf(t1, DB, AF.Exp, r=[Bk, Blg], w=[Bt1], scale=lb_)
                self.tt(t1, t1, UB, ALU.mult, r=[Bt1, Bk], w=[Bt1])
                self.tt(MT, MT, t1, ALU.add, r=[BMT, Bt1], w=[BMT])
                self.ts(MT, MT, 0.125, None, ALU.mult, r=[BMT], w=[BMT])
                self.actf(WQ, TQ, AF.Exp, r=[Bk, Blgs], w=[BWQ], scale=lgs[:, h:h + 1])
                self.actf(KW, TK, AF.Exp, r=[Bk, Blgs], w=[BKW], scale=lgs[:, h:h + 1])
                self.ts(KW, KW, 0.125, None, ALU.mult, r=[BKW], w=[BKW])
                for (c0, dst, Bd) in ((A_Q + h * 64, qr, Bqr), (A_K + h * 64, kr, Bkr)):
                    w0, Bw0 = self.wload(self.w_in[l][:, c0:c0 + 64], 8, ncols=64)
                    for a in range(2):
                        kw = {'w': [Bwd]} if a == 0 else {'wp': [Bwd]}
                        self.cp(wd[:, :, a * 64:(a + 1) * 64], w0, r=[Bw0], eng='pool', **kw)
                        for j2 in range(2):
                            self.cp(wsw[:, :, a * 64 + j2 * 32:a * 64 + (j2 + 1) * 32], w0[:, :, (1 - j2) * 32:(2 - j2) * 32], r=[Bw0], wp=[Bwd], eng='pool')
                    for tq in range(NTQ):
                        sl = slice(tq * 512, (tq + 1) * 512)
                        psn, _, pbn = self.bank()
                        pss, _, pbs = self.bank()
                        lh_n = [wd[:, k, :] for k in range(8)]
                        lh_s = [wsw[:, k, :] for k in range(8)]
                        rh = [self.hT[:, k, sl] for k in range(8)]
                        self.mm_group(psn[:, :], lh_n, rh, [Bwd, self.B_hT], pbn)
                        self.mm_group(pss[:, :], lh_s, rh, [Bwd, self.B_hT], pbs)
                        ta, Bta = tmps[ti % 4]
                        tb, Btb = tmps[(ti + 1) % 4]
                        ti += 2
                        self.tt(ta, psn[:, :], COS[:, sl], ALU.mult, r=[pbn, Bk], w=[Bta])
                        self.tt(tb, pss[:, :], SIN[:, sl], ALU.mult, r=[pbs, Bk], w=[Btb])
                        kw = {'w': [Bd]} if tq == 0 else {'wp': [Bd]}
                        self.tt(dst[:, sl], ta, tb, ALU.add, r=[Bta, Btb], eng='pool', **kw)
                self.tt(c3(qh), c3(qr), bc16(WQ), ALU.mult, r=[Bqr, BWQ], w=[Bqh])
                self.tt(c3(kst), c3(kr), bc16(KW), ALU.mult, r=[Bkr, BKW], w=[Bkst])
                for g8 in range(2):
                    ps, psb, pb = self.bank()
                    for i8 in range(8):
                        tt = g8 * 8 + i8
                        self.tr(psb[:, i8 * 128:(i8 + 1) * 128], kst[:, tt * 128:(tt + 1) * 128], self.ident_b[:], [Bkst, self.B_const], pb, first=(i8 == 0))
                    kw = {'w': [Bkhtok]} if g8 == 0 else {'wp': [Bkhtok]}
                    self.cp(khtok[:, g8 * 8:(g8 + 1) * 8, :], psb.rearrange("p (a b) -> p a b", a=8), r=[pb], eng='act', **kw)
                w, Bw = self.wload(self.w_in[l][:, A_V + h * 128:A_V + (h + 1) * 128], 8)
                for g4 in range(4):
                    ps, _, pb = self.bank()
                    for i4 in range(4):
                        tt = g4 * 4 + i4
                        for k in range(8):
                            kw = {'w': [pb]} if (i4 == 0 and k == 0) else {'wp': [pb]}
                            self.P.pe(lambda hh, o_=ps[:, i4 * 128:(i4 + 1) * 128], a_=self.hT[:, k, tt * 128:(tt + 1) * 128], b_=w[:, k, :], s_=(k == 0), e_=(k == 7):
                                      hh.matmul(o_, lhsT=a_, rhs=b_, start=s_, stop=e_), r=[Bw, self.B_hT], **kw)
                    kw = {'w': [Bvtok]} if g4 == 0 else {'wp': [Bvtok]}
                    self.cp(vtok[:, g4 * 4:(g4 + 1) * 4, :], ps[:, :].rearrange("p (a b) -> p a b", a=4), r=[pb], eng='act', **kw)

        a_prologue(0)
        for h in range(4):
            for g4 in range(4):
                ps, _, pb = self.bank()
                for i4 in range(4):
                    n = g4 * 4 + i4
                    kw = {'w': [pb]} if i4 == 0 else {'wp': [pb]}
                    self.P.pe(lambda hh, o_=ps[:, i4 * 128:(i4 + 1) * 128], a_=khtok[:, n, :], b_=vtok[:, n, :]: hh.matmul(o_, lhsT=a_, rhs=b_, start=True, stop=True),
                              r=[Bkhtok, Bvtok], **kw)
                kw = {'w': [BX]} if g4 == 0 else {'wp': [BX]}
                self.cp(X[:, g4 * 4:(g4 + 1) * 4, :], ps[:, :].rearrange("p (a b) -> p a b", a=4), r=[pb], eng='act', **kw)
            for n in range(1, NTT):
                self.stt(X[0:64, n, :], X[0:64, n - 1, :], cd[0:64, h:h + 1], X[0:64, n, :], ALU.mult, ALU.add, r=[BX, Blgs], w=[BX])
            for n in range(NTT - 2, -1, -1):
                self.stt(X[64:128, n, :], X[64:128, n + 1, :], cd[64:128, h:h + 1], X[64:128, n, :], ALU.mult, ALU.add, r=[BX, Blgs], w=[BX])
            self.mset(prev[0:64, 0, :], 0.0, w=[Bprev], eng='pool')
            self.mset(prev[64:128, NTT - 1, :], 0.0, wp=[Bprev], eng='pool')
            self.cp(prev[0:64, 1:NTT, :], X[0:64, 0:NTT - 1, :], r=[BX], wp=[Bprev], eng='pool')
            self.cp(prev[64:128, 0:NTT - 1, :], X[64:128, 1:NTT, :], r=[BX], wp=[Bprev], eng='pool')
            MT_bc = bc4(MT)
            def a1(tq):
                sl = slice(tq * 512, (tq + 1) * 512)
                psS, _, pbS = self.bank()
                for i4 in range(4):
                    tsl = slice(tq * 512 + i4 * 128, tq * 512 + (i4 + 1) * 128)
                    kw = {'w': [pbS]} if i4 == 0 else {'wp': [pbS]}
                    self.P.pe(lambda hh, o_=psS[:, i4 * 128:(i4 + 1) * 128], a_=kr[0:64, tsl], b_=qr[0:64, tsl]: hh.matmul(o_, lhsT=a_, rhs=b_, start=True, stop=True),
                              r=[Bkr, Bqr], **kw)
                attm, Battm = attms[tq % 2]
                self.tt(attm.rearrange("p (a b) -> p a b", a=4), psS[:, :].rearrange("p (a b) -> p a b", a=4), MT_bc, ALU.mult, r=[pbS, BMT], w=[Battm])
                return (tq, sl, attm, Battm)

            def a2(ctx):
                tq, sl, attm, Battm = ctx
                psO, _, pbO = self.bank()
                for i4 in range(4):
                    n = tq * 4 + i4
                    osl = slice(i4 * 128, (i4 + 1) * 128)
                    tsl = slice(n * 128, (n + 1) * 128)
                    kw = {'w': [pbO]} if i4 == 0 else {'wp': [pbO]}
                    self.P.pe(lambda hh, o_=psO[:, osl], a_=vtok[:, n, :], b_=attm[:, osl]: hh.matmul(o_, lhsT=a_, rhs=b_, start=True, stop=False),
                              r=[Bvtok, Battm], **kw)
                    self.P.pe(lambda hh, o_=psO[:, osl], a_=prev[:, n, :], b_=qh[:, tsl]: hh.matmul(o_, lhsT=a_, rhs=b_, start=False, stop=True),
                              r=[Bprev, Bqh], wp=[pbO])
                kw = {'w': [BO]} if tq == 0 else {'wp': [BO]}
                self.cp(O[:, sl], psO[:, :], r=[pbO], eng='act', **kw)
            prev_a = None
            for tq in range(NTQ):
                ctx_a = a1(tq)
                if prev_a is not None:
                    a2(prev_a)
                prev_a = ctx_a
            a2(prev_a)
            if h < 3:
                a_prologue(h + 1)
            SQ = X.rearrange("p a b -> p (a b)")
            self.actf(SQ, O, AF.Square, r=[BO, BX, Bprev], w=[BX])
            for tq in range(NTQ):
                sl = slice(tq * 512, (tq + 1) * 512)
                ps, _, pb = self.bank()
                self.mm(ps[:, :], self.ones_f[:], SQ[:, sl], True, True, [BX, self.B_const], pb)
                rt, Brt = tmps[tq]
                self.actf(rt, ps[:, :], AF.Ln, r=[pb], w=[Brt], scale=1.0 / 128, bias=EPS)
                self.actf(rt, rt, AF.Exp, r=[Brt], w=[Brt], scale=-0.5)
                self.tt(O[:, sl], O[:, sl], rt, ALU.mult, r=[BO, Brt], wp=[BO])
            w, Bw = self.wload(self.w_in[l][:, A_G + h * 128:A_G + (h + 1) * 128], 8)
            for tq in range(NTQ):
                sl = slice(tq * 512, (tq + 1) * 512)
                ps, _, pb = self.bank()
                self.mm_group(ps[:, :], [w[:, k, :] for k in range(8)], [self.hT[:, k, sl] for k in range(8)], [Bw, self.B_hT], pb)
                gt, Bgt = tmps[tq]
                self.actf(gt, ps[:, :], AF.Silu, r=[pb], w=[Bgt])
                kw = {'w': [self.B_brT[h]]} if tq == 0 else {'wp': [self.B_brT[h]]}
                self.tt(self.brT[:, h, sl], O[:, sl], gt, ALU.mult, r=[BO, Bgt], **kw)
            if h == 3:
                self.dump("A_O", O, [BO])
                self.dump("A_qr", qr, [Bqr], BF16)
                self.dump("A_kr", kr, [Bkr], BF16)
                self.dump("A_MT", MT, [BMT])
                self.dump("A_X", X, [BX])
                self.dump("A_lg", lg, [Blg])
        self.dump("A_brT", self.brT[:], self.B_brT, BF16)

    def branch_B(self, l, dst=None, Bdst=None):
        self.phase()
        if dst is None:
            odst, Bdst = self.brT, self.B_brT
        else:
            odst = self.carve([128, 4, T], BF16)
        self.bank_set = [4, 5, 6, 7]
        self.setup_wbufs(3, 1024)
        qT, kT = [self.carve([128, T], BF16) for _ in range(2)]
        Va = self.carve([128, 16, 128], BF16)
        Vb = self.carve([128, 16, 128], BF16)
        G = self.carve([128, T], F32)
        EB = self.carve([128, 2, 14, 64], F32)
        exs = [(self.carve([128, 512], F32), Buf()) for _ in range(2)]
        Ps = [(self.carve([128, 512], BF16), Buf()) for _ in range(2)]
        rds = [(self.carve([128, 512], F32), Buf()) for _ in range(2)]
        BqT, BkT, BVa, BVb, BG, BEB = [Buf() for _ in range(6)]
        it = 0
        for fc in range(4 if BST >= 1 else 0):
            self.dma(EB, self.ebs[l, :, 2 * fc:2 * fc + 2, :, :], r=[self.B_ebs], w=[BEB])
            for (c0, dst, Bd, fn) in ((B_Q, qT, BqT, None), (B_K, kT, BkT, None), (B_G, G, BG, AF.Silu)):
                w, Bw = self.wload(self.w_in[l][:, c0 + fc * 128:c0 + (fc + 1) * 128], 8)
                for tq in range(NTQ):
                    sl = slice(tq * 512, (tq + 1) * 512)
                    ps, _, pb = self.bank()
                    self.mm_group(ps[:, :], [w[:, k, :] for k in range(8)], [self.hT[:, k, sl] for k in range(8)], [Bw, self.B_hT], pb)
                    kw = {'w': [Bd]} if tq == 0 else {'wp': [Bd]}
                    if fn is None:
                        self.cp(dst[:, sl], ps[:, :], r=[pb], eng='act', **kw)
                    else:
                        self.actf(dst[:, sl], ps[:, :], fn, r=[pb], **kw)
            w, Bw = self.wload(self.w_in[l][:, B_V + fc * 128:B_V + (fc + 1) * 128], 8)
            for (Vt, BV, off, ntile) in ((Va, BVa, 0, 16), (Vb, BVb, 64, 15)):
                for g4 in range(4):
                    n4 = min(4, ntile - g4 * 4)
                    ps, _, pb = self.bank()
                    for i4 in range(n4):
                        t0 = off + (g4 * 4 + i4) * 128
                        for k in range(8):
                            kw = {'w': [pb]} if (i4 == 0 and k == 0) else {'wp': [pb]}
                            self.P.pe(lambda hh, o_=ps[:, i4 * 128:(i4 + 1) * 128], a_=self.hT[:, k, t0:t0 + 128], b_=w[:, k, :], s_=(k == 0), e_=(k == 7):
                                      hh.matmul(o_, lhsT=a_, rhs=b_, start=s_, stop=e_), r=[Bw, self.B_hT], **kw)
                    kw = {'w': [BV]} if g4 == 0 else {'wp': [BV]}
                    self.cp(Vt[:, g4 * 4:g4 * 4 + n4, :], ps[:, 0:n4 * 128].rearrange("p (a b) -> p a b", a=n4), r=[pb], eng='act', **kw)
            def s1(r):
                nonlocal it
                rs = min(max(r - 4, 0), 24)
                o = r - rs
                ex, Bex = exs[it % 2]
                Pt, BP = Ps[it % 2]
                it += 1
                for hh in range(2):
                    hb = hh * 64
                    psS, _, pbS = self.bank()
                    for i in range(4):
                        k0 = (rs + 2 * i) * 64
                        kw = {'w': [pbS]} if i == 0 else {'wp': [pbS]}
                        self.P.pe(lambda h_, o_=psS[:, i * 64:(i + 1) * 64], a_=kT[hb:hb + 64, k0:k0 + 128], b_=qT[hb:hb + 64, r * 64:(r + 1) * 64]:
                                  h_.matmul(o_, lhsT=a_, rhs=b_, start=True, stop=True), r=[BkT, BqT], **kw)
                    kw = {'w': [Bex]} if hh == 0 else {'wp': [Bex]}
                    self.actf(ex[:, hh * 256:(hh + 1) * 256], psS[:, 0:256], AF.Exp, r=[pbS], scale=0.125, **kw)
                eb = EB[:, :, 7 - o:7 - o + 7:2, :]
                self.tt(Pt.rearrange("p (h i q) -> p h i q", h=2, i=4), ex.rearrange("p (h i q) -> p h i q", h=2, i=4), eb, ALU.mult,
                        r=[Bex, BEB], w=[BP])
                return (r, rs, Pt, BP)

            def s2(ctx):
                r, rs, Pt, BP = ctx
                rg, r4 = r // 4, r % 4
                a = rs % 2
                Vt, BV = (Va, BVa) if a == 0 else (Vb, BVb)
                psN, _, pbN = self.bankx(2 * (rg % 2))
                psD, _, pbD = self.bankx(2 * (rg % 2) + 1)
                for hh in range(2):
                    oc = hh * 256 + r4 * 64
                    for (psX, pbX, isnum) in ((psN, pbN, True), (psD, pbD, False)):
                        for i in range(4):
                            ti = (rs + 2 * i - a) // 2
                            lh = Vt[:, ti, :] if isnum else self.ones_b[:, :]
                            kw = {'w': [pbX]} if (r4 == 0 and hh == 0 and i == 0) else {'wp': [pbX]}
                            self.P.pe(lambda h_, o_=psX[:, oc:oc + 64], a_=lh, b_=Pt[:, (hh * 4 + i) * 64:(hh * 4 + i + 1) * 64], s_=(i == 0), e_=(i == 3):
                                      h_.matmul(o_, lhsT=a_, rhs=b_, start=s_, stop=e_), r=[BV, BP, self.B_const], **kw)
                if r4 != 3:
                    return
                rd, Brd = rds[rg % 2]
                sl = slice(rg * 256, (rg + 1) * 256)
                for hh in range(2):
                    pr_ = slice(hh * 64, (hh + 1) * 64)
                    cs_ = slice(hh * 256, (hh + 1) * 256)
                    kw = {'w': [Brd]} if hh == 0 else {'wp': [Brd]}
                    self.actf(rd[pr_, 0:256], psD[pr_, cs_], AF.Ln, r=[pbD], **kw)
                    self.actf(rd[pr_, 0:256], rd[pr_, 0:256], AF.Exp, r=[Brd], wp=[Brd], scale=-1.0)
                    self.tt(rd[pr_, 0:256], rd[pr_, 0:256], psN[pr_, cs_], ALU.mult, r=[Brd, pbN], wp=[Brd])
                kw = {'w': [Bdst[fc]]} if rg == 0 else {'wp': [Bdst[fc]]}
                self.tt(odst[:, fc, sl], rd[:, 0:256], G[:, sl], ALU.mult, r=[Brd, BG], **kw)

            prev = None
            for r in range(32):
                ctx = s1(r)
                if prev is not None:
                    s2(prev)
                prev = ctx
            s2(prev)
        self.dump("B_brT", odst, Bdst, BF16)
        self.bank_set = list(range(8))

    def build(self):
        self.declare()
        self.alloc()
        self.phase0()
        if self.mask[1]:
            self.build_nat_tables()
        outs = []
        self.B_brT2 = [Buf() for _ in range(4)]
        for s in range(self.nseq):
            for l in range(self.nlayers):
                self.phase1(s, l)
                if all(self.mask):
                    self.branch_A(l)
                    self.phase_merge(l, 0, True)
                    self.branch_D(l)
                    self.branch_B(l, dst='arena', Bdst=self.B_brT2)
                    self.phase()
                    brT2 = self.carve([128, 4, T], BF16)
                    self.brT_free = False
                    mres = self.merge_res(3, 1)
                    gm = self.gen_merge(l, 3, False, res=mres)
                    gm2 = self.gen_merge(l, 1, False, src=brT2, Bsrc=self.B_brT2, res=mres)
                    gc = self.gen_C(l)
                    nmd = 0
                    cstep = 0
                    c_done = False
                    md_done = False
                    while not md_done:
                        if not c_done and cstep < 7:
                            try:
                                next(gc)
                                cstep += 1
                            except StopIteration:
                                c_done = True
                        for _ in range(5):
                            try:
                                next(gm)
                            except StopIteration:
                                md_done = True
                                break
                    self.brT_free = True
                    mb_done = False
                    while not (c_done and mb_done):
                        if not c_done:
                            try:
                                next(gc)
                            except StopIteration:
                                c_done = True
                        if not mb_done:
                            try:
                                next(gm2)
                            except StopIteration:
                                mb_done = True
                    self.phase_merge(l, 2, False)
                    first = False
                else:
                    first = True
                    for j, fn in enumerate((self.branch_A, self.branch_B, self.branch_C, self.branch_D)):
                        if not self.mask[j] or fn is None:
                            continue
                        fn(l)
                        self.phase_merge(l, j, first)
                        first = False
                o = self.phase3(s, l, not first)
                if l == self.nlayers - 1:
                    outs += o
        self.P.emit(final_wait=outs + self.dbg_outs)
        self.st.close()


_CACHE = {}


def get_nc(nseq, nlayers, mask):
    key = (nseq, nlayers, tuple(mask))
    if key not in _CACHE:
        nc = bass.Bass("TRN2", target_bir_lowering=False)
        kb = KB(nc, nseq, nlayers, mask)
        kb.build()
        _CACHE[key] = nc
    return _CACHE[key]


WNAMES = ['norm_mix', 'w_in', 'ret_decay_logit', 'nat_rpb', 'lru_conv_w', 'lru_conv_b', 'lru_wa', 'lru_ba', 'lru_wx',
          'lru_bx', 'lru_lambda', 'hgrn_lb_logits', 'hgrn_norm', 'w_branch', 'w_merge', 'w_out', 'ple_norm',
          'w_ple_gate', 'w_ple_proj', 'final_norm']


def host_consts():
    s = np.arange(128)[:, None]
    t = np.arange(128)[None, :]
    same = (s // 32) == (t // 32)
    half = 32
    inv = (np.float32(10000.0) ** (-(np.arange(half, dtype=np.float32) / np.float32(half)))).astype(np.float32)
    ang = (np.arange(T, dtype=np.float32)[None, :] * inv[:, None]).astype(np.float32)
    cs = np.cos(ang.astype(np.float64)).astype(np.float32)
    sn = np.sin(ang.astype(np.float64)).astype(np.float32)
    cos64 = np.concatenate([cs, cs], 0)
    sin64 = np.concatenate([-sn, sn], 0)
    c_cos = np.concatenate([cos64, cos64], 0)
    c_sin = np.concatenate([sin64, sin64], 0)
    tau = np.arange(128, dtype=np.float32)
    DF = np.maximum(t - s, 0).astype(np.float32)
    UF = (t >= s).astype(np.float32)
    DB = np.maximum(s - t, 0).astype(np.float32)
    UB = (s >= t).astype(np.float32)
    TQ = np.concatenate([np.tile(tau + 1, (64, 1)), np.tile(128 - tau, (64, 1))], 0)
    TK = np.concatenate([np.tile(127 - tau, (64, 1)), np.tile(tau, (64, 1))], 0)
    c_ret = np.stack([DF, UF, DB, UB, TQ, TK]).astype(np.float32)
    return {'c_ident': np.eye(128, dtype=np.float32), 'c_cos': c_cos, 'c_sin': c_sin, 'c_ret': c_ret,
            'c_maskF': (same & (s <= t)).astype(np.float32),
            'c_maskB': (same & (s >= t)).astype(np.float32)}


def run_seqs(xs, ps, weights, nlayers=NL, mask=(1, 1, 1, 1), ncores=8):
    n = xs.shape[0]
    per = n // ncores
    nc = get_nc(per, nlayers, mask)
    consts = host_consts()
    in_maps = []
    for c in range(ncores):
        m = {'x': np.ascontiguousarray(xs[c * per:(c + 1) * per]),
             'p': np.ascontiguousarray(ps[:, c * per:(c + 1) * per])}
        for k in WNAMES:
            m[k] = weights[k]
        m.update(consts)
        in_maps.append(m)
    res = run_bass_kernel_spmd(nc, in_maps, core_ids=list(range(ncores)))
    return np.concatenate([r['y'] for r in res.results], axis=0)


def kernel(**inputs):
    inputs = {k: np.asarray(v) for k, v in inputs.items()}
    xs = np.concatenate([inputs['x_prompt'], inputs['x_sample']], axis=0)
    ps = np.concatenate([inputs['p_prompt'], inputs['p_sample']], axis=1)
    weights = {k: np.ascontiguousarray(inputs[k], dtype=np.float32) for k in WNAMES}
    y = run_seqs(xs.astype(np.float32, copy=False), ps.astype(np.float32, copy=False), weights)
    nb = inputs['x_prompt'].shape[0]
    return (np.ascontiguousarray(y[:nb]), np.ascontiguousarray(y[nb:]))
```
